# Optimizing a Trainium2 kernel written in Bass

```python
import jax, jax.numpy as jnp
from jax import lax
import numpy as np

D_MODEL = 1024
BATCH = 16
SEQ = 256
DEPTH = 2
DEC_BATCH = 2
DEC_SEQ = 2048
PAST_LEN = 256

GRID_W = 64
D_MIX = 1024
ATTN_HEADS = 8
ATTN_KV_HEADS = 2
ATTN_HEAD_DIM = 64
ATTN_WINDOW = 128
ATTN_BLOCK = 128
ROPE_THETA = 10000.0
MLSTM_HEADS = 4
MLSTM_HEAD_DIM = 64
MLSTM_CHUNK = 64
FORGET_BIAS = 3.0
POOL_GROUPS = 4
POOL_GROUP_DIM = 64
POOL_WINDOWS = (2, 4, 8, 16)
PEER_HEADS = 8
PEER_N_KEYS = 128
PEER_N_EXPERTS = PEER_N_KEYS * PEER_N_KEYS
PEER_QUERY_DIM = 256
PEER_HALF = PEER_QUERY_DIM // 2
PEER_TOPK = 16
PEER_TOKEN_BLOCK = 128
NORM_EPS = 1e-6

ATTN_Q = ATTN_HEADS * ATTN_HEAD_DIM
ATTN_KV = ATTN_KV_HEADS * ATTN_HEAD_DIM
MLSTM_W = MLSTM_HEADS * MLSTM_HEAD_DIM
POOL_W = POOL_GROUPS * POOL_GROUP_DIM
N_GATES = 4 * MLSTM_HEADS
D_IN = ATTN_Q + 2 * ATTN_KV + 4 * MLSTM_W + N_GATES + POOL_W

kernel_name = 'hybrid_dit_mlstm_pool_swa_peer_step'

F32 = jnp.float32


def rmsnorm(x, g):
    xf = x.astype(F32)
    y = xf * lax.rsqrt(jnp.mean(xf * xf, axis=-1, keepdims=True) + NORM_EPS)
    return (y * g.astype(F32)).astype(x.dtype)


def modulation(cond, w_mod, b_mod):
    m = jax.nn.silu(cond) @ w_mod + b_mod
    return tuple(part[..., None, :] for part in jnp.split(m, 6, axis=-1))


def axial_rope(x, row, col):
    half = ATTN_HEAD_DIM // 2
    inv = ROPE_THETA ** (-jnp.arange(0, half, 2, dtype=F32) / half)

    def rot(xa, pos):
        ang = pos.astype(F32)[:, None] * inv[None, :]
        cos = jnp.cos(ang)[None, :, None, :]
        sin = jnp.sin(ang)[None, :, None, :]
        x1, x2 = jnp.split(xa.astype(F32), 2, axis=-1)
        return jnp.concatenate([x1 * cos - x2 * sin, x2 * cos + x1 * sin], axis=-1)

    xr, xc = jnp.split(x, 2, axis=-1)
    return jnp.concatenate([rot(xr, row), rot(xc, col)], axis=-1).astype(x.dtype)


def softmax_with_sink(logits, sink):
    sink_col = jnp.broadcast_to(sink, logits.shape[:-1] + (1,))
    p = jax.nn.softmax(jnp.concatenate([logits, sink_col], axis=-1), axis=-1)
    return p[..., :-1]


def context_attention(q, k, v, sink):
    B, S = q.shape[:2]
    G = ATTN_HEADS // ATTN_KV_HEADS
    nb = S // ATTN_BLOCK
    scale = ATTN_HEAD_DIM ** -0.5
    qb = q.reshape(B, nb, ATTN_BLOCK, ATTN_KV_HEADS, G, ATTN_HEAD_DIM).swapaxes(0, 1)
    sink_l = sink.astype(F32).reshape(ATTN_KV_HEADS, G, 1, 1)

    def block(qi):
        s = jnp.einsum('bqkgd,bskd->bkgqs', qi, k).astype(F32) * scale
        p = softmax_with_sink(s, sink_l)
        return jnp.einsum('bkgqs,bskd->bqkgd', p.astype(v.dtype), v)

    o = lax.map(block, qb)
    return o.swapaxes(0, 1).reshape(B, S, ATTN_Q)


def latent_attention(q, k, v, kc, vc, sink):
    B, L = q.shape[:2]
    G = ATTN_HEADS // ATTN_KV_HEADS
    nb = L // ATTN_BLOCK
    nband = 3 * ATTN_BLOCK
    scale = ATTN_HEAD_DIM ** -0.5
    qb = q.reshape(B, nb, ATTN_BLOCK, ATTN_KV_HEADS, G, ATTN_HEAD_DIM)

    def band(t):
        tp = jnp.pad(t, ((0, 0), (ATTN_BLOCK, ATTN_BLOCK), (0, 0), (0, 0)))
        tp = tp.reshape(B, nb + 2, ATTN_BLOCK, ATTN_KV_HEADS, ATTN_HEAD_DIM)
        return jnp.concatenate([tp[:, :-2], tp[:, 1:-1], tp[:, 2:]], axis=2)

    kb, vb = band(k), band(v)
    blk = jnp.arange(nb)[:, None] * ATTN_BLOCK
    qpos = (blk + jnp.arange(ATTN_BLOCK)[None, :])[:, :, None]
    kpos = (blk - ATTN_BLOCK + jnp.arange(nband)[None, :])[:, None, :]
    valid = (jnp.abs(qpos - kpos) <= ATTN_WINDOW) & (kpos >= 0) & (kpos < L)
    s_band = jnp.einsum('bnqkgd,bnskd->bnkgqs', qb, kb).astype(F32) * scale
    s_band = jnp.where(valid[None, :, None, None], s_band, -jnp.inf)
    s_ctx = jnp.einsum('bnqkgd,bpkd->bnkgqp', qb, kc).astype(F32) * scale
    sink_l = sink.astype(F32).reshape(ATTN_KV_HEADS, G, 1, 1)
    p = softmax_with_sink(jnp.concatenate([s_band, s_ctx], axis=-1), sink_l)
    o = (jnp.einsum('bnkgqs,bnskd->bnqkgd', p[..., :nband].astype(vb.dtype), vb)
         + jnp.einsum('bnkgqp,bpkd->bnqkgd', p[..., nband:].astype(vc.dtype), vc))
    return o.reshape(B, L, ATTN_Q)


def mlstm_chunkwise(q, k, v, ig, lf, init):
    B, H, L, _ = q.shape
    nc = L // MLSTM_CHUNK

    def chunks(t):
        return jnp.moveaxis(t.reshape((B, H, nc, MLSTM_CHUNK) + t.shape[3:]), 2, 0)

    causal = jnp.tril(jnp.ones((MLSTM_CHUNK, MLSTM_CHUNK), bool))

    def step(carry, xs):
        C, n, m = carry
        qc, kc, vc, igc, lfc = xs
        b = jnp.cumsum(lfc, axis=-1)
        dlog = jnp.where(causal, b[..., :, None] - b[..., None, :] + igc[..., None, :], -jnp.inf)
        state_log = b + m[..., None]
        m_t = jnp.maximum(state_log, jnp.max(dlog, axis=-1))
        w = jnp.exp(dlog - m_t[..., None]) * jnp.einsum('bhtd,bhsd->bhts', qc, kc)
        sc = jnp.exp(state_log - m_t)
        num = sc[..., None] * jnp.einsum('bhtd,bhde->bhte', qc, C) + jnp.einsum('bhts,bhse->bhte', w, vc)
        nq = sc * jnp.einsum('bhtd,bhd->bht', qc, n) + jnp.sum(w, axis=-1)
        h = num / jnp.maximum(jnp.abs(nq), jnp.exp(-m_t))[..., None]
        b_last = b[..., -1]
        wlog = b_last[..., None] - b + igc
        m_new = jnp.maximum(b_last + m, jnp.max(wlog, axis=-1))
        decay = jnp.exp(b_last + m - m_new)
        wk = jnp.exp(wlog - m_new[..., None])[..., None] * kc
        C_new = decay[..., None, None] * C + jnp.einsum('bhsd,bhse->bhde', wk, vc)
        n_new = decay[..., None] * n + jnp.sum(wk, axis=2)
        return (C_new, n_new, m_new), h

    final, h = lax.scan(step, init, (chunks(q), chunks(k), chunks(v), chunks(ig), chunks(lf)))
    h = jnp.moveaxis(h, 0, 2).reshape(B, H, L, v.shape[-1])
    return h, final


def mlstm_bidirectional(q, k, v, gates, init_f, init_b):
    ig_f, fg_f, ig_b, fg_b = [jnp.moveaxis(g, -1, 1) for g in jnp.split(gates, 4, axis=-1)]
    h_f, fin_f = mlstm_chunkwise(q, k, v, ig_f, jax.nn.log_sigmoid(fg_f), init_f)

    def flip(t):
        return jnp.flip(t, axis=2)

    h_b, fin_b = mlstm_chunkwise(flip(q), flip(k), flip(v), flip(ig_b), flip(jax.nn.log_sigmoid(fg_b)), init_b)
    return h_f + flip(h_b), fin_f, fin_b


def multiscale_pool(x, pool_w, pool_scale):
    B, L, _ = x.shape
    xf = x.astype(F32)
    cs = jnp.concatenate([jnp.zeros((B, 1, POOL_W), F32), jnp.cumsum(xf, axis=1)], axis=1)
    t = jnp.arange(L)
    outs = []
    for g, w in enumerate(POOL_WINDOWS):
        lo = jnp.clip(t - w // 2, 0, L)
        hi = jnp.clip(t + w // 2, 0, L)
        sl = slice(g * POOL_GROUP_DIM, (g + 1) * POOL_GROUP_DIM)
        mean = (cs[:, hi, sl] - cs[:, lo, sl]) / (hi - lo).astype(F32)[None, :, None]
        outs.append((mean - xf[..., sl]).astype(x.dtype) @ pool_w[g])
    return jnp.concatenate(outs, axis=-1) * pool_scale


def peer_ffn(x, w_q, sub_keys, u_table, v_table):
    B, L, D = x.shape
    T = B * L
    xt = x.reshape(T, D)
    q = (xt @ w_q).reshape(T, PEER_HEADS, 2, PEER_HALF)
    s = jnp.einsum('thcd,hckd->thck', q, sub_keys).astype(F32)
    top_s, top_i = lax.top_k(s, PEER_TOPK)
    cand = (top_s[:, :, 0, :, None] + top_s[:, :, 1, None, :]).reshape(T, PEER_HEADS, PEER_TOPK * PEER_TOPK)
    best_s, best_p = lax.top_k(cand, PEER_TOPK)
    i1 = jnp.take_along_axis(top_i[:, :, 0, :], best_p // PEER_TOPK, axis=-1)
    i2 = jnp.take_along_axis(top_i[:, :, 1, :], best_p % PEER_TOPK, axis=-1)
    expert = i1 * PEER_N_KEYS + i2
    gate = jax.nn.softmax(best_s, axis=-1).astype(x.dtype)
    nblk = T // PEER_TOKEN_BLOCK

    def block(args):
        xb, eb, gb = args
        a = jax.nn.gelu(jnp.einsum('thkd,td->thk', u_table[eb], xb), approximate=False)
        return jnp.einsum('thk,thkd->td', gb * a, v_table[eb])

    out = lax.map(block, (xt.reshape(nblk, PEER_TOKEN_BLOCK, D),
                          expert.reshape(nblk, PEER_TOKEN_BLOCK, PEER_HEADS, PEER_TOPK),
                          gate.reshape(nblk, PEER_TOKEN_BLOCK, PEER_HEADS, PEER_TOPK)))
    return out.reshape(B, L, D)


def mixing_sublayer(h, p, ctx_kv, mlstm_init, rope_pos):
    B, L, _ = h.shape
    sizes = (ATTN_Q, ATTN_KV, ATTN_KV, MLSTM_W, MLSTM_W, MLSTM_W, MLSTM_W, N_GATES, POOL_W)
    offs = []
    acc = 0
    for sz in sizes[:-1]:
        acc += sz
        offs.append(acc)
    q_a, k_a, v_a, q_m, k_m, v_m, o_m, g_m, x_p = jnp.split(h @ p['w_in'], offs, axis=-1)

    qa = q_a.reshape(B, L, ATTN_HEADS, ATTN_HEAD_DIM)
    ka = k_a.reshape(B, L, ATTN_KV_HEADS, ATTN_HEAD_DIM)
    va = v_a.reshape(B, L, ATTN_KV_HEADS, ATTN_HEAD_DIM)
    if ctx_kv is None:
        attn = context_attention(qa, ka, va, p['attn_sink'])
        kv_out = (ka, va)
    else:
        row, col = rope_pos
        attn = latent_attention(axial_rope(qa, row, col), axial_rope(ka, row, col), va,
                                ctx_kv[0], ctx_kv[1], p['attn_sink'])
        kv_out = None

    def heads(t):
        return t.reshape(B, L, MLSTM_HEADS, MLSTM_HEAD_DIM).transpose(0, 2, 1, 3).astype(F32)

    gates = g_m.astype(F32) + p['gate_b'].astype(F32)
    h_m, fin_f, fin_b = mlstm_bidirectional(heads(q_m), heads(k_m) * MLSTM_HEAD_DIM ** -0.5, heads(v_m),
                                            gates, mlstm_init[0], mlstm_init[1])
    h_m = h_m * lax.rsqrt(jnp.mean(h_m * h_m, axis=-1, keepdims=True) + NORM_EPS)
    h_m = h_m.transpose(0, 2, 1, 3).reshape(B, L, MLSTM_W) * p['mlstm_norm_g'].astype(F32)
    mlstm_out = (jax.nn.sigmoid(o_m.astype(F32)) * h_m).astype(h.dtype)

    pool_out = multiscale_pool(x_p, p['pool_w'], p['pool_scale'])

    out = jnp.concatenate([attn, mlstm_out, pool_out], axis=-1) @ p['w_out']
    return out, kv_out, (fin_f, fin_b)


def trunk_layer(x, mods, p, ctx_kv, mlstm_init, rope_pos):
    sh1, sc1, g1, sh2, sc2, g2 = mods
    h = rmsnorm(x, p['norm1_g']) * (1 + sc1) + sh1
    y, kv, fins = mixing_sublayer(h, p, ctx_kv, mlstm_init, rope_pos)
    x = x + g1 * y
    h = rmsnorm(x, p['norm2_g']) * (1 + sc2) + sh2
    x = x + g2 * peer_ffn(h, p['peer_wq'], p['peer_keys'], p['peer_u'], p['peer_v'])
    return x, kv, fins


def setup_inputs(seed: int = 0) -> dict:
    key = jax.random.key(seed)
    ks = jax.random.split(key, 26)

    def nrm(k, shape, s=1.0):
        return s * jax.random.normal(k, shape, F32)

    gate_offset = jnp.repeat(jnp.array([0.0, FORGET_BIAS, 0.0, FORGET_BIAS], F32), MLSTM_HEADS)
    return {
        'x_prompt': nrm(ks[0], (BATCH, SEQ, D_MODEL)),
        'x_sample': nrm(ks[1], (DEC_BATCH, DEC_SEQ, D_MODEL)),
        'cache_k': nrm(ks[2], (DEC_BATCH, DEPTH, PAST_LEN, ATTN_KV_HEADS, ATTN_HEAD_DIM)),
        'cache_v': nrm(ks[3], (DEC_BATCH, DEPTH, PAST_LEN, ATTN_KV_HEADS, ATTN_HEAD_DIM)),
        'state_C': nrm(ks[4], (DEC_BATCH, DEPTH, 2, MLSTM_HEADS, MLSTM_HEAD_DIM, MLSTM_HEAD_DIM), 0.5),
        'state_n': nrm(ks[5], (DEC_BATCH, DEPTH, 2, MLSTM_HEADS, MLSTM_HEAD_DIM), 0.5),
        'state_m': nrm(ks[6], (DEC_BATCH, DEPTH, 2, MLSTM_HEADS)),
        'c': nrm(ks[7], (DEC_BATCH, D_MODEL)),
        'c_ctx': nrm(ks[8], (D_MODEL,)),
        'w_mod': nrm(ks[9], (DEPTH, D_MODEL, 6 * D_MODEL), 0.5 * D_MODEL ** -0.5),
        'b_mod': nrm(ks[10], (DEPTH, 6 * D_MODEL), 0.02),
        'norm1_g': 1.0 + nrm(ks[11], (DEPTH, D_MODEL), 0.05),
        'norm2_g': 1.0 + nrm(ks[12], (DEPTH, D_MODEL), 0.05),
        'w_in': nrm(ks[13], (DEPTH, D_MODEL, D_IN), D_MODEL ** -0.5),
        'gate_b': gate_offset + nrm(ks[14], (DEPTH, N_GATES), 0.1),
        'attn_sink': nrm(ks[15], (DEPTH, ATTN_HEADS)),
        'mlstm_norm_g': 1.0 + nrm(ks[16], (DEPTH, MLSTM_W), 0.05),
        'pool_w': nrm(ks[17], (DEPTH, POOL_GROUPS, POOL_GROUP_DIM, POOL_GROUP_DIM), POOL_GROUP_DIM ** -0.5),
        'pool_scale': 1.0 + nrm(ks[18], (DEPTH, POOL_W), 0.05),
        'w_out': nrm(ks[19], (DEPTH, D_MIX, D_MODEL), D_MIX ** -0.5),
        'peer_wq': nrm(ks[20], (DEPTH, D_MODEL, PEER_HEADS * PEER_QUERY_DIM), D_MODEL ** -0.5),
        'peer_keys': nrm(ks[21], (DEPTH, PEER_HEADS, 2, PEER_N_KEYS, PEER_HALF), PEER_HALF ** -0.5),
        'peer_u': nrm(ks[22], (DEPTH, PEER_N_EXPERTS, D_MODEL), D_MODEL ** -0.5),
        'peer_v': nrm(ks[23], (DEPTH, PEER_N_EXPERTS, D_MODEL), (PEER_HEADS * PEER_TOPK) ** -0.5),
        'final_norm_g': 1.0 + nrm(ks[24], (D_MODEL,), 0.05),
    }


def reference(x_prompt, x_sample, cache_k, cache_v, state_C, state_n, state_m, c, c_ctx,
              w_mod, b_mod, norm1_g, norm2_g, w_in, gate_b, attn_sink, mlstm_norm_g, pool_w, pool_scale,
              w_out, peer_wq, peer_keys, peer_u, peer_v, final_norm_g):
    rows = x_sample.shape[1] // GRID_W
    row = jnp.repeat(jnp.arange(rows), GRID_W)
    col = jnp.tile(jnp.arange(GRID_W), rows)
    B = x_prompt.shape[0]
    zero_state = (jnp.zeros((B, MLSTM_HEADS, MLSTM_HEAD_DIM, MLSTM_HEAD_DIM), F32),
                  jnp.zeros((B, MLSTM_HEADS, MLSTM_HEAD_DIM), F32),
                  jnp.zeros((B, MLSTM_HEADS), F32))
    xp, xs = x_prompt, x_sample
    ks_, vs_, Cs_, ns_, ms_ = [], [], [], [], []
    for l in range(DEPTH):
        p = {'w_in': w_in[l], 'gate_b': gate_b[l], 'attn_sink': attn_sink[l], 'mlstm_norm_g': mlstm_norm_g[l],
             'pool_w': pool_w[l], 'pool_scale': pool_scale[l], 'w_out': w_out[l],
             'norm1_g': norm1_g[l], 'norm2_g': norm2_g[l], 'peer_wq': peer_wq[l], 'peer_keys': peer_keys[l],
             'peer_u': peer_u[l], 'peer_v': peer_v[l]}
        xp, (k_l, v_l), (fin_f, fin_b) = trunk_layer(xp, modulation(c_ctx, w_mod[l], b_mod[l]), p,
                                                      None, (zero_state, zero_state), None)
        ks_.append(k_l)
        vs_.append(v_l)
        Cs_.append(jnp.stack([fin_f[0], fin_b[0]], axis=1))
        ns_.append(jnp.stack([fin_f[1], fin_b[1]], axis=1))
        ms_.append(jnp.stack([fin_f[2], fin_b[2]], axis=1))
        init = tuple((state_C[:, l, d].astype(F32), state_n[:, l, d].astype(F32), state_m[:, l, d].astype(F32))
                     for d in range(2))
        xs, _, _ = trunk_layer(xs, modulation(c, w_mod[l], b_mod[l]), p,
                               (cache_k[:, l], cache_v[:, l]), init, (row, col))
    y_prompt = rmsnorm(xp, final_norm_g)
    y_sample = rmsnorm(xs, final_norm_g)
    return (y_prompt, y_sample, jnp.stack(ks_, axis=1), jnp.stack(vs_, axis=1),
            jnp.stack(Cs_, axis=1), jnp.stack(ns_, axis=1), jnp.stack(ms_, axis=1))
```

```python
import numpy as np
import concourse.bass as bass
import concourse.mybir as mybir
from concourse.bass_utils import run_bass_kernel_spmd

F32 = mybir.dt.float32
BF16 = mybir.dt.bfloat16
I32 = mybir.dt.int32
U32 = mybir.dt.uint32
AF = mybir.ActivationFunctionType
ALU = mybir.AluOpType
AX = mybir.AxisListType

D = 1024
DEPTH = 2
NTOK = 2560
SEQS = [(0, 256, 0), (256, 256, 0), (512, 2048, 1)]
NW = 2576
EPS = 1e-6
NDS = 24
SUB = 99
import os
DBG = os.environ.get('KDBG', 'z')


class KB:
    def __init__(self, nc):
        self.nc = nc
        self.eng = dict(pe=nc.tensor, dve=nc.vector, act=nc.scalar, pool=nc.gpsimd, sp=nc.sync)
        self.sems = {e: nc.semaphore("sem_" + e).__enter__() for e in self.eng}
        self.cnt = {e: 0 for e in self.eng}
        self.sems['cc'] = nc.semaphore("sem_cc").__enter__()
        self.cnt['cc'] = 0
        self.dsems = []
        self.dcnt = []
        self.dname = {}
        self.known = {e: {} for e in self.eng}
        self.defer = None
        self.lastw = {}
        self.rd = {}
        self.tiles = {}

    def key(self, ap):
        return ap.tensor.name

    def _deps(self, e, reads, writes, skip=()):
        need = {}

        def add(dep):
            if dep is None:
                return
            kind, a, v = dep
            if kind == 'e':
                if a == 'pe' and e == 'pe':
                    return
                if a in skip:
                    return
                s = self.sems[a]
                kid = ('e', a)
            else:
                s = self.dsems[a]
                v = self.dcnt[a]
                kid = ('d', a)
            if need.get(kid, (None, 0))[1] < v:
                need[kid] = (s, v)

        for r in reads:
            add(self.lastw.get(r))
        for w in writes:
            add(self.lastw.get(w))
            for d in self.rd.get(w, ()):
                add(d)
        for kid, (s, v) in need.items():
            if self.known[e].get(kid, 0) >= v:
                continue
            self.eng[e].wait_ge(s, v)
            self.known[e][kid] = v

    def _record(self, dep, reads, writes):
        for r in reads:
            self.rd.setdefault(r, []).append(dep)
        for w in writes:
            self.lastw[w] = dep
            self.rd[w] = []

    def op(self, e, fn, reads=(), writes=()):
        if self.defer is not None:
            self.defer.append(('op', e, fn, reads, writes))
            return
        reads = [self.key(r) if not isinstance(r, str) else r for r in reads]
        writes = [self.key(w) if not isinstance(w, str) else w for w in writes]
        self._deps(e, reads, writes)
        ins = fn(self.eng[e])
        self.cnt[e] += 1
        ins.then_inc(self.sems[e], 1)
        self._record(('e', e, self.cnt[e]), reads, writes)

    def dma(self, q, out, in_, si=None, extra_reads=(), wkey=None, **kw):
        if self.defer is not None:
            self.defer.append(('dma', q, out, in_, kw))
            return
        reads = [self.key(in_)] + [self.key(r) for r in extra_reads]
        writes = [wkey if wkey is not None else self.key(out)]
        self._deps(q, reads, writes)
        si = self._dsem(writes[0])
        ins = self.eng[q].dma_start(out=out, in_=in_, **kw)
        self.dcnt[si] += 16
        ins.then_inc(self.dsems[si], 16)
        self._record(('d', si, self.dcnt[si]), reads, writes)

    def _dsem(self, name):
        parts = name.rsplit('_', 1)
        if len(parts) == 2 and parts[1].isdigit() and len(parts[1]) == 1:
            name = parts[0]
        if name not in self.dname:
            self.dname[name] = len(self.dsems)
            self.dsems.append(self.nc.semaphore("dsem%d" % len(self.dsems)).__enter__())
            self.dcnt.append(0)
        return self.dname[name]

    def emit(self, item):
        if item[0] == 'op':
            self.op(item[1], item[2], item[3], item[4])
        elif item[0] == 'dma':
            self.dma(item[1], item[2], item[3], **item[4])
        else:
            self.gather(item[1], item[2], item[3], 0)

    def max8(self, out, in_):
        self.op('dve', lambda e: e.max(out, in_), reads=[in_], writes=[out])

    def maxidx(self, out, mx, vals):
        self.op('dve', lambda e: e.max_index(out, mx, vals), reads=[mx, vals], writes=[out])

    def mrep(self, out, rep, vals, imm):
        self.op('dve', lambda e: e.match_replace(out, rep, vals, imm), reads=[rep, vals], writes=[out])

    def treduce(self, out, in_, axis, op):
        self.op('dve', lambda e: e.tensor_reduce(out, in_, axis, op), reads=[in_], writes=[out])

    def gather(self, out, table, idx, si, skip=()):
        if self.defer is not None:
            self.defer.append(('gather', out, table, idx))
            return
        reads = [self.key(table), self.key(idx)]
        writes = [self.key(out)]
        self._deps('pool', reads, writes, skip)
        si = self._dsem(writes[0])
        ins = self.nc.gpsimd.indirect_dma_start(
            out=out, out_offset=None, in_=table,
            in_offset=bass.IndirectOffsetOnAxis(ap=idx, axis=0))
        self.dcnt[si] += 16
        ins.then_inc(self.dsems[si], 16)
        self._record(('d', si, self.dcnt[si]), reads, writes)

    def allgather(self, out, in_, groups):
        reads = [self.key(in_)]
        writes = [self.key(out)]
        self._deps('pool', reads, writes)
        ins = self.nc.gpsimd.collective_compute("AllGather", mybir.AluOpType.bypass, replica_groups=groups,
                                                ins=[in_.opt()], outs=[out.opt()])
        self.cnt['cc'] += 1
        ins.then_inc(self.sems['cc'])
        self._record(('e', 'cc', self.cnt['cc']), reads, writes)

    def barrier_all(self):
        for e in self.eng:
            for o in self.sems:
                v = self.cnt[o]
                if v and self.known[e].get(('e', o), 0) < v:
                    self.eng[e].wait_ge(self.sems[o], v)
                    self.known[e][('e', o)] = v
            for i in range(len(self.dsems)):
                v = self.dcnt[i]
                if v and self.known[e].get(('d', i), 0) < v:
                    self.eng[e].wait_ge(self.dsems[i], v)
                    self.known[e][('d', i)] = v

    def sb(self, name, shape, dt=F32):
        t = self.nc.sbuf_tensor(name, list(shape), dt).__enter__()
        self.tiles[name] = t
        return t

    def mm(self, out, lhsT, rhs, start=True, stop=True, **kw):
        self.op('pe', lambda e: e.matmul(out, lhsT, rhs, start=start, stop=stop, **kw),
                reads=[lhsT, rhs], writes=[out])

    def tr(self, out, in_, ident):
        self.op('pe', lambda e: e.transpose(out, in_, ident), reads=[in_, ident], writes=[out])

    def act(self, out, in_, func, bias=None, scale=1.0, accum_out=None, eng='act'):
        reads = [in_]
        kw = {}
        if bias is not None:
            kw['bias'] = bias
            if not isinstance(bias, (int, float)):
                reads.append(bias)
        if not isinstance(scale, (int, float)):
            reads.append(scale)
        writes = [out]
        if accum_out is not None:
            kw['accum_out'] = accum_out
            writes.append(accum_out)
        self.op('act', lambda e: e.activation(out, in_, func, scale=scale, **kw), reads=reads, writes=writes)

    def tt(self, out, in0, in1, op, eng='dve'):
        self.op(eng, lambda e: e.tensor_tensor(out, in0, in1, op), reads=[in0, in1], writes=[out])

    def ts(self, out, in0, s1, s2, op0, op1=None, eng='dve', accum_out=None):
        reads = [in0] + [s for s in (s1, s2) if s is not None and not isinstance(s, (int, float))]
        writes = [out] + ([accum_out] if accum_out is not None else [])
        kw = {}
        if op1 is not None:
            kw['op1'] = op1
        if accum_out is not None:
            kw['accum_out'] = accum_out
        self.op(eng, lambda e: e.tensor_scalar(out, in0, s1, s2, op0, **kw), reads=reads, writes=writes)

    def stt(self, out, in0, scalar, in1, op0, op1):
        reads = [in0, in1] + ([scalar] if not isinstance(scalar, (int, float)) else [])
        self.op('dve', lambda e: e.scalar_tensor_tensor(out, in0, scalar, in1, op0, op1), reads=reads, writes=[out])

    def ttr(self, out, in0, in1, op0, op1, accum_out, scale=1.0, scalar=0.0):
        self.op('dve', lambda e: e.scalar_tensor_tensor(out, in0, 1.0, in1, ALU.mult, ALU.mult, accum_out=accum_out),
                reads=[in0, in1], writes=[out, accum_out])

    def cp(self, out, in_, eng='dve'):
        if eng == 'act':
            self.op('act', lambda e: e.copy(out, in_), reads=[in_], writes=[out])
        else:
            self.op(eng, lambda e: e.tensor_copy(out, in_), reads=[in_], writes=[out])

    def memset(self, ap, val, eng='dve'):
        self.op(eng, lambda e: e.memset(ap, val), writes=[ap])

    def recip(self, out, in_):
        self.op('dve', lambda e: e.reciprocal(out, in_), reads=[in_], writes=[out])

    def scan(self, out, d0, d1, init, op0, op1):
        reads = [d0, d1] + ([init] if not isinstance(init, (int, float)) else [])
        self.op('dve', lambda e: e.tensor_tensor_scan(out, d0, d1, init, op0, op1), reads=reads, writes=[out])


def build(stage=99, nl=DEPTH):
    nc = bass.Bass("TRN2", target_bir_lowering=False)
    k = KB(nc)

    def din(name, shape, dt=F32):
        return nc.dram_tensor(name, list(shape), dt, kind="ExternalInput").ap()

    def dout(name, shape, dt=F32):
        return nc.dram_tensor(name, list(shape), dt, kind="ExternalOutput").ap()

    X = din("X", [NTOK, D])
    condT = din("condT", [128, 2, 8])
    cachek = din("cachek", [DEPTH, 256, 128])
    cachev = din("cachev", [DEPTH, 256, 128])
    c0a_d = din("c0a", [DEPTH, 128, 2, 2, 65])
    m0b_d = din("m0b", [DEPTH, 128, 8])
    m0r_d = din("m0r", [DEPTH, 36, 1])
    w_mod = din("w_mod", [DEPTH, D, 6 * D])
    bmodF = din("bmodF", [DEPTH, 128, 4, 8])
    bmodG = din("bmodG", [DEPTH, 128, 2, D])
    n1g = din("n1g", [DEPTH, 128, 8])
    n2g = din("n2g", [DEPTH, 128, 8])
    w_in = din("w_in", [DEPTH, 128, 8 * NW])
    gateb = din("gateb", [DEPTH, 128, 16])
    sinkb = din("sinkb", [DEPTH, 128, 8])
    mng = din("mng", [DEPTH, 128, 256])
    poolw = din("poolw", [DEPTH, 64, 4, 64])
    pscale = din("pscale", [DEPTH, 64, 4])
    w_out = din("w_out", [DEPTH, D, D])
    wq = din("wq", [DEPTH, D, 2048])
    keysT = din("keysT", [DEPTH, 128, 16, 128])
    ntab = DEPTH * 16384 if stage >= 2 else 16
    puv = din("puv", [ntab, 2 * D])
    fng = din("fng", [128, D])
    ident_d = din("ident", [128, 128])
    sel_d = din("sel", [36, 8, 128])
    ropeC_d = din("ropeC", [128, 2048])
    ropeS_d = din("ropeS", [128, 2048])
    prot_d = din("prot", [128, 128])
    tri01_d = din("tri01", [128, 2, 128])
    amask_d = din("amask", [128, 384])
    band_d = din("band", [128, 4, 5, 128])
    iota_d = din("iota16", [128, 16])

    full = not (stage < 2 or nl < DEPTH)
    Y = dout("Y", [1024 if full else NTOK, D])
    qidx_d = din("qidx", [128, 4], I32)
    NK = dout("NK", [2, DEPTH, 256, 128])
    NV = dout("NV", [2, DEPTH, 256, 128])
    NC_ = dout("NC", [2, DEPTH, 2, 4, 64, 64])
    NN = dout("NN", [2, DEPTH, 2, 4, 64])
    NM = dout("NM", [2, DEPTH, 2, 4])

    Xd = nc.dram_tensor("Xd", [NTOK, D], F32).ap()
    XB = [nc.dram_tensor("XB%d" % i, [256, D], F32).ap() for i in range(2)]
    XG = [nc.dram_tensor("XG%d" % i, [1024, D], F32).ap() for i in range(2)]

    PS = [nc.psum_tensor("ps%d" % i, [128, 512], F32).__enter__() for i in range(8)]

    ident = k.sb("identf", [128, 128])
    identb = k.sb("identb", [128, 128], BF16)
    sel = k.sb("selc", [36, 8, 128])
    prot = k.sb("protc", [128, 128])
    tri01 = k.sb("tri01c", [128, 2, 128], BF16)
    amask = k.sb("amaskc", [128, 384], BF16)
    band = k.sb("bandc", [128, 4, 5, 128], BF16)
    iota16 = k.sb("iota16c", [128, 16])
    onec = k.sb("onec", [128, 1])
    epsc = k.sb("epsc", [128, 1])
    k.dma('sp', ident[:], ident_d)
    k.dma('pool', identb[:], ident_d)
    k.dma('sp', sel[:], sel_d)
    k.dma('sp', prot[:], prot_d)
    k.dma('pool', tri01[:], tri01_d)
    k.dma('pool', amask[:], amask_d)
    k.dma('pool', band[:], band_d)
    k.dma('sp', iota16[:], iota_d)
    k.memset(onec[:], 1.0)
    k.memset(epsc[:], EPS)

    xt = k.sb("xt", [128, D])
    xn = k.sb("xn", [128, D])
    st = k.sb("stat", [128, 8])

    for i in range(NTOK // 128):
        k.dma('sp', xt[:], X[i * 128:(i + 1) * 128, :])
        k.dma('sp', Xd[i * 128:(i + 1) * 128, :], xt[:])

    puv16 = nc.dram_tensor("puv16", [ntab, 2 * D], BF16).ap()
    w16 = nc.dram_tensor("w16", [DEPTH, 128, 8 * NW], BF16).ap()
    cT = k.sb("cT", [128, 2, 8])
    scT = k.sb("scT", [128, 2, 8])
    k.dma('pool', cT[:], condT)
    k.act(scT[:], cT[:], AF.Silu)
    modFs = [k.sb("modF%d" % l, [128, 2, 4, 8]) for l in range(DEPTH)]
    A1s = [k.sb("A1_%d" % l, [128, 2, 8]) for l in range(DEPTH)]
    A2s = [k.sb("A2_%d" % l, [128, 2, 8]) for l in range(DEPTH)]
    gB = k.sb("gB", [128, 2, 2, D], BF16)
    gBd = nc.dram_tensor("gBd", [DEPTH, 128, 2 * 2 * D], BF16).ap()
    n1t = k.sb("n1t", [128, 8])
    n2t = k.sb("n2t", [128, 8])
    bmF = k.sb("bmF", [128, 4, 8])

    qidx = k.sb("qidx_sb", [128, 4], I32)
    k.dma('sp', qidx[:], qidx_d)

    def rmsnorm_tile(tok0, gidx=None):
        if gidx is None:
            k.dma('sp', xt[:], Xd[tok0:tok0 + 128, :])
        else:
            k.gather(xt[:], Xd, qidx[:, gidx:gidx + 1], 0)
        k.ttr(xn[:], xt[:], xt[:], ALU.mult, ALU.add, st[:, 0:1])
        k.act(st[:, 1:2], st[:, 0:1], AF.Ln, bias=epsc[:, 0:1], scale=1.0 / D)
        k.act(st[:, 2:3], st[:, 1:2], AF.Exp, scale=-0.5)
        k.ts(xn[:], xt[:], st[:, 2:3], None, ALU.mult)

    with nc.sbuf_tensor("pcf0", [128, 4 * D], F32) as pcf0, nc.sbuf_tensor("pcf1", [128, 4 * D], F32) as pcf1, \
            nc.sbuf_tensor("pcf2", [128, 4 * D], F32) as pcf2, nc.sbuf_tensor("pcf3", [128, 4 * D], F32) as pcf3, \
            nc.sbuf_tensor("pcb0", [128, 4 * D], BF16) as pcb0, nc.sbuf_tensor("pcb1", [128, 4 * D], BF16) as pcb1, \
            nc.sbuf_tensor("pcb2", [128, 4 * D], BF16) as pcb2, nc.sbuf_tensor("pcb3", [128, 4 * D], BF16) as pcb3:
        pcf = [pcf0, pcf1, pcf2, pcf3]
        pcb = [pcb0, pcb1, pcb2, pcb3]
        wi = 0
        for l in range(DEPTH):
            for o in range(0, 8 * NW, 4096):
                n = min(4096, 8 * NW - o)
                k.dma('sp', pcf[wi % 4][:, 0:n], w_in[l, :, o:o + n])
                k.cp(pcb[wi % 4][:, 0:n], pcf[wi % 4][:, 0:n], eng='act')
                k.dma('act', w16[l, :, o:o + n], pcb[wi % 4][:, 0:n], wkey="w16w%d" % (wi % 4))
                wi += 1
        if stage >= 2:
            for ch in range(ntab // 256):
                src = puv[ch * 256:(ch + 1) * 256, :].rearrange("(p r) n -> p (r n)", r=2)
                dst = puv16[ch * 256:(ch + 1) * 256, :].rearrange("(p r) n -> p (r n)", r=2)
                k.dma('sp', pcf[(ch + wi) % 4][:], src)
                k.cp(pcb[(ch + wi) % 4][:], pcf[(ch + wi) % 4][:], eng='act')
                k.dma('act', dst, pcb[(ch + wi) % 4][:], wkey="puv16w%d" % (ch % 4))
        with nc.sbuf_tensor("wmB0", [128, 8, 512], F32) as wmB0, nc.sbuf_tensor("wmB1", [128, 8, 512], F32) as wmB1, \
                nc.sbuf_tensor("screp", [128, 2, 8, 128], F32) as screp, \
                nc.sbuf_tensor("bmG", [128, 2, D], F32) as bmG:
            wmBs = [wmB0, wmB1]
            for g in range(2):
                for kc in range(8):
                    k.cp(screp[:, g, kc, :], scT[:, g, kc:kc + 1].to_broadcast([128, 128]))
            for l in range(nl):
                modF = modFs[l]; A1 = A1s[l]; A2 = A2s[l]
                k.dma('pool', n1t[:], n1g[l])
                k.dma('pool', n2t[:], n2g[l])
                k.dma('pool', bmF[:], bmodF[l])
                k.dma('pool', bmG[:], bmodG[l])
                parts = [0, 1, 3, 4]
                cnt = 0
                for pi, p in enumerate(parts):
                    for hf in range(2):
                        c0 = p * D + hf * 512
                        wm = wmBs[cnt % 2]
                        cnt += 1
                        k.dma('pool', wm[:], w_mod[l, :, c0:c0 + 512].rearrange("(kc p) n -> p kc n", p=128))
                        for q in range(4):
                            dc = hf * 4 + q
                            for kc in range(8):
                                k.mm(PS[q % 2][:, 0:2], wm[:, kc, q * 128:(q + 1) * 128], scT[:, :, kc], start=(kc == 0), stop=(kc == 7))
                            k.cp(modF[:, :, pi, dc], PS[q % 2][:, 0:2])
                for g in range(2):
                    k.tt(modF[:, g, :, :], modF[:, g, :, :], bmF[:], ALU.add)
                    k.ts(A1[:, g, :], modF[:, g, 1, :], 1.0, None, ALU.add)
                    k.tt(A1[:, g, :], A1[:, g, :], n1t[:], ALU.mult)
                    k.ts(A2[:, g, :], modF[:, g, 3, :], 1.0, None, ALU.add)
                    k.tt(A2[:, g, :], A2[:, g, :], n2t[:], ALU.mult)
                for gi, p in enumerate([2, 5]):
                    for hf in range(2):
                        c0 = p * D + hf * 512
                        wm = wmBs[cnt % 2]
                        cnt += 1
                        k.dma('pool', wm[:], w_mod[l, :, c0:c0 + 512].rearrange("(kc p) n -> p kc n", p=128))
                        for g in range(2):
                            pp = PS[2 + 2 * (cnt % 2) + g]
                            for kc in range(8):
                                k.mm(pp[:], screp[:, g, kc, :], wm[:, kc, :], start=(kc == 0), stop=(kc == 7))
                            k.tt(gB[:, g, gi, hf * 512:(hf + 1) * 512], pp[:], bmG[:, gi, hf * 512:(hf + 1) * 512], ALU.add)
                k.dma('pool', gBd[l], gB[:].rearrange("p a b d -> p (a b d)"))
        k.barrier_all()

    for l in range(nl):
        modF = modFs[l]; A1 = A1s[l]; A2 = A2s[l]
        k.dma('sp', gB[:].rearrange("p a b d -> p (a b d)"), gBd[l])
        if stage >= 1:
            phase_a(nc, k, l, locals())
        k.barrier_all()
        if stage >= 2:
            phase_b(nc, k, l, locals())
        k.barrier_all()
        if full and l < DEPTH - 1:
            for hf in range(2):
                k.allgather(XG[hf], XB[hf], [[0, 2, 4, 6], [1, 3, 5, 7]])
            for r in range(4):
                for j in range(4):
                    src = XG[j // 2][r * 256 + (j % 2) * 128:r * 256 + (j % 2) * 128 + 128, :]
                    k.dma('sp', Xd[512 + r * 512 + j * 128:512 + r * 512 + (j + 1) * 128, :], src)
            k.barrier_all()

    if stage < 2 or nl < DEPTH:
        for i in range(NTOK // 128):
            k.dma('sp', xt[:], Xd[i * 128:(i + 1) * 128, :])
            k.dma('sp', Y[i * 128:(i + 1) * 128, :], xt[:])
    k.barrier_all()
    return nc


def phase_a(nc, k, l, E):
    PS = E['PS']; Xd = E['Xd']; xt = E['xt']; xn = E['xn']; st = E['st']
    ident = E['ident']; identb = E['identb']; sel = E['sel']; prot = E['prot']
    tri01 = E['tri01']; amask = E['amask']; band = E['band']
    A1 = E['A1']; modF = E['modF']; gB = E['gB']; onec = E['onec']; epsc = E['epsc']
    rmsnorm_tile = E['rmsnorm_tile']
    w16 = E['w16']; w_out = E['w_out']
    from contextlib import ExitStack
    es = ExitStack()

    def sb(name, shape, dt=F32):
        return es.enter_context(nc.sbuf_tensor("%s_%d" % (name, l), list(shape), dt))

    with es:
        hT = sb("hT", [128, 8, 512], BF16)
        wfm = [sb("wfm%d" % i, [128, 8, 128], BF16) for i in range(2)]
        wtm = [sb("wtm0", [128, 8, 512], BF16)]
        WO = sb("WO", [128, 6, D], BF16)
        WOP = sb("WOP", [64, 4, D], BF16)
        ropeC = sb("ropeC", [128, 512])
        ropeS = sb("ropeS", [128, 512])
        QAT = sb("QAT", [128, 4, 2048], BF16)
        KAT = sb("KAT", [128, 2, 2048], BF16)
        VA = sb("VA", [128, 16, 2, 66], BF16)
        KCT = sb("KCT", [128, 2, 256], BF16)
        VC = sb("VC", [128, 2, 2, 66], BF16)
        QMT = sb("QMT", [128, 2, 2048], BF16)
        KMT = sb("KMT", [128, 2, 2048], BF16)
        VM = sb("VM", [128, 16, 4, 66], BF16)
        OM = sb("OM", [128, 16, 256], BF16)
        XP = sb("XP", [128, 16, 256], BF16)
        GPI = sb("GPI", [128, 16, 36])
        GPF = sb("GPF", [128, 16, 36])
        A_tok = sb("A_tok", [128, 16, 36])
        B_tok = sb("B_tok", [128, 16, 36])
        E_tok = sb("E_tok", [128, 16, 36])
        RA = sb("RA", [36, 2048])
        RB = sb("RB", [36, 2048])
        M0r = sb("M0r", [36, 1])
        M0b = sb("M0b", [128, 8])
        BT = sb("BT", [36, 1])
        MF = sb("MF", [36, 1])
        C0A = sb("C0A", [128, 2, 2, 66], BF16)
        gbt = sb("gbt", [128, 16])
        sinkE = sb("sinkE", [128, 8])
        mngt = sb("mngt", [128, 256])
        PW = sb("PW", [64, 4, 64], BF16)
        psc = sb("psc", [64, 4])
        qf = sb("qf", [128, 512])
        Dbuf = [sb("Dbuf%d" % i, [128, 512]) for i in range(2)]
        t1, t2 = Dbuf
        Wbuf = [sb("Wbuf%d" % i, [128, 512], BF16) for i in range(2)]
        OTb = [sb("OTb%d" % i, [65, 512]) for i in range(2)]
        HM = sb("HM", [128, 4, 256])
        ATTT = sb("ATTT", [128, 4, 512], BF16)
        MLST = sb("MLST", [128, 2, 512], BF16)
        PLT = sb("PLT", [64, 4, 512], BF16)
        DT = sb("DTb", [64, 128], BF16)
        sm = sb("sm", [128, 16])
        kcf = sb("kcf", [128, 2, 128])
        ktok = sb("ktok", [128, 128])
        vtok = sb("vtok", [128, 128])
        KMtok = sb("KMtok", [128, 2, 256])
        MLB = sb("MLB", [128, 8])
        Ftok = sb("Ftok", [128, 2, 8])
        kw_ = sb("kw", [128, 64], BF16)
        cst = sb("cst", [64, 65])

        k.dma('sp', gbt[:], E['gateb'][l])
        k.dma('sp', sinkE[:], E['sinkb'][l])
        k.act(sinkE[:], sinkE[:], AF.Exp)
        k.dma('sp', mngt[:], E['mng'][l])
        k.dma('pool', PW[:], E['poolw'][l])
        k.dma('sp', psc[:], E['pscale'][l])
        k.dma('pool', WO[:], w_out[l, 0:768, :].rearrange("(kc p) n -> p kc n", p=128))
        k.dma('pool', WOP[:], w_out[l, 768:1024, :].rearrange("(g p) n -> p g n", p=64))
        k.dma('pool', C0A[:, :, :, 0:65], E['c0a_d'][l])
        k.memset(VA[:, :, :, 64:65], 1.0)
        k.memset(VC[:, :, :, 64:65], 1.0)
        k.memset(VM[:, :, :, 64:65], 1.0)
        k.memset(GPI[:], 0.0)
        k.memset(GPF[:], 0.0)
        for j in range(2):
            k.dma('sp', kcf[:, j, :], E['cachek'][l, j * 128:(j + 1) * 128, :])
        for j in range(2):
            k.tr(PS[0][:, j * 128:(j + 1) * 128], kcf[:, j, :], ident[:])
        k.cp(KCT[:, 0, :], PS[0][:, 0:256])
        for j in range(2):
            k.tr(PS[1][0:64, j * 128:(j + 1) * 128], kcf[:, j, 64:128], ident[:])
        k.cp(KCT[0:64, 1, :], PS[1][0:64, 0:256])
        for j in range(2):
            k.tr(PS[2][0:64, j * 128:(j + 1) * 128], kcf[:, j, 0:64], ident[:])
        k.cp(t1[0:64, 0:256], PS[2][0:64, 0:256])
        k.cp(Wbuf[0][0:64, 0:256], t1[0:64, 0:256])
        k.dma('sp', KCT[64:128, 1, :], Wbuf[0][0:64, 0:256])
        for j in range(2):
            k.dma('sp', vtok[:], E['cachev'][l, j * 128:(j + 1) * 128, :])
            k.cp(VC[:, j, :, 0:64], vtok[:].rearrange("p (a b) -> p a b", a=2))

        for si, (t0, L, g) in enumerate(SEQS):
            nt = L // 128
            rope = (g == 1)
            if SUB < 1:
                continue
            GS = min(L, 512)
            ntg = GS // 128
            ngrp = L // GS
            for gi in range(ngrp):
                for ti in range(ntg):
                    tile = gi * ntg + ti
                    rmsnorm_tile(t0 + tile * 128)
                    for half in range(2):
                        for q in range(4):
                            dc = half * 4 + q
                            k.tr(PS[half][:, q * 128:(q + 1) * 128], xn[:, dc * 128:(dc + 1) * 128], ident[:])
                        for q in range(4):
                            dc = half * 4 + q
                            k.act(hT[:, dc, ti * 128:(ti + 1) * 128], PS[half][:, q * 128:(q + 1) * 128], AF.Identity,
                                  bias=modF[:, g, 0, dc:dc + 1], scale=A1[:, g, dc:dc + 1])
                c0 = gi * GS
                if DBG < 'b':
                    continue
                if rope:
                    k.dma('sp', ropeC[:, 0:GS], E['ropeC_d'][:, c0:c0 + GS])
                    k.dma('sp', ropeS[:, 0:GS], E['ropeS_d'][:, c0:c0 + GS])
                for ob in range(10):
                    wb = wfm[ob % 2]
                    k.dma('sp', wb[:], w16[l, :, 8 * ob * 128:8 * (ob + 1) * 128].rearrange("p (kc n) -> p kc n", kc=8))
                    pp = PS[2 + (ob % 2)]
                    for kc in range(8):
                        k.mm(pp[:, 0:GS], wb[:, kc, :], hT[:, kc, 0:GS], start=(kc == 0), stop=(kc == 7))
                    if ob < 4:
                        dst = QAT[:, ob, c0:c0 + GS]
                    elif ob < 6:
                        dst = KAT[:, ob - 4, c0:c0 + GS]
                    elif ob < 8:
                        dst = QMT[:, ob - 6, c0:c0 + GS]
                    else:
                        dst = KMT[:, ob - 8, c0:c0 + GS]
                    if rope and ob < 6:
                        k.cp(qf[:, 0:GS], pp[:, 0:GS], eng='act')
                        k.mm(PS[4][:, 0:GS], prot[:], qf[:, 0:GS])
                        k.tt(t1[:, 0:GS], PS[4][:, 0:GS], ropeS[:, 0:GS], ALU.mult)
                        k.tt(t2[:, 0:GS], qf[:, 0:GS], ropeC[:, 0:GS], ALU.mult, eng='pool')
                        k.tt(dst, t1[:, 0:GS], t2[:, 0:GS], ALU.add)
                    else:
                        k.cp(dst, pp[:, 0:GS], eng='act')
                if DBG < 'c':
                    continue
                tmb = [(1280, 400), (1680, 512)] + ([(2192, 384)] if not rope else [])
                for bi, (cb, ncol) in enumerate(tmb):
                    wb = wtm[0]
                    k.dma('sp', wb[:, :, 0:ncol], w16[l, :, 8 * cb:8 * (cb + ncol)].rearrange("p (kc n) -> p kc n", kc=8))
                    for ti in range(ntg):
                        tile = gi * ntg + ti
                        pp = PS[5 + (ti % 2)]
                        for kc in range(8):
                            k.mm(pp[:, 0:ncol], hT[:, kc, ti * 128:(ti + 1) * 128], wb[:, kc, 0:ncol], start=(kc == 0), stop=(kc == 7))
                        if DBG < 'd':
                            continue
                        if bi == 0:
                            D2 = os.environ.get('KD2', '1234')
                            if '1' in D2:
                                k.cp(VA[:, tile, :, 0:64], pp[:, 0:128].rearrange("p (a b) -> p a b", a=2))
                            if not rope and '2' in D2:
                                k.cp(vtok[:], pp[:, 0:128])
                                k.dma('sp', E['NV'][si, l, tile * 128:(tile + 1) * 128, :], vtok[:])
                            if '3' in D2:
                                k.cp(VM[:, tile, :, 0:64], pp[:, 128:384].rearrange("p (a b) -> p a b", a=4))
                            if '4' in D2:
                                k.tt(GPI[:, tile, 0:4], pp[:, 384:388], gbt[:, 0:4], ALU.add)
                                k.tt(GPF[:, tile, 0:4], pp[:, 388:392], gbt[:, 4:8], ALU.add)
                                k.tt(GPI[:, tile, 32:36], pp[:, 392:396], gbt[:, 8:12], ALU.add)
                                k.tt(GPF[:, tile, 32:36], pp[:, 396:400], gbt[:, 12:16], ALU.add)
                        elif bi == 1 and DBG >= 'e':
                            k.act(OM[:, tile, :], pp[:, 0:256], AF.Sigmoid)
                            k.cp(XP[:, tile, :], pp[:, 256:512], eng='act')
                        elif bi == 2 and DBG >= 'f':
                            k.cp(ktok[:], pp[:, 0:128])
                            k.dma('sp', E['NK'][si, l, tile * 128:(tile + 1) * 128, :], ktok[:])
                            k.cp(KMtok[:, tile, :], pp[:, 128:384])

            if SUB < 2:
                continue
            if rope:
                k.dma('sp', M0r[:], E['m0r_d'][l])
                k.dma('sp', M0b[:], E['m0b_d'][l])
            else:
                k.memset(M0r[:], 0.0)
                k.memset(M0b[:], 0.0)
            for tile in range(nt):
                k.tr(PS[0][0:36, 0:128], GPI[:, tile, :], ident[:])
                k.cp(RA[:, tile * 128:(tile + 1) * 128], PS[0][0:36, 0:128])
                k.tr(PS[1][0:36, 0:128], GPF[:, tile, :], ident[:])
                k.cp(RB[:, tile * 128:(tile + 1) * 128], PS[1][0:36, 0:128], eng='act')
            k.act(RB[:, 0:L], RB[:, 0:L], AF.Exp, scale=-1.0)
            k.act(RB[:, 0:L], RB[:, 0:L], AF.Ln, bias=onec[0:36, 0:1])
            k.scan(RB[0:4, 0:L], RB[0:4, 0:L], RB[0:4, 0:L], 0.0, ALU.add, ALU.max)
            k.scan(RB[32:36, 0:L][:, ::-1], RB[32:36, 0:L][:, ::-1], RB[32:36, 0:L][:, ::-1], 0.0, ALU.add, ALU.max)
            k.tt(RA[:, 0:L], RA[:, 0:L], RB[:, 0:L], ALU.add)
            k.cp(BT[0:4, :], RB[0:4, L - 1:L])
            k.cp(BT[32:36, :], RB[32:36, 0:1])
            for tile in range(nt):
                k.tr(PS[0][:, 0:36], RA[:, tile * 128:(tile + 1) * 128], ident[0:36, 0:36])
                k.cp(A_tok[:, tile, :], PS[0][:, 0:36])
                k.tr(PS[1][:, 0:36], RB[:, tile * 128:(tile + 1) * 128], ident[0:36, 0:36])
                k.cp(B_tok[:, tile, :], PS[1][:, 0:36], eng='act')
            k.scan(RB[0:4, 0:L], RA[0:4, 0:L], RA[0:4, 0:L], M0r[0:4, 0:1], ALU.max, ALU.max)
            k.scan(RB[32:36, 0:L][:, ::-1], RA[32:36, 0:L][:, ::-1], RA[32:36, 0:L][:, ::-1], M0r[32:36, 0:1], ALU.max, ALU.max)
            for tile in range(nt):
                k.tr(PS[0][:, 0:36], RB[:, tile * 128:(tile + 1) * 128], ident[0:36, 0:36])
                k.cp(E_tok[:, tile, :], PS[0][:, 0:36])
            k.tt(E_tok[:, 0:nt, :], B_tok[:, 0:nt, :], E_tok[:, 0:nt, :], ALU.subtract)
            k.act(E_tok[:, 0:nt, :], E_tok[:, 0:nt, :], AF.Exp)

            if not rope and SUB >= 3:
                k.tt(MF[0:4, :], RB[0:4, L - 1:L], BT[0:4, :], ALU.subtract)
                k.tt(MF[32:36, :], RB[32:36, 0:1], BT[32:36, :], ALU.subtract)
                for d in range(2):
                    k.dma('sp', E['NM'][si, l, d, :].rearrange("(h o) -> h o", o=1), MF[d * 32:d * 32 + 4, :])
                for d in range(2):
                    col = (L - 1) if d == 0 else 0
                    for h in range(4):
                        k.mm(PS[2][:, d * 4 + h:d * 4 + h + 1], sel[:, d * 4 + h, :], RB[:, col:col + 1])
                k.cp(MLB[:], PS[2][:, 0:8])
                for j in range(nt):
                    for d in range(2):
                        k.tt(Ftok[:, j, d * 4:d * 4 + 4], A_tok[:, j, d * 32:d * 32 + 4], MLB[:, d * 4:d * 4 + 4], ALU.subtract)
                k.act(Ftok[:], Ftok[:], AF.Exp)
                for d in range(2):
                    for h in range(4):
                        for j in range(nt):
                            k.ts(kw_[:], KMtok[:, j, h * 64:(h + 1) * 64], Ftok[:, j, d * 4 + h:d * 4 + h + 1], 0.125, ALU.mult, ALU.mult)
                            k.mm(PS[3][0:64, 0:65], kw_[:], VM[:, j, h, 0:65], start=(j == 0), stop=(j == nt - 1))
                        k.cp(cst[:], PS[3][0:64, 0:65])
                        k.dma('sp', E['NC_'][si, l, d, h], cst[:, 0:64])
                        k.dma('sp', E['NN'][si, l, d, h, :].rearrange("(p o) -> p o", o=1), cst[:, 64:65])

            if SUB < 4:
                continue
            for ci in range(ngrp):
                c0t = ci * ntg
                c0 = c0t * 128
                def capture(fn):
                    k.defer = []
                    fn()
                    items = k.defer
                    k.defer = None
                    return items

                def emit_list(items):
                    for it in items:
                        k.emit(it)

                def att_blocks(h):
                    kvh = h // 4
                    pair = h // 2
                    base = (h % 2) * 64
                    var = 0 if kvh * 64 == base else 1
                    po = PS[4 + (h % 2)]
                    keyl = []
                    if rope:
                        keyl += [('c', 0), ('c', 1)]
                        for j in range(max(0, c0t - 1), min(nt, c0t + ntg + 1)):
                            keyl.append(('b', j))
                    else:
                        keyl += [('f', j) for j in range(nt)]

                    def a_stage1(n_):
                        kind, j = keyl[n_]
                        ps = PS[n_ % 2]
                        wbf = Wbuf[n_ % 2]
                        if kind == 'c':
                            lo, hi = 0, ntg
                            k.mm(ps[:, 0:GS], KCT[base:base + 64, var, j * 128:(j + 1) * 128], QAT[base:base + 64, pair, c0:c0 + GS])
                        elif kind == 'f':
                            lo, hi = 0, ntg
                            k.mm(ps[:, 0:GS], KAT[base:base + 64, var, j * 128:(j + 1) * 128], QAT[base:base + 64, pair, c0:c0 + GS])
                        else:
                            ilo = max(c0t, j - 1)
                            ihi = min(c0t + ntg - 1, j + 1)
                            lo, hi = ilo - c0t, ihi - c0t + 1
                            ncol = (hi - lo) * 128
                            k.mm(ps[:, lo * 128:hi * 128], KAT[base:base + 64, var, j * 128:(j + 1) * 128],
                                 QAT[base:base + 64, pair, c0 + lo * 128:c0 + hi * 128], start=True, stop=False)
                            mo = (ilo - (j - 1)) * 128
                            k.mm(ps[:, lo * 128:hi * 128], identb[:], amask[:, mo:mo + ncol], start=False, stop=True)
                        k.act(wbf[:, lo * 128:hi * 128], ps[:, lo * 128:hi * 128], AF.Exp, scale=0.125)
                        return lo, hi

                    def a_stage2(n_, lo, hi):
                        kind, j = keyl[n_]
                        wbf = Wbuf[n_ % 2]
                        vsrc = VC[:, j, kvh, 0:65] if kind == 'c' else VA[:, j, kvh, 0:65]
                        k.mm(po[0:65, lo * 128:hi * 128], vsrc, wbf[:, lo * 128:hi * 128], start=(n_ == 0),
                             stop=(n_ == len(keyl) - 1), skip_group_check=True)
                    rng = a_stage1(0)
                    for n_ in range(len(keyl)):
                        nrng = a_stage1(n_ + 1) if n_ + 1 < len(keyl) else None
                        a_stage2(n_, *rng)
                        rng = nrng

                def att_epi(h):
                    pair = h // 2
                    hb = (h % 2) * 64
                    po = PS[4 + (h % 2)]
                    ot = OTb[h % 2]
                    k.cp(ot[:, 0:GS], po[0:65, 0:GS])
                    pt4 = PS[6 + (h % 2)]
                    for ti in range(ntg):
                        k.tr(pt4[:, ti * 66:ti * 66 + 65], ot[:, ti * 128:(ti + 1) * 128], ident[0:65, 0:65])
                    pv = pt4[:, 0:ntg * 66].rearrange("p (t c) -> p t c", c=66)
                    k.ts(sm[:, 0:ntg], pv[:, :, 64], sinkE[:, h:h + 1], None, ALU.add)
                    k.recip(sm[:, 4:4 + ntg], sm[:, 0:ntg])
                    k.tt(HM[:, 0:ntg, hb:hb + 64], pv[:, :, 0:64], sm[:, 4:4 + ntg].unsqueeze(2).broadcast_to([128, ntg, 64]), ALU.mult)
                    if h % 2 == 1:
                        for ti in range(ntg):
                            k.tr(PS[2][:, ti * 128:(ti + 1) * 128], HM[:, ti, 0:128], ident[:])
                        k.cp(ATTT[:, pair, 0:GS], PS[2][:, 0:GS], eng='act')

                if SUB >= 5:
                    blk = [capture(lambda h=h: att_blocks(h)) for h in range(8)]
                    epi = [capture(lambda h=h: att_epi(h)) for h in range(8)]
                    emit_list(blk[0])
                    for h in range(8):
                        if h + 1 < 8:
                            emit_list(blk[h + 1])
                        emit_list(epi[h])

                def ml_pro(h, d):
                    pair = h // 2
                    base = (h % 2) * 64
                    po = PS[4 + d]
                    if rope:
                        k.mm(PS[3][:, 0:GS], sel[:, d * 4 + h, :], RB[:, c0:c0 + GS])
                        k.act(Dbuf[0][base:base + 64, 0:GS], PS[3][base:base + 64, 0:GS], AF.Exp, scale=-1.0,
                              bias=M0b[base:base + 64, d * 4 + h:d * 4 + h + 1])
                        k.tt(Wbuf[0][base:base + 64, 0:GS], QMT[base:base + 64, pair, c0:c0 + GS], Dbuf[0][base:base + 64, 0:GS], ALU.mult)
                        k.mm(po[0:65, 0:GS], C0A[base:base + 64, d, pair, 0:65], Wbuf[0][base:base + 64, 0:GS], start=True, stop=False)
                    k.mm(PS[3][:, 0:GS], sel[:, d * 4 + h, :], RB[:, c0:c0 + GS])
                    k.cp(qf[:, 0:GS], PS[3][:, 0:GS], eng='act')

                def ml_blocks(h, d):
                    pair = h // 2
                    base = (h % 2) * 64
                    po = PS[4 + d]
                    js = list(range(0, c0t + ntg)) if d == 0 else list(range(nt - 1, c0t - 1, -1))

                    def m_rng(n_):
                        r = js[n_] - c0t
                        if 0 <= r < ntg:
                            return ((r, ntg) if d == 0 else (0, r + 1)), r
                        return (0, ntg), None

                    def m_stage1(n_):
                        j = js[n_]
                        (lo, hi), r = m_rng(n_)
                        ps = PS[n_ % 3]
                        db = Dbuf[n_ % 2]
                        k.mm(ps[:, lo * 128:hi * 128], KMT[base:base + 64, pair, j * 128:(j + 1) * 128],
                             QMT[base:base + 64, pair, c0 + lo * 128:c0 + hi * 128])
                        k.act(db[:, lo * 128:hi * 128], qf[:, lo * 128:hi * 128], AF.Exp, scale=-1.0, bias=A_tok[:, j, d * 32 + h:d * 32 + h + 1])

                    def m_stage2(n_, first_):
                        j = js[n_]
                        (lo, hi), r = m_rng(n_)
                        ps = PS[n_ % 3]
                        db = Dbuf[n_ % 2]
                        wbf = Wbuf[n_ % 2]
                        k.stt(wbf[:, lo * 128:hi * 128], ps[:, lo * 128:hi * 128], 0.125, db[:, lo * 128:hi * 128], ALU.mult, ALU.mult)
                        if r is not None:
                            k.tt(wbf[:, r * 128:(r + 1) * 128], wbf[:, r * 128:(r + 1) * 128], tri01[:, d, :], ALU.mult, eng='pool')
                        k.mm(po[0:65, lo * 128:hi * 128], VM[:, j, h, 0:65], wbf[:, lo * 128:hi * 128], start=first_, stop=(n_ == len(js) - 1),
                             skip_group_check=True)
                    first = not rope
                    m_stage1(0)
                    for n_ in range(len(js)):
                        if n_ + 1 < len(js):
                            m_stage1(n_ + 1)
                        m_stage2(n_, first)
                        first = False

                def ml_epi(h, d):
                    po = PS[4 + d]
                    ot = OTb[d]
                    k.cp(ot[:, 0:GS], po[0:65, 0:GS], eng='act')
                    pt4 = PS[6 + d]
                    for ti in range(ntg):
                        k.tr(pt4[:, ti * 66:ti * 66 + 65], ot[:, ti * 128:(ti + 1) * 128], ident[0:65, 0:65])
                    pv = pt4[:, 0:ntg * 66].rearrange("p (t c) -> p t c", c=66)
                    k.ts(sm[:, 0:ntg], pv[:, :, 64], -1.0, None, ALU.mult)
                    k.tt(sm[:, 0:ntg], sm[:, 0:ntg], pv[:, :, 64], ALU.max)
                    k.tt(sm[:, 0:ntg], sm[:, 0:ntg], E_tok[:, c0t:c0t + ntg, d * 32 + h], ALU.max)
                    k.recip(sm[:, 4:4 + ntg], sm[:, 0:ntg])
                    if d == 0:
                        k.tt(HM[:, 0:ntg, h * 64:(h + 1) * 64], pv[:, :, 0:64],
                             sm[:, 4:4 + ntg].unsqueeze(2).broadcast_to([128, ntg, 64]), ALU.mult)
                    else:
                        for ti in range(ntg):
                            k.stt(HM[:, ti, h * 64:(h + 1) * 64], pv[:, ti, 0:64], sm[:, 4 + ti:5 + ti], HM[:, ti, h * 64:(h + 1) * 64], ALU.mult, ALU.add)

                if SUB >= 6:
                    hd = [(h, d) for h in range(4) for d in range(2)]
                    pro = [capture(lambda h=h, d=d: ml_pro(h, d)) for (h, d) in hd]
                    blk = [capture(lambda h=h, d=d: ml_blocks(h, d)) for (h, d) in hd]
                    epi = [capture(lambda h=h, d=d: ml_epi(h, d)) for (h, d) in hd]
                    emit_list(pro[0])
                    for i in range(len(hd)):
                        emit_list(blk[i])
                        if i + 1 < len(hd):
                            emit_list(pro[i + 1])
                        emit_list(epi[i])
                for ti in range(ntg):
                    for h in range(4):
                        k.ttr(Dbuf[0][:, 0:64], HM[:, ti, h * 64:(h + 1) * 64], HM[:, ti, h * 64:(h + 1) * 64], ALU.mult, ALU.add, sm[:, 8 + h:9 + h])
                    k.act(sm[:, 12:16], sm[:, 8:12], AF.Ln, bias=epsc[:, 0:1], scale=1.0 / 64)
                    k.act(sm[:, 12:16], sm[:, 12:16], AF.Exp, scale=-0.5)
                    for h in range(4):
                        k.ts(HM[:, ti, h * 64:(h + 1) * 64], HM[:, ti, h * 64:(h + 1) * 64], sm[:, 12 + h:13 + h], None, ALU.mult)
                    k.tt(HM[:, ti, :], HM[:, ti, :], mngt[:], ALU.mult)
                    k.tt(HM[:, ti, :], HM[:, ti, :], OM[:, c0t + ti, :], ALU.mult)
                    for p2 in range(2):
                        k.tr(PS[p2][:, ti * 128:(ti + 1) * 128], HM[:, ti, p2 * 128:(p2 + 1) * 128], ident[:])
                for p2 in range(2):
                    k.cp(MLST[:, p2, 0:GS], PS[p2][:, 0:GS], eng='act')

                for ti in range(ntg):
                    i = c0t + ti
                    for gq in range(4):
                        jl = [j for j in (i - 1, i, i + 1) if 0 <= j < nt]
                        for n_, j in enumerate(jl):
                            if j == i - 1:
                                kind = 3
                            elif j == i + 1:
                                kind = 4
                            else:
                                kind = 0 if i == 0 else (2 if i == nt - 1 else 1)
                            k.mm(PS[2][0:64, 0:128], XP[:, j, gq * 64:(gq + 1) * 64], band[:, gq, kind, :], start=(n_ == 0), stop=(n_ == len(jl) - 1))
                        k.cp(DT[:], PS[2][0:64, 0:128], eng='act')
                        k.mm(PS[3][0:64, 0:128], PW[:, gq, :], DT[:])
                        k.ts(PLT[:, gq, ti * 128:(ti + 1) * 128], PS[3][0:64, 0:128], psc[:, gq:gq + 1], None, ALU.mult)

                for ti in range(ntg):
                    tok = t0 + (c0t + ti) * 128
                    k.dma('sp', xt[:], Xd[tok:tok + 128, :])
                    for half in range(2):
                        pp = PS[6 + half]
                        for kc in range(4):
                            k.mm(pp[:], ATTT[:, kc, ti * 128:(ti + 1) * 128], WO[:, kc, half * 512:(half + 1) * 512], start=(kc == 0), stop=False)
                        for kc in range(2):
                            k.mm(pp[:], MLST[:, kc, ti * 128:(ti + 1) * 128], WO[:, 4 + kc, half * 512:(half + 1) * 512], start=False, stop=False)
                        for gq in range(4):
                            k.mm(pp[:], PLT[:, gq, ti * 128:(ti + 1) * 128], WOP[:, gq, half * 512:(half + 1) * 512], start=False, stop=(gq == 3))
                        k.tt(xn[:, half * 512:(half + 1) * 512], pp[:], gB[:, g, 0, half * 512:(half + 1) * 512], ALU.mult)
                    k.tt(xn[:], xn[:], xt[:], ALU.add)
                    k.dma('sp', Xd[tok:tok + 128, :], xn[:])


def phase_b(nc, k, l, E):
    PS = E['PS']; Xd = E['Xd']; st = E['st']
    ident = E['ident']; A2 = E['A2']; modF = E['modF']; gB = E['gB']; epsc = E['epsc']
    iota16 = E['iota16']; qidx = E['qidx']
    from contextlib import ExitStack
    es = ExitStack()

    def sb(name, shape, dt=F32):
        return es.enter_context(nc.sbuf_tensor("%s_%d" % (name, l), list(shape), dt))

    with es:
        WQ = sb("WQ", [128, 8, 2048], BF16)
        fngt = sb("fngt", [128, D])
        junk = sb("junkb", [128, D])
        k.dma('sp', fngt[:], E['fng'])
        SH4 = sb("SH4", [128, 8, 16], U32)
        M15 = sb("M15", [128, 8, 16], U32)
        k.memset(SH4[:], 4)
        k.memset(M15[:], 15)
        KT = sb("KT", [128, 16, 128], BF16)
        xnb = sb("xnb", [128, D])
        stb = sb("stb", [128, 8])
        hTf = sb("hTf", [128, 8, 128])
        hTb = sb("hTb", [128, 8, 128], BF16)
        qT = sb("qT", [128, 16, 128], BF16)
        SC = sb("SC", [128, 16, 128])
        SC2 = sb("SC2", [128, 128])
        TV = sb("TV", [128, 8, 2, 16])
        TI = sb("TI", [128, 8, 2, 16], U32)
        TIf = sb("TIf", [128, 8, 2, 16], BF16)
        CAND = sb("CAND", [128, 8, 256])
        CAND2 = sb("CAND2", [128, 256])
        BSv = sb("BSv", [128, 8, 16])
        BP = sb("BP", [128, 8, 16], U32)
        K1 = sb("K1", [128, 8, 16], U32)
        K2 = sb("K2", [128, 8, 16], U32)
        K1f = sb("K1f", [128, 8, 16], BF16)
        K2f = sb("K2f", [128, 8, 16], BF16)
        EQ = sb("EQ", [128, 8, 16, 16], BF16)
        iob = sb("iob", [128, 16], BF16)
        k.cp(iob[:], iota16[:])
        I1f = sb("I1f", [128, 8, 16])
        I2f = sb("I2f", [128, 8, 16])
        Zs = sb("Zs", [128, 8])
        YO = sb("YO", [128, D])
        XT = [sb("XTb%d" % i, [128, D]) for i in range(2)]
        H2 = [sb("H2b%d" % i, [128, D]) for i in range(2)]
        EX = [sb("EXi%d" % i, [128, 128], I32) for i in range(2)]
        GTs = [sb("GT%d" % i, [128, 8, 16]) for i in range(2)]
        NB = 17
        UV = [sb("UV%d" % i, [128, 2 * D], BF16) for i in range(NB)]
        DG = [sb("DG%d" % i, [128, 128], BF16) for i in range(4)]
        AVs = [sb("AVs%d" % i, [128, 1]) for i in range(8)]
        GAs = [sb("GAs%d" % i, [128, 1]) for i in range(8)]
        puv = E['puv16']
        k.dma('pool', WQ[:], E['wq'][l].rearrange("(kc p) n -> p kc n", p=128))
        k.dma('pool', KT[:], E['keysT'][l])
        tiles = []
        for si, (t0, L, g) in enumerate(SEQS):
            quarter = E['full'] and g == 1
            for tile in range(4 if quarter else L // 128):
                tiles.append((t0 + tile * 128, g, (tile if quarter else None)))
        XB = E['XB']

        def route(info, b):
            k.defer = []
            tok, g, gidx = info
            xt = XT[b]; h2 = H2[b]; EXi = EX[b]; GT = GTs[b]
            if gidx is None:
                k.dma('sp', xt[:], Xd[tok:tok + 128, :])
            else:
                k.gather(xt[:], Xd, qidx[:, gidx:gidx + 1], 0)
            k.ttr(xnb[:], xt[:], xt[:], ALU.mult, ALU.add, stb[:, 0:1])
            k.act(stb[:, 1:2], stb[:, 0:1], AF.Ln, bias=epsc[:, 0:1], scale=1.0 / D)
            k.act(stb[:, 2:3], stb[:, 1:2], AF.Exp, scale=-0.5)
            k.act(xnb[:], xt[:], AF.Copy, scale=stb[:, 2:3])
            for half in range(2):
                for q in range(4):
                    dc = half * 4 + q
                    k.tr(PS[2 + half][:, q * 128:(q + 1) * 128], xnb[:, dc * 128:(dc + 1) * 128], ident[:])
                for q in range(4):
                    dc = half * 4 + q
                    k.act(hTf[:, dc, :], PS[2 + half][:, q * 128:(q + 1) * 128], AF.Identity,
                          bias=modF[:, g, 2, dc:dc + 1], scale=A2[:, g, dc:dc + 1])
                    k.act(hTb[:, dc, :], PS[2 + half][:, q * 128:(q + 1) * 128], AF.Identity,
                          bias=modF[:, g, 2, dc:dc + 1], scale=A2[:, g, dc:dc + 1])
            for half in range(2):
                for q in range(4):
                    dc = half * 4 + q
                    k.tr(PS[4 + half][:, q * 128:(q + 1) * 128], hTf[:, dc, :], ident[:])
                k.cp(h2[:, half * 512:(half + 1) * 512], PS[4 + half][:], eng='act')
            for hc in range(16):
                pp = PS[6 + (hc % 2)]
                for kc in range(8):
                    k.mm(pp[:, 0:128], WQ[:, kc, hc * 128:(hc + 1) * 128], hTb[:, kc, :], start=(kc == 0), stop=(kc == 7))
                k.cp(qT[:, hc, :], pp[:, 0:128], eng='act')
            for q4 in range(4):
                pp = PS[2 + (q4 % 2)]
                for q in range(4):
                    hc = q4 * 4 + q
                    k.mm(pp[:, q * 128:(q + 1) * 128], qT[:, hc, :], KT[:, hc, :])
                k.cp(SC[:, q4 * 4:(q4 + 1) * 4, :], pp[:].rearrange("p (a b) -> p a b", a=4), eng='act')
            for hc in range(16):
                hh, cc = hc // 2, hc % 2
                k.max8(TV[:, hh, cc, 0:8], SC[:, hc, :])
                k.maxidx(TI[:, hh, cc, 0:8], TV[:, hh, cc, 0:8], SC[:, hc, :])
                k.mrep(SC2[:], TV[:, hh, cc, 0:8], SC[:, hc, :], -1e30)
                k.max8(TV[:, hh, cc, 8:16], SC2[:])
                k.maxidx(TI[:, hh, cc, 8:16], TV[:, hh, cc, 8:16], SC2[:])
            k.cp(TIf[:], TI[:])
            k.tt(CAND[:].rearrange("p h (a b) -> p h a b", a=16),
                 TV[:, :, 0, :].unsqueeze(3).broadcast_to([128, 8, 16, 16]),
                 TV[:, :, 1, :].unsqueeze(2).broadcast_to([128, 8, 16, 16]), ALU.add)
            for hh in range(8):
                k.max8(BSv[:, hh, 0:8], CAND[:, hh, :])
                k.maxidx(BP[:, hh, 0:8], BSv[:, hh, 0:8], CAND[:, hh, :])
                k.mrep(CAND2[:], BSv[:, hh, 0:8], CAND[:, hh, :], -1e30)
                k.max8(BSv[:, hh, 8:16], CAND2[:])
                k.maxidx(BP[:, hh, 8:16], BSv[:, hh, 8:16], CAND2[:])
            k.tt(GT[:], BSv[:], BSv[:, :, 0:1].broadcast_to([128, 8, 16]), ALU.subtract)
            k.act(GT[:], GT[:], AF.Exp)
            k.treduce(Zs[:], GT[:], AX.X, ALU.add)
            k.recip(Zs[:], Zs[:])
            k.tt(GT[:], GT[:], Zs[:].unsqueeze(2).broadcast_to([128, 8, 16]), ALU.mult)
            k.tt(K1[:], BP[:], SH4[:], ALU.logical_shift_right)
            k.tt(K2[:], BP[:], M15[:], ALU.bitwise_and)
            k.cp(K1f[:], K1[:])
            k.cp(K2f[:], K2[:])
            for (Kf, cc, If_) in ((K1f, 0, I1f), (K2f, 1, I2f)):
                k.tt(EQ[:], Kf[:].unsqueeze(3).broadcast_to([128, 8, 16, 16]),
                     iob[:].unsqueeze(1).unsqueeze(1).broadcast_to([128, 8, 16, 16]), ALU.is_equal)
                k.tt(EQ[:], EQ[:], TIf[:, :, cc, :].unsqueeze(2).broadcast_to([128, 8, 16, 16]), ALU.mult)
                k.treduce(If_[:], EQ[:], AX.X, ALU.add)
            k.stt(I1f[:], I1f[:], 128.0, I2f[:], ALU.mult, ALU.add)
            k.ts(I1f[:], I1f[:], float(l * 16384), None, ALU.add)
            k.cp(EXi[:], I1f[:].rearrange("p h k -> p (h k)"))
            items = k.defer
            k.defer = None
            return items

        def drain(items):
            if items:
                while items:
                    k.emit(items.pop(0))

        def pull(items):
            if items:
                k.emit(items.pop(0))

        def gather_loop(info, b, nxt):
            tok, g, gidx = info
            xt = XT[b]; h2 = H2[b]; EXi = EX[b]; GT = GTs[b]
            GTf = GT[:].rearrange("p h k -> p (h k)")

            def dot(s_):
                uv = UV[s_ % NB]
                k.gather(uv[:], puv, EXi[:, s_:s_ + 1], 0, skip=('dve',) if s_ >= NB else ())
                k.ttr(junk[:], uv[:, 0:D], h2[:], ALU.mult, ALU.add, AVs[s_ % 8][:])

            def gelu(s_):
                k.act(GAs[s_ % 8][:], AVs[s_ % 8][:], AF.Gelu)
            dot(0)
            dot(1)
            gelu(0)
            for s_ in range(128):
                ga = GAs[s_ % 8]
                dg = DG[s_ % 4]
                if s_ + 2 < 128:
                    dot(s_ + 2)
                pull(nxt)
                if s_ + 1 < 128:
                    gelu(s_ + 1)
                k.act(ga[:], ga[:], AF.Copy, scale=GTf[:, s_:s_ + 1])
                pull(nxt)
                k.act(dg[:], ident[:], AF.Copy, scale=ga[:, 0:1])
                pull(nxt)
                for half in range(2):
                    k.mm(PS[half][:], dg[:], UV[s_ % NB][:, D + half * 512:D + (half + 1) * 512], start=(s_ == 0), stop=(s_ == 127))
            for half in range(2):
                k.tt(YO[:, half * 512:(half + 1) * 512], PS[half][:], gB[:, g, 1, half * 512:(half + 1) * 512], ALU.mult)
            k.tt(YO[:], YO[:], xt[:], ALU.add)
            if l < DEPTH - 1:
                if gidx is None:
                    k.dma('sp', Xd[tok:tok + 128, :], YO[:])
                else:
                    k.dma('sp', XB[gidx // 2][(gidx % 2) * 128:(gidx % 2) * 128 + 128, :], YO[:])
            else:
                k.ttr(junk[:], YO[:], YO[:], ALU.mult, ALU.add, st[:, 4:5])
                k.act(st[:, 5:6], st[:, 4:5], AF.Ln, bias=epsc[:, 0:1], scale=1.0 / D)
                k.act(st[:, 6:7], st[:, 5:6], AF.Exp, scale=-0.5)
                k.stt(YO[:], YO[:], st[:, 6:7], fngt[:], ALU.mult, ALU.mult)
                k.dma('sp', E['Y'][tok:tok + 128, :], YO[:])

        drain(route(tiles[0], 0))
        for i, info in enumerate(tiles):
            nxt = route(tiles[i + 1], (i + 1) % 2) if i + 1 < len(tiles) else None
            gather_loop(info, i % 2, nxt)
            drain(nxt)


def _consts():
    c = {}
    c['ident'] = np.eye(128, dtype=np.float32)
    sel = np.zeros((36, 8, 128), np.float32)
    for d in range(2):
        for h in range(4):
            sel[d * 32 + h, d * 4 + h, :] = 1.0
    c['sel'] = sel
    half = 32
    inv = (10000.0 ** (-np.arange(0, half, 2, dtype=np.float32) / half)).astype(np.float32)
    t = np.arange(2048)
    row = (t // 64).astype(np.float32)
    col = (t % 64).astype(np.float32)
    C = np.zeros((64, 2048), np.float32)
    S = np.zeros((64, 2048), np.float32)
    angr = (row[None, :] * inv[:, None]).astype(np.float32)
    angc = (col[None, :] * inv[:, None]).astype(np.float32)
    C[0:16] = np.cos(angr); C[16:32] = np.cos(angr); C[32:48] = np.cos(angc); C[48:64] = np.cos(angc)
    S[0:16] = np.sin(angr); S[16:32] = np.sin(angr); S[32:48] = np.sin(angc); S[48:64] = np.sin(angc)
    c['ropeC'] = np.concatenate([C, C], 0)
    c['ropeS'] = np.concatenate([S, S], 0)
    P = np.zeros((64, 64), np.float32)
    for o in (0, 32):
        for i in range(16):
            P[o + i, o + 16 + i] = -1.0
            P[o + 16 + i, o + i] = 1.0
    P2 = np.zeros((128, 128), np.float32)
    P2[0:64, 0:64] = P
    P2[64:128, 64:128] = P
    c['prot'] = np.ascontiguousarray(P2.T)
    BIG = 30000.0
    s = np.arange(128)[:, None]
    tt_ = np.arange(128)[None, :]
    trif = np.where(s > tt_, BIG, 0.0).astype(np.float32)
    trib = np.where(s < tt_, BIG, 0.0).astype(np.float32)
    full = np.full((128, 128), BIG, np.float32)
    zero = np.zeros((128, 128), np.float32)
    c['tri01'] = np.stack([(s <= tt_).astype(np.float32), (s >= tt_).astype(np.float32)], 1)
    c['amask'] = np.concatenate([-trif, zero, -trib], 1).astype(np.float32)
    band = np.zeros((128, 4, 5, 128), np.float32)
    Lb = 384
    for gq, w in enumerate((2, 4, 8, 16)):
        def mat(L):
            M = np.zeros((L, L), np.float32)
            for t_ in range(L):
                lo = max(t_ - w // 2, 0); hi = min(t_ + w // 2, L)
                M[t_, lo:hi] = 1.0 / (hi - lo)
                M[t_, t_] -= 1.0
            return M
        M = mat(Lb)
        MT = M.T
        band[:, gq, 0] = MT[0:128, 0:128]
        band[:, gq, 1] = MT[128:256, 128:256]
        band[:, gq, 2] = MT[256:384, 256:384]
        band[:, gq, 3] = MT[0:128, 128:256]
        band[:, gq, 4] = MT[256:384, 128:256]
    c['band'] = band
    c['iota16'] = np.tile(np.arange(16, dtype=np.float32)[None, :], (128, 1))
    return c


_NC_CACHE = {}


def _prep_shared(inp):
    f = lambda a: np.ascontiguousarray(np.asarray(a, dtype=np.float32))
    sh = {}
    w_in = f(inp['w_in'])
    cols = np.concatenate([
        np.arange(0, 512), np.arange(512, 640), np.arange(576, 640), np.arange(512, 576),
        np.arange(768, 1024), np.arange(1024, 1280),
        np.arange(640, 768), np.arange(1280, 1536), np.arange(1792, 1808),
        np.arange(1536, 1792), np.arange(1808, 2064),
        np.arange(512, 640), np.arange(1024, 1280)])
    assert cols.size == NW
    wext = w_in[:, :, cols]
    blocks = [(ob * 128, 128) for ob in range(10)] + [(1280, 400), (1680, 512), (2192, 384)]
    wb_ = np.zeros((DEPTH, 128, 8 * NW), np.float32)
    for (c0, ncol) in blocks:
        blk = wext[:, :, c0:c0 + ncol].reshape(DEPTH, 8, 128, ncol).transpose(0, 2, 1, 3).reshape(DEPTH, 128, 8 * ncol)
        wb_[:, :, 8 * c0:8 * (c0 + ncol)] = blk
    sh['w_in'] = wb_
    sh['w_mod'] = f(inp['w_mod'])
    bm = f(inp['b_mod']).reshape(DEPTH, 6, 8, 128)
    sh['bmodF'] = np.ascontiguousarray(bm[:, [0, 1, 3, 4]].transpose(0, 3, 1, 2))
    sh['bmodG'] = np.ascontiguousarray(np.broadcast_to(f(inp['b_mod']).reshape(DEPTH, 1, 6, D)[:, :, [2, 5]], (DEPTH, 128, 2, D)))
    sh['n1g'] = np.ascontiguousarray(f(inp['norm1_g']).reshape(DEPTH, 8, 128).transpose(0, 2, 1))
    sh['n2g'] = np.ascontiguousarray(f(inp['norm2_g']).reshape(DEPTH, 8, 128).transpose(0, 2, 1))
    sh['gateb'] = np.ascontiguousarray(np.broadcast_to(f(inp['gate_b'])[:, None, :], (DEPTH, 128, 16)))
    sh['sinkb'] = np.ascontiguousarray(np.broadcast_to(f(inp['attn_sink'])[:, None, :], (DEPTH, 128, 8)))
    sh['mng'] = np.ascontiguousarray(np.broadcast_to(f(inp['mlstm_norm_g'])[:, None, :], (DEPTH, 128, 256)))
    sh['poolw'] = np.ascontiguousarray(f(inp['pool_w']).transpose(0, 2, 1, 3))
    sh['pscale'] = np.ascontiguousarray(f(inp['pool_scale']).reshape(DEPTH, 4, 64).transpose(0, 2, 1))
    sh['w_out'] = f(inp['w_out'])
    sh['wq'] = f(inp['peer_wq'])
    pk = f(inp['peer_keys'])
    sh['keysT'] = np.ascontiguousarray(pk.transpose(0, 4, 1, 2, 3).reshape(DEPTH, 128, 16, 128))
    sh['puv'] = np.concatenate([f(inp['peer_u']).reshape(DEPTH * 16384, D), f(inp['peer_v']).reshape(DEPTH * 16384, D)], 1)
    sh['fng'] = np.ascontiguousarray(np.broadcast_to(f(inp['final_norm_g'])[None, :], (128, D)))
    sh.update(_consts())
    return sh


def _prep_core(inp, c):
    f = lambda a: np.ascontiguousarray(np.asarray(a, dtype=np.float32))
    b = c % 2
    m = {}
    m['X'] = np.concatenate([f(inp['x_prompt'][2 * c]), f(inp['x_prompt'][2 * c + 1]), f(inp['x_sample'][b])], 0)
    cond = np.stack([f(inp['c_ctx']), f(inp['c'][b])], 0)
    m['condT'] = np.ascontiguousarray(cond.reshape(2, 8, 128).transpose(2, 0, 1))
    m['cachek'] = f(inp['cache_k'][b]).reshape(DEPTH, 256, 128)
    m['cachev'] = f(inp['cache_v'][b]).reshape(DEPTH, 256, 128)
    sC = f(inp['state_C'][b])
    sn = f(inp['state_n'][b])
    smm = f(inp['state_m'][b])
    ca = np.concatenate([sC, sn[..., None]], -1)
    ca = ca.reshape(DEPTH, 2, 2, 2, 64, 65)
    m['c0a'] = np.ascontiguousarray(ca.transpose(0, 3, 4, 1, 2, 5).reshape(DEPTH, 128, 2, 2, 65))
    m['m0b'] = np.ascontiguousarray(np.broadcast_to(smm.reshape(DEPTH, 1, 8), (DEPTH, 128, 8)))
    m0r = np.zeros((DEPTH, 36, 1), np.float32)
    m0r[:, 0:4, 0] = smm[:, 0]
    m0r[:, 32:36, 0] = smm[:, 1]
    m['m0r'] = m0r
    r = c // 2
    m['qidx'] = (512 + r * 512 + np.arange(4)[None, :] * 128 + np.arange(128)[:, None]).astype(np.int32)
    return m


def kernel(**inputs):
    stage = int(inputs.pop('_stage', 99))
    nl = int(inputs.pop('_nl', DEPTH))
    cores = inputs.pop('_cores', None)
    if (stage, nl) not in _NC_CACHE:
        _NC_CACHE[(stage, nl)] = build(stage, nl)
    nc = _NC_CACHE[(stage, nl)]
    sh = _prep_shared(inputs)
    if stage < 2:
        sh['puv'] = sh['puv'][0:16]
    if cores is not None:
        in_maps = []
        for c in cores:
            m = dict(sh)
            m.update(_prep_core(inputs, c))
            in_maps.append(m)
        res = run_bass_kernel_spmd(nc, in_maps, core_ids=list(range(len(cores))))
        return res.results
    in_maps = []
    for c in range(8):
        m = dict(sh)
        m.update(_prep_core(inputs, c))
        in_maps.append(m)
    res = run_bass_kernel_spmd(nc, in_maps, core_ids=list(range(8)))
    R = res.results
    y_prompt = np.stack([R[c]['Y'][0:512].reshape(2, 256, D) for c in range(8)], 0).reshape(16, 256, D)
    y_sample = np.stack([np.concatenate([R[b + 2 * r]['Y'][512:1024] for r in range(4)], 0) for b in range(2)], 0)
    nk = np.concatenate([R[c]['NK'] for c in range(8)], 0).reshape(16, DEPTH, 256, 2, 64)
    nv = np.concatenate([R[c]['NV'] for c in range(8)], 0).reshape(16, DEPTH, 256, 2, 64)
    nC = np.concatenate([R[c]['NC'] for c in range(8)], 0)
    nn = np.concatenate([R[c]['NN'] for c in range(8)], 0)
    nm = np.concatenate([R[c]['NM'] for c in range(8)], 0)
    return (y_prompt.astype(np.float32), y_sample.astype(np.float32), nk.astype(np.float32), nv.astype(np.float32),
            nC.astype(np.float32), nn.astype(np.float32), nm.astype(np.float32))
```

```python
import numpy as np
import concourse.bass as bass
import concourse.mybir as mybir
from concourse.bass_utils import run_bass_kernel_spmd

F32 = mybir.dt.float32
BF16 = mybir.dt.bfloat16
I32 = mybir.dt.int32
U32 = mybir.dt.uint32
AF = mybir.ActivationFunctionType
ALU = mybir.AluOpType
AX = mybir.AxisListType

D = 1024
DEPTH = 2
NTOK = 2560
SEQS = [(0, 256, 0), (256, 256, 0), (512, 2048, 1)]
NW = 2576
EPS = 1e-6
NDS = 24
SUB = 99
import os
DBG = os.environ.get('KDBG', 'z')


class KB:
    def __init__(self, nc):
        self.nc = nc
        self.eng = dict(pe=nc.tensor, dve=nc.vector, act=nc.scalar, pool=nc.gpsimd, sp=nc.sync)
        self.sems = {e: nc.semaphore("sem_" + e).__enter__() for e in self.eng}
        self.cnt = {e: 0 for e in self.eng}
        self.sems['cc'] = nc.semaphore("sem_cc").__enter__()
        self.cnt['cc'] = 0
        self.dsems = []
        self.dcnt = []
        self.dname = {}
        self.known = {e: {} for e in self.eng}
        self.defer = None
        self.lastw = {}
        self.rd = {}
        self.tiles = {}

    def key(self, ap):
        return ap.tensor.name

    def _deps(self, e, reads, writes, skip=()):
        need = {}

        def add(dep):
            if dep is None:
                return
            kind, a, v = dep
            if kind == 'e':
                if a == 'pe' and e == 'pe':
                    return
                if a in skip:
                    return
                s = self.sems[a]
                kid = ('e', a)
            else:
                s = self.dsems[a]
                v = self.dcnt[a]
                kid = ('d', a)
            if need.get(kid, (None, 0))[1] < v:
                need[kid] = (s, v)

        for r in reads:
            add(self.lastw.get(r))
        for w in writes:
            add(self.lastw.get(w))
            for d in self.rd.get(w, ()):
                add(d)
        for kid, (s, v) in need.items():
            if self.known[e].get(kid, 0) >= v:
                continue
            self.eng[e].wait_ge(s, v)
            self.known[e][kid] = v

    def _record(self, dep, reads, writes):
        for r in reads:
            self.rd.setdefault(r, []).append(dep)
        for w in writes:
            self.lastw[w] = dep
            self.rd[w] = []

    def op(self, e, fn, reads=(), writes=()):
        if self.defer is not None:
            self.defer.append(('op', e, fn, reads, writes))
            return
        reads = [self.key(r) if not isinstance(r, str) else r for r in reads]
        writes = [self.key(w) if not isinstance(w, str) else w for w in writes]
        self._deps(e, reads, writes)
        ins = fn(self.eng[e])
        self.cnt[e] += 1
        ins.then_inc(self.sems[e], 1)
        self._record(('e', e, self.cnt[e]), reads, writes)

    def dma(self, q, out, in_, si=None, extra_reads=(), wkey=None, **kw):
        if self.defer is not None:
            self.defer.append(('dma', q, out, in_, kw))
            return
        reads = [self.key(in_)] + [self.key(r) for r in extra_reads]
        writes = [wkey if wkey is not None else self.key(out)]
        self._deps(q, reads, writes)
        si = self._dsem(writes[0])
        ins = self.eng[q].dma_start(out=out, in_=in_, **kw)
        self.dcnt[si] += 16
        ins.then_inc(self.dsems[si], 16)
        self._record(('d', si, self.dcnt[si]), reads, writes)

    def _dsem(self, name):
        parts = name.rsplit('_', 1)
        if len(parts) == 2 and parts[1].isdigit() and len(parts[1]) == 1:
            name = parts[0]
        if name not in self.dname:
            self.dname[name] = len(self.dsems)
            self.dsems.append(self.nc.semaphore("dsem%d" % len(self.dsems)).__enter__())
            self.dcnt.append(0)
        return self.dname[name]

    def emit(self, item):
        if item[0] == 'op':
            self.op(item[1], item[2], item[3], item[4])
        elif item[0] == 'dma':
            self.dma(item[1], item[2], item[3], **item[4])
        else:
            self.gather(item[1], item[2], item[3], 0)

    def max8(self, out, in_):
        self.op('dve', lambda e: e.max(out, in_), reads=[in_], writes=[out])

    def maxidx(self, out, mx, vals):
        self.op('dve', lambda e: e.max_index(out, mx, vals), reads=[mx, vals], writes=[out])

    def mrep(self, out, rep, vals, imm):
        self.op('dve', lambda e: e.match_replace(out, rep, vals, imm), reads=[rep, vals], writes=[out])

    def treduce(self, out, in_, axis, op):
        self.op('dve', lambda e: e.tensor_reduce(out, in_, axis, op), reads=[in_], writes=[out])

    def gather(self, out, table, idx, si, skip=()):
        if self.defer is not None:
            self.defer.append(('gather', out, table, idx))
            return
        reads = [self.key(table), self.key(idx)]
        writes = [self.key(out)]
        self._deps('pool', reads, writes, skip)
        si = self._dsem(writes[0])
        ins = self.nc.gpsimd.indirect_dma_start(
            out=out, out_offset=None, in_=table,
            in_offset=bass.IndirectOffsetOnAxis(ap=idx, axis=0))
        self.dcnt[si] += 16
        ins.then_inc(self.dsems[si], 16)
        self._record(('d', si, self.dcnt[si]), reads, writes)

    def allgather(self, out, in_, groups):
        reads = [self.key(in_)]
        writes = [self.key(out)]
        self._deps('pool', reads, writes)
        ins = self.nc.gpsimd.collective_compute("AllGather", mybir.AluOpType.bypass, replica_groups=groups,
                                                ins=[in_.opt()], outs=[out.opt()])
        self.cnt['cc'] += 1
        ins.then_inc(self.sems['cc'])
        self._record(('e', 'cc', self.cnt['cc']), reads, writes)

    def barrier_all(self):
        for e in self.eng:
            for o in self.sems:
                v = self.cnt[o]
                if v and self.known[e].get(('e', o), 0) < v:
                    self.eng[e].wait_ge(self.sems[o], v)
                    self.known[e][('e', o)] = v
            for i in range(len(self.dsems)):
                v = self.dcnt[i]
                if v and self.known[e].get(('d', i), 0) < v:
                    self.eng[e].wait_ge(self.dsems[i], v)
                    self.known[e][('d', i)] = v

    def sb(self, name, shape, dt=F32):
        t = self.nc.sbuf_tensor(name, list(shape), dt).__enter__()
        self.tiles[name] = t
        return t

    def mm(self, out, lhsT, rhs, start=True, stop=True, **kw):
        self.op('pe', lambda e: e.matmul(out, lhsT, rhs, start=start, stop=stop, **kw),
                reads=[lhsT, rhs], writes=[out])

    def tr(self, out, in_, ident):
        self.op('pe', lambda e: e.transpose(out, in_, ident), reads=[in_, ident], writes=[out])

    def act(self, out, in_, func, bias=None, scale=1.0, accum_out=None, eng='act'):
        reads = [in_]
        kw = {}
        if bias is not None:
            kw['bias'] = bias
            if not isinstance(bias, (int, float)):
                reads.append(bias)
        if not isinstance(scale, (int, float)):
            reads.append(scale)
        writes = [out]
        if accum_out is not None:
            kw['accum_out'] = accum_out
            writes.append(accum_out)
        self.op('act', lambda e: e.activation(out, in_, func, scale=scale, **kw), reads=reads, writes=writes)

    def tt(self, out, in0, in1, op, eng='dve'):
        self.op(eng, lambda e: e.tensor_tensor(out, in0, in1, op), reads=[in0, in1], writes=[out])

    def ts(self, out, in0, s1, s2, op0, op1=None, eng='dve', accum_out=None):
        reads = [in0] + [s for s in (s1, s2) if s is not None and not isinstance(s, (int, float))]
        writes = [out] + ([accum_out] if accum_out is not None else [])
        kw = {}
        if op1 is not None:
            kw['op1'] = op1
        if accum_out is not None:
            kw['accum_out'] = accum_out
        self.op(eng, lambda e: e.tensor_scalar(out, in0, s1, s2, op0, **kw), reads=reads, writes=writes)

    def stt(self, out, in0, scalar, in1, op0, op1):
        reads = [in0, in1] + ([scalar] if not isinstance(scalar, (int, float)) else [])
        self.op('dve', lambda e: e.scalar_tensor_tensor(out, in0, scalar, in1, op0, op1), reads=reads, writes=[out])

    def ttr(self, out, in0, in1, op0, op1, accum_out, scale=1.0, scalar=0.0):
        self.op('dve', lambda e: e.scalar_tensor_tensor(out, in0, 1.0, in1, ALU.mult, ALU.mult, accum_out=accum_out),
                reads=[in0, in1], writes=[out, accum_out])

    def cp(self, out, in_, eng='dve'):
        if eng == 'act':
            self.op('act', lambda e: e.copy(out, in_), reads=[in_], writes=[out])
        else:
            self.op(eng, lambda e: e.tensor_copy(out, in_), reads=[in_], writes=[out])

    def memset(self, ap, val, eng='dve'):
        self.op(eng, lambda e: e.memset(ap, val), writes=[ap])

    def recip(self, out, in_):
        self.op('dve', lambda e: e.reciprocal(out, in_), reads=[in_], writes=[out])

    def scan(self, out, d0, d1, init, op0, op1):
        reads = [d0, d1] + ([init] if not isinstance(init, (int, float)) else [])
        self.op('dve', lambda e: e.tensor_tensor_scan(out, d0, d1, init, op0, op1), reads=reads, writes=[out])


def build(stage=99, nl=DEPTH):
    nc = bass.Bass("TRN2", target_bir_lowering=False)
    k = KB(nc)

    def din(name, shape, dt=F32):
        return nc.dram_tensor(name, list(shape), dt, kind="ExternalInput").ap()

    def dout(name, shape, dt=F32):
        return nc.dram_tensor(name, list(shape), dt, kind="ExternalOutput").ap()

    X = din("X", [NTOK, D])
    condT = din("condT", [128, 2, 8])
    cachek = din("cachek", [DEPTH, 256, 128])
    cachev = din("cachev", [DEPTH, 256, 128])
    c0a_d = din("c0a", [DEPTH, 128, 2, 2, 65])
    m0b_d = din("m0b", [DEPTH, 128, 8])
    m0r_d = din("m0r", [DEPTH, 36, 1])
    w_mod = din("w_mod", [DEPTH, D, 6 * D])
    bmodF = din("bmodF", [DEPTH, 128, 4, 8])
    bmodG = din("bmodG", [DEPTH, 128, 2, D])
    n1g = din("n1g", [DEPTH, 128, 8])
    n2g = din("n2g", [DEPTH, 128, 8])
    w_in = din("w_in", [DEPTH, 128, 8 * NW])
    gateb = din("gateb", [DEPTH, 128, 16])
    sinkb = din("sinkb", [DEPTH, 128, 8])
    mng = din("mng", [DEPTH, 128, 256])
    poolw = din("poolw", [DEPTH, 64, 4, 64])
    pscale = din("pscale", [DEPTH, 64, 4])
    w_out = din("w_out", [DEPTH, D, D])
    wq = din("wq", [DEPTH, D, 2048])
    keysT = din("keysT", [DEPTH, 128, 16, 128])
    ntab = DEPTH * 16384 if stage >= 2 else 16
    puv = din("puv", [ntab, 2 * D])
    fng = din("fng", [128, D])
    ident_d = din("ident", [128, 128])
    sel_d = din("sel", [36, 8, 128])
    ropeC_d = din("ropeC", [128, 2048])
    ropeS_d = din("ropeS", [128, 2048])
    prot_d = din("prot", [128, 128])
    tri01_d = din("tri01", [128, 2, 128])
    amask_d = din("amask", [128, 384])
    band_d = din("band", [128, 4, 5, 128])
    iota_d = din("iota16", [128, 16])

    full = not (stage < 2 or nl < DEPTH)
    Y = dout("Y", [1024 if full else NTOK, D])
    qidx_d = din("qidx", [128, 4], I32)
    NK = dout("NK", [2, DEPTH, 256, 128])
    NV = dout("NV", [2, DEPTH, 256, 128])
    NC_ = dout("NC", [2, DEPTH, 2, 4, 64, 64])
    NN = dout("NN", [2, DEPTH, 2, 4, 64])
    NM = dout("NM", [2, DEPTH, 2, 4])

    Xd = nc.dram_tensor("Xd", [NTOK, D], F32).ap()
    XB = [nc.dram_tensor("XB%d" % i, [256, D], F32).ap() for i in range(2)]
    XG = [nc.dram_tensor("XG%d" % i, [1024, D], F32).ap() for i in range(2)]

    PS = [nc.psum_tensor("ps%d" % i, [128, 512], F32).__enter__() for i in range(8)]

    ident = k.sb("identf", [128, 128])
    identb = k.sb("identb", [128, 128], BF16)
    sel = k.sb("selc", [36, 8, 128])
    prot = k.sb("protc", [128, 128])
    tri01 = k.sb("tri01c", [128, 2, 128], BF16)
    amask = k.sb("amaskc", [128, 384], BF16)
    band = k.sb("bandc", [128, 4, 5, 128], BF16)
    iota16 = k.sb("iota16c", [128, 16])
    onec = k.sb("onec", [128, 1])
    epsc = k.sb("epsc", [128, 1])
    k.dma('sp', ident[:], ident_d)
    k.dma('pool', identb[:], ident_d)
    k.dma('sp', sel[:], sel_d)
    k.dma('sp', prot[:], prot_d)
    k.dma('pool', tri01[:], tri01_d)
    k.dma('pool', amask[:], amask_d)
    k.dma('pool', band[:], band_d)
    k.dma('sp', iota16[:], iota_d)
    k.memset(onec[:], 1.0)
    k.memset(epsc[:], EPS)

    xt = k.sb("xt", [128, D])
    xn = k.sb("xn", [128, D])
    st = k.sb("stat", [128, 8])

    for i in range(NTOK // 128):
        k.dma('sp', xt[:], X[i * 128:(i + 1) * 128, :])
        k.dma('sp', Xd[i * 128:(i + 1) * 128, :], xt[:])

    puv16 = nc.dram_tensor("puv16", [ntab, 2 * D], BF16).ap()
    w16 = nc.dram_tensor("w16", [DEPTH, 128, 8 * NW], BF16).ap()
    cT = k.sb("cT", [128, 2, 8])
    scT = k.sb("scT", [128, 2, 8])
    k.dma('pool', cT[:], condT)
    k.act(scT[:], cT[:], AF.Silu)
    modFs = [k.sb("modF%d" % l, [128, 2, 4, 8]) for l in range(DEPTH)]
    A1s = [k.sb("A1_%d" % l, [128, 2, 8]) for l in range(DEPTH)]
    A2s = [k.sb("A2_%d" % l, [128, 2, 8]) for l in range(DEPTH)]
    gB = k.sb("gB", [128, 2, 2, D], BF16)
    gBd = nc.dram_tensor("gBd", [DEPTH, 128, 2 * 2 * D], BF16).ap()
    n1t = k.sb("n1t", [128, 8])
    n2t = k.sb("n2t", [128, 8])
    bmF = k.sb("bmF", [128, 4, 8])

    qidx = k.sb("qidx_sb", [128, 4], I32)
    k.dma('sp', qidx[:], qidx_d)

    def rmsnorm_tile(tok0, gidx=None):
        if gidx is None:
            k.dma('sp', xt[:], Xd[tok0:tok0 + 128, :])
        else:
            k.gather(xt[:], Xd, qidx[:, gidx:gidx + 1], 0)
        k.ttr(xn[:], xt[:], xt[:], ALU.mult, ALU.add, st[:, 0:1])
        k.act(st[:, 1:2], st[:, 0:1], AF.Ln, bias=epsc[:, 0:1], scale=1.0 / D)
        k.act(st[:, 2:3], st[:, 1:2], AF.Exp, scale=-0.5)
        k.ts(xn[:], xt[:], st[:, 2:3], None, ALU.mult)

    with nc.sbuf_tensor("pcf0", [128, 4 * D], F32) as pcf0, nc.sbuf_tensor("pcf1", [128, 4 * D], F32) as pcf1, \
            nc.sbuf_tensor("pcf2", [128, 4 * D], F32) as pcf2, nc.sbuf_tensor("pcf3", [128, 4 * D], F32) as pcf3, \
            nc.sbuf_tensor("pcb0", [128, 4 * D], BF16) as pcb0, nc.sbuf_tensor("pcb1", [128, 4 * D], BF16) as pcb1, \
            nc.sbuf_tensor("pcb2", [128, 4 * D], BF16) as pcb2, nc.sbuf_tensor("pcb3", [128, 4 * D], BF16) as pcb3:
        pcf = [pcf0, pcf1, pcf2, pcf3]
        pcb = [pcb0, pcb1, pcb2, pcb3]
        wi = 0
        for l in range(DEPTH):
            for o in range(0, 8 * NW, 4096):
                n = min(4096, 8 * NW - o)
                k.dma('sp', pcf[wi % 4][:, 0:n], w_in[l, :, o:o + n])
                k.cp(pcb[wi % 4][:, 0:n], pcf[wi % 4][:, 0:n], eng='act')
                k.dma('act', w16[l, :, o:o + n], pcb[wi % 4][:, 0:n], wkey="w16w%d" % (wi % 4))
                wi += 1
        if stage >= 2:
            for ch in range(ntab // 256):
                src = puv[ch * 256:(ch + 1) * 256, :].rearrange("(p r) n -> p (r n)", r=2)
                dst = puv16[ch * 256:(ch + 1) * 256, :].rearrange("(p r) n -> p (r n)", r=2)
                k.dma('sp', pcf[(ch + wi) % 4][:], src)
                k.cp(pcb[(ch + wi) % 4][:], pcf[(ch + wi) % 4][:], eng='act')
                k.dma('act', dst, pcb[(ch + wi) % 4][:], wkey="puv16w%d" % (ch % 4))
        with nc.sbuf_tensor("wmB0", [128, 8, 512], F32) as wmB0, nc.sbuf_tensor("wmB1", [128, 8, 512], F32) as wmB1, \
                nc.sbuf_tensor("screp", [128, 2, 8, 128], F32) as screp, \
                nc.sbuf_tensor("bmG", [128, 2, D], F32) as bmG:
            wmBs = [wmB0, wmB1]
            for g in range(2):
                for kc in range(8):
                    k.cp(screp[:, g, kc, :], scT[:, g, kc:kc + 1].to_broadcast([128, 128]))
            for l in range(nl):
                modF = modFs[l]; A1 = A1s[l]; A2 = A2s[l]
                k.dma('pool', n1t[:], n1g[l])
                k.dma('pool', n2t[:], n2g[l])
                k.dma('pool', bmF[:], bmodF[l])
                k.dma('pool', bmG[:], bmodG[l])
                parts = [0, 1, 3, 4]
                cnt = 0
                for pi, p in enumerate(parts):
                    for hf in range(2):
                        c0 = p * D + hf * 512
                        wm = wmBs[cnt % 2]
                        cnt += 1
                        k.dma('pool', wm[:], w_mod[l, :, c0:c0 + 512].rearrange("(kc p) n -> p kc n", p=128))
                        for q in range(4):
                            dc = hf * 4 + q
                            for kc in range(8):
                                k.mm(PS[q % 2][:, 0:2], wm[:, kc, q * 128:(q + 1) * 128], scT[:, :, kc], start=(kc == 0), stop=(kc == 7))
                            k.cp(modF[:, :, pi, dc], PS[q % 2][:, 0:2])
                for g in range(2):
                    k.tt(modF[:, g, :, :], modF[:, g, :, :], bmF[:], ALU.add)
                    k.ts(A1[:, g, :], modF[:, g, 1, :], 1.0, None, ALU.add)
                    k.tt(A1[:, g, :], A1[:, g, :], n1t[:], ALU.mult)
                    k.ts(A2[:, g, :], modF[:, g, 3, :], 1.0, None, ALU.add)
                    k.tt(A2[:, g, :], A2[:, g, :], n2t[:], ALU.mult)
                for gi, p in enumerate([2, 5]):
                    for hf in range(2):
                        c0 = p * D + hf * 512
                        wm = wmBs[cnt % 2]
                        cnt += 1
                        k.dma('pool', wm[:], w_mod[l, :, c0:c0 + 512].rearrange("(kc p) n -> p kc n", p=128))
                        for g in range(2):
                            pp = PS[2 + 2 * (cnt % 2) + g]
                            for kc in range(8):
                                k.mm(pp[:], screp[:, g, kc, :], wm[:, kc, :], start=(kc == 0), stop=(kc == 7))
                            k.tt(gB[:, g, gi, hf * 512:(hf + 1) * 512], pp[:], bmG[:, gi, hf * 512:(hf + 1) * 512], ALU.add)
                k.dma('pool', gBd[l], gB[:].rearrange("p a b d -> p (a b d)"))
        k.barrier_all()

    pending_exchange = False
    for l in range(nl):
        modF = modFs[l]; A1 = A1s[l]; A2 = A2s[l]
        k.dma('sp', gB[:].rearrange("p a b d -> p (a b d)"), gBd[l])
        if stage >= 1:
            phase_a(nc, k, l, locals())
        k.barrier_all()
        if stage >= 2:
            phase_b(nc, k, l, locals())
        k.barrier_all()
        if full and l < DEPTH - 1:
            for hf in range(2):
                k.allgather(XG[hf], XB[hf], [[0, 2, 4, 6], [1, 3, 5, 7]])
            pending_exchange = True

    if stage < 2 or nl < DEPTH:
        for i in range(NTOK // 128):
            k.dma('sp', xt[:], Xd[i * 128:(i + 1) * 128, :])
            k.dma('sp', Y[i * 128:(i + 1) * 128, :], xt[:])
    k.barrier_all()
    return nc


def phase_a(nc, k, l, E):
    PS = E['PS']; Xd = E['Xd']; xt = E['xt']; xn = E['xn']; st = E['st']
    ident = E['ident']; identb = E['identb']; sel = E['sel']; prot = E['prot']
    tri01 = E['tri01']; amask = E['amask']; band = E['band']
    A1 = E['A1']; modF = E['modF']; gB = E['gB']; onec = E['onec']; epsc = E['epsc']
    rmsnorm_tile = E['rmsnorm_tile']
    w16 = E['w16']; w_out = E['w_out']
    from contextlib import ExitStack
    es = ExitStack()

    def sb(name, shape, dt=F32):
        return es.enter_context(nc.sbuf_tensor("%s_%d" % (name, l), list(shape), dt))

    with es:
        hT = sb("hT", [128, 8, 512], BF16)
        wfm = [sb("wfm%d" % i, [128, 8, 128], BF16) for i in range(2)]
        wtm = [sb("wtm0", [128, 8, 512], BF16)]
        WO = sb("WO", [128, 6, D], BF16)
        WOP = sb("WOP", [64, 4, D], BF16)
        ropeC = sb("ropeC", [128, 512])
        ropeS = sb("ropeS", [128, 512])
        QAT = sb("QAT", [128, 4, 2048], BF16)
        KAT = sb("KAT", [128, 2, 2048], BF16)
        VA = sb("VA", [128, 16, 2, 66], BF16)
        KCT = sb("KCT", [128, 2, 256], BF16)
        VC = sb("VC", [128, 2, 2, 66], BF16)
        QMT = sb("QMT", [128, 2, 2048], BF16)
        KMT = sb("KMT", [128, 2, 2048], BF16)
        VM = sb("VM", [128, 16, 4, 66], BF16)
        OM = sb("OM", [128, 16, 256], BF16)
        XP = sb("XP", [128, 16, 256], BF16)
        GPI = sb("GPI", [128, 16, 36])
        GPF = sb("GPF", [128, 16, 36])
        A_tok = sb("A_tok", [128, 16, 36])
        B_tok = sb("B_tok", [128, 16, 36])
        E_tok = sb("E_tok", [128, 16, 36])
        RA = sb("RA", [36, 2048])
        RB = sb("RB", [36, 2048])
        M0r = sb("M0r", [36, 1])
        M0b = sb("M0b", [128, 8])
        BT = sb("BT", [36, 1])
        MF = sb("MF", [36, 1])
        C0A = sb("C0A", [128, 2, 2, 66], BF16)
        gbt = sb("gbt", [128, 16])
        sinkE = sb("sinkE", [128, 8])
        mngt = sb("mngt", [128, 256])
        PW = sb("PW", [64, 4, 64], BF16)
        psc = sb("psc", [64, 4])
        qf = sb("qf", [128, 512])
        Dbuf = [sb("Dbuf%d" % i, [128, 512]) for i in range(2)]
        t1, t2 = Dbuf
        Wbuf = [sb("Wbuf%d" % i, [128, 512], BF16) for i in range(2)]
        OTb = [sb("OTb%d" % i, [65, 512]) for i in range(2)]
        HM = sb("HM", [128, 4, 256])
        ATTT = sb("ATTT", [128, 4, 512], BF16)
        MLST = sb("MLST", [128, 2, 512], BF16)
        PLT = sb("PLT", [64, 4, 512], BF16)
        DT = sb("DTb", [64, 128], BF16)
        sm = sb("sm", [128, 16])
        kcf = sb("kcf", [128, 2, 128])
        ktok = sb("ktok", [128, 128])
        vtok = sb("vtok", [128, 128])
        KMtok = sb("KMtok", [128, 2, 256])
        MLB = sb("MLB", [128, 8])
        Ftok = sb("Ftok", [128, 2, 8])
        kw_ = sb("kw", [128, 64], BF16)
        cst = sb("cst", [64, 65])

        k.dma('sp', gbt[:], E['gateb'][l])
        k.dma('sp', sinkE[:], E['sinkb'][l])
        k.act(sinkE[:], sinkE[:], AF.Exp)
        k.dma('sp', mngt[:], E['mng'][l])
        k.dma('pool', PW[:], E['poolw'][l])
        k.dma('sp', psc[:], E['pscale'][l])
        k.dma('pool', WO[:], w_out[l, 0:768, :].rearrange("(kc p) n -> p kc n", p=128))
        k.dma('pool', WOP[:], w_out[l, 768:1024, :].rearrange("(g p) n -> p g n", p=64))
        k.dma('pool', C0A[:, :, :, 0:65], E['c0a_d'][l])
        k.memset(VA[:, :, :, 64:65], 1.0)
        k.memset(VC[:, :, :, 64:65], 1.0)
        k.memset(VM[:, :, :, 64:65], 1.0)
        k.memset(GPI[:], 0.0)
        k.memset(GPF[:], 0.0)
        for j in range(2):
            k.dma('sp', kcf[:, j, :], E['cachek'][l, j * 128:(j + 1) * 128, :])
        for j in range(2):
            k.tr(PS[0][:, j * 128:(j + 1) * 128], kcf[:, j, :], ident[:])
        k.cp(KCT[:, 0, :], PS[0][:, 0:256])
        for j in range(2):
            k.tr(PS[1][0:64, j * 128:(j + 1) * 128], kcf[:, j, 64:128], ident[:])
        k.cp(KCT[0:64, 1, :], PS[1][0:64, 0:256])
        for j in range(2):
            k.tr(PS[2][0:64, j * 128:(j + 1) * 128], kcf[:, j, 0:64], ident[:])
        k.cp(t1[0:64, 0:256], PS[2][0:64, 0:256])
        k.cp(Wbuf[0][0:64, 0:256], t1[0:64, 0:256])
        k.dma('sp', KCT[64:128, 1, :], Wbuf[0][0:64, 0:256])
        for j in range(2):
            k.dma('sp', vtok[:], E['cachev'][l, j * 128:(j + 1) * 128, :])
            k.cp(VC[:, j, :, 0:64], vtok[:].rearrange("p (a b) -> p a b", a=2))

        for si, (t0, L, g) in enumerate(SEQS):
            nt = L // 128
            rope = (g == 1)
            if SUB < 1:
                continue
            if rope and E['pending_exchange']:
                XG = E['XG']
                for r_ in range(4):
                    for j_ in range(4):
                        src = XG[j_ // 2][r_ * 256 + (j_ % 2) * 128:r_ * 256 + (j_ % 2) * 128 + 128, :]
                        k.dma('sp', Xd[512 + r_ * 512 + j_ * 128:512 + r_ * 512 + (j_ + 1) * 128, :], src)
            GS = min(L, 512)
            ntg = GS // 128
            ngrp = L // GS
            for gi in range(ngrp):
                for ti in range(ntg):
                    tile = gi * ntg + ti
                    rmsnorm_tile(t0 + tile * 128)
                    for half in range(2):
                        for q in range(4):
                            dc = half * 4 + q
                            k.tr(PS[half][:, q * 128:(q + 1) * 128], xn[:, dc * 128:(dc + 1) * 128], ident[:])
                        for q in range(4):
                            dc = half * 4 + q
                            k.act(hT[:, dc, ti * 128:(ti + 1) * 128], PS[half][:, q * 128:(q + 1) * 128], AF.Identity,
                                  bias=modF[:, g, 0, dc:dc + 1], scale=A1[:, g, dc:dc + 1])
                c0 = gi * GS
                if DBG < 'b':
                    continue
                if rope:
                    k.dma('sp', ropeC[:, 0:GS], E['ropeC_d'][:, c0:c0 + GS])
                    k.dma('sp', ropeS[:, 0:GS], E['ropeS_d'][:, c0:c0 + GS])
                for ob in range(10):
                    wb = wfm[ob % 2]
                    k.dma('sp', wb[:], w16[l, :, 8 * ob * 128:8 * (ob + 1) * 128].rearrange("p (kc n) -> p kc n", kc=8))
                    pp = PS[2 + (ob % 2)]
                    for kc in range(8):
                        k.mm(pp[:, 0:GS], wb[:, kc, :], hT[:, kc, 0:GS], start=(kc == 0), stop=(kc == 7))
                    if ob < 4:
                        dst = QAT[:, ob, c0:c0 + GS]
                    elif ob < 6:
                        dst = KAT[:, ob - 4, c0:c0 + GS]
                    elif ob < 8:
                        dst = QMT[:, ob - 6, c0:c0 + GS]
                    else:
                        dst = KMT[:, ob - 8, c0:c0 + GS]
                    if rope and ob < 6:
                        k.cp(qf[:, 0:GS], pp[:, 0:GS], eng='act')
                        k.mm(PS[4][:, 0:GS], prot[:], qf[:, 0:GS])
                        k.tt(t1[:, 0:GS], PS[4][:, 0:GS], ropeS[:, 0:GS], ALU.mult)
                        k.tt(t2[:, 0:GS], qf[:, 0:GS], ropeC[:, 0:GS], ALU.mult, eng='pool')
                        k.tt(dst, t1[:, 0:GS], t2[:, 0:GS], ALU.add)
                    else:
                        k.cp(dst, pp[:, 0:GS], eng='act')
                if DBG < 'c':
                    continue
                tmb = [(1280, 400), (1680, 512)] + ([(2192, 384)] if not rope else [])
                for bi, (cb, ncol) in enumerate(tmb):
                    wb = wtm[0]
                    k.dma('sp', wb[:, :, 0:ncol], w16[l, :, 8 * cb:8 * (cb + ncol)].rearrange("p (kc n) -> p kc n", kc=8))
                    for ti in range(ntg):
                        tile = gi * ntg + ti
                        pp = PS[5 + (ti % 2)]
                        for kc in range(8):
                            k.mm(pp[:, 0:ncol], hT[:, kc, ti * 128:(ti + 1) * 128], wb[:, kc, 0:ncol], start=(kc == 0), stop=(kc == 7))
                        if DBG < 'd':
                            continue
                        if bi == 0:
                            D2 = os.environ.get('KD2', '1234')
                            if '1' in D2:
                                k.cp(VA[:, tile, :, 0:64], pp[:, 0:128].rearrange("p (a b) -> p a b", a=2))
                            if not rope and '2' in D2:
                                k.cp(vtok[:], pp[:, 0:128])
                                k.dma('sp', E['NV'][si, l, tile * 128:(tile + 1) * 128, :], vtok[:])
                            if '3' in D2:
                                k.cp(VM[:, tile, :, 0:64], pp[:, 128:384].rearrange("p (a b) -> p a b", a=4))
                            if '4' in D2:
                                k.tt(GPI[:, tile, 0:4], pp[:, 384:388], gbt[:, 0:4], ALU.add)
                                k.tt(GPF[:, tile, 0:4], pp[:, 388:392], gbt[:, 4:8], ALU.add)
                                k.tt(GPI[:, tile, 32:36], pp[:, 392:396], gbt[:, 8:12], ALU.add)
                                k.tt(GPF[:, tile, 32:36], pp[:, 396:400], gbt[:, 12:16], ALU.add)
                        elif bi == 1 and DBG >= 'e':
                            k.act(OM[:, tile, :], pp[:, 0:256], AF.Sigmoid)
                            k.cp(XP[:, tile, :], pp[:, 256:512], eng='act')
                        elif bi == 2 and DBG >= 'f':
                            k.cp(ktok[:], pp[:, 0:128])
                            k.dma('sp', E['NK'][si, l, tile * 128:(tile + 1) * 128, :], ktok[:])
                            k.cp(KMtok[:, tile, :], pp[:, 128:384])

            if SUB < 2:
                continue
            if rope:
                k.dma('sp', M0r[:], E['m0r_d'][l])
                k.dma('sp', M0b[:], E['m0b_d'][l])
            else:
                k.memset(M0r[:], 0.0)
                k.memset(M0b[:], 0.0)
            for tile in range(nt):
                k.tr(PS[0][0:36, 0:128], GPI[:, tile, :], ident[:])
                k.cp(RA[:, tile * 128:(tile + 1) * 128], PS[0][0:36, 0:128])
                k.tr(PS[1][0:36, 0:128], GPF[:, tile, :], ident[:])
                k.cp(RB[:, tile * 128:(tile + 1) * 128], PS[1][0:36, 0:128], eng='act')
            k.act(RB[:, 0:L], RB[:, 0:L], AF.Exp, scale=-1.0)
            k.act(RB[:, 0:L], RB[:, 0:L], AF.Ln, bias=onec[0:36, 0:1])
            k.scan(RB[0:4, 0:L], RB[0:4, 0:L], RB[0:4, 0:L], 0.0, ALU.add, ALU.max)
            k.scan(RB[32:36, 0:L][:, ::-1], RB[32:36, 0:L][:, ::-1], RB[32:36, 0:L][:, ::-1], 0.0, ALU.add, ALU.max)
            k.tt(RA[:, 0:L], RA[:, 0:L], RB[:, 0:L], ALU.add)
            k.cp(BT[0:4, :], RB[0:4, L - 1:L])
            k.cp(BT[32:36, :], RB[32:36, 0:1])
            for tile in range(nt):
                k.tr(PS[0][:, 0:36], RA[:, tile * 128:(tile + 1) * 128], ident[0:36, 0:36])
                k.cp(A_tok[:, tile, :], PS[0][:, 0:36])
                k.tr(PS[1][:, 0:36], RB[:, tile * 128:(tile + 1) * 128], ident[0:36, 0:36])
                k.cp(B_tok[:, tile, :], PS[1][:, 0:36], eng='act')
            k.scan(RB[0:4, 0:L], RA[0:4, 0:L], RA[0:4, 0:L], M0r[0:4, 0:1], ALU.max, ALU.max)
            k.scan(RB[32:36, 0:L][:, ::-1], RA[32:36, 0:L][:, ::-1], RA[32:36, 0:L][:, ::-1], M0r[32:36, 0:1], ALU.max, ALU.max)
            for tile in range(nt):
                k.tr(PS[0][:, 0:36], RB[:, tile * 128:(tile + 1) * 128], ident[0:36, 0:36])
                k.cp(E_tok[:, tile, :], PS[0][:, 0:36])
            k.tt(E_tok[:, 0:nt, :], B_tok[:, 0:nt, :], E_tok[:, 0:nt, :], ALU.subtract)
            k.act(E_tok[:, 0:nt, :], E_tok[:, 0:nt, :], AF.Exp)

            if not rope and SUB >= 3:
                k.tt(MF[0:4, :], RB[0:4, L - 1:L], BT[0:4, :], ALU.subtract)
                k.tt(MF[32:36, :], RB[32:36, 0:1], BT[32:36, :], ALU.subtract)
                for d in range(2):
                    k.dma('sp', E['NM'][si, l, d, :].rearrange("(h o) -> h o", o=1), MF[d * 32:d * 32 + 4, :])
                for d in range(2):
                    col = (L - 1) if d == 0 else 0
                    for h in range(4):
                        k.mm(PS[2][:, d * 4 + h:d * 4 + h + 1], sel[:, d * 4 + h, :], RB[:, col:col + 1])
                k.cp(MLB[:], PS[2][:, 0:8])
                for j in range(nt):
                    for d in range(2):
                        k.tt(Ftok[:, j, d * 4:d * 4 + 4], A_tok[:, j, d * 32:d * 32 + 4], MLB[:, d * 4:d * 4 + 4], ALU.subtract)
                k.act(Ftok[:], Ftok[:], AF.Exp)
                for d in range(2):
                    for h in range(4):
                        for j in range(nt):
                            k.ts(kw_[:], KMtok[:, j, h * 64:(h + 1) * 64], Ftok[:, j, d * 4 + h:d * 4 + h + 1], 0.125, ALU.mult, ALU.mult)
                            k.mm(PS[3][0:64, 0:65], kw_[:], VM[:, j, h, 0:65], start=(j == 0), stop=(j == nt - 1))
                        k.cp(cst[:], PS[3][0:64, 0:65])
                        k.dma('sp', E['NC_'][si, l, d, h], cst[:, 0:64])
                        k.dma('sp', E['NN'][si, l, d, h, :].rearrange("(p o) -> p o", o=1), cst[:, 64:65])

            if SUB < 4:
                continue
            for ci in range(ngrp):
                c0t = ci * ntg
                c0 = c0t * 128
                def capture(fn):
                    k.defer = []
                    fn()
                    items = k.defer
                    k.defer = None
                    return items

                def emit_list(items):
                    for it in items:
                        k.emit(it)

                def att_blocks(h):
                    kvh = h // 4
                    pair = h // 2
                    base = (h % 2) * 64
                    var = 0 if kvh * 64 == base else 1
                    po = PS[4 + (h % 2)]
                    keyl = []
                    if rope:
                        keyl += [('c', 0), ('c', 1)]
                        for j in range(max(0, c0t - 1), min(nt, c0t + ntg + 1)):
                            keyl.append(('b', j))
                    else:
                        keyl += [('f', j) for j in range(nt)]

                    def a_stage1(n_):
                        kind, j = keyl[n_]
                        ps = PS[n_ % 2]
                        wbf = Wbuf[n_ % 2]
                        if kind == 'c':
                            lo, hi = 0, ntg
                            k.mm(ps[:, 0:GS], KCT[base:base + 64, var, j * 128:(j + 1) * 128], QAT[base:base + 64, pair, c0:c0 + GS])
                        elif kind == 'f':
                            lo, hi = 0, ntg
                            k.mm(ps[:, 0:GS], KAT[base:base + 64, var, j * 128:(j + 1) * 128], QAT[base:base + 64, pair, c0:c0 + GS])
                        else:
                            ilo = max(c0t, j - 1)
                            ihi = min(c0t + ntg - 1, j + 1)
                            lo, hi = ilo - c0t, ihi - c0t + 1
                            ncol = (hi - lo) * 128
                            k.mm(ps[:, lo * 128:hi * 128], KAT[base:base + 64, var, j * 128:(j + 1) * 128],
                                 QAT[base:base + 64, pair, c0 + lo * 128:c0 + hi * 128], start=True, stop=False)
                            mo = (ilo - (j - 1)) * 128
                            k.mm(ps[:, lo * 128:hi * 128], identb[:], amask[:, mo:mo + ncol], start=False, stop=True)
                        k.act(wbf[:, lo * 128:hi * 128], ps[:, lo * 128:hi * 128], AF.Exp, scale=0.125)
                        return lo, hi

                    def a_stage2(n_, lo, hi):
                        kind, j = keyl[n_]
                        wbf = Wbuf[n_ % 2]
                        vsrc = VC[:, j, kvh, 0:65] if kind == 'c' else VA[:, j, kvh, 0:65]
                        k.mm(po[0:65, lo * 128:hi * 128], vsrc, wbf[:, lo * 128:hi * 128], start=(n_ == 0),
                             stop=(n_ == len(keyl) - 1), skip_group_check=True)
                    rng = a_stage1(0)
                    for n_ in range(len(keyl)):
                        nrng = a_stage1(n_ + 1) if n_ + 1 < len(keyl) else None
                        a_stage2(n_, *rng)
                        rng = nrng

                def att_epi(h):
                    pair = h // 2
                    hb = (h % 2) * 64
                    po = PS[4 + (h % 2)]
                    ot = OTb[h % 2]
                    k.cp(ot[:, 0:GS], po[0:65, 0:GS])
                    pt4 = PS[6 + (h % 2)]
                    for ti in range(ntg):
                        k.tr(pt4[:, ti * 66:ti * 66 + 65], ot[:, ti * 128:(ti + 1) * 128], ident[0:65, 0:65])
                    pv = pt4[:, 0:ntg * 66].rearrange("p (t c) -> p t c", c=66)
                    k.ts(sm[:, 0:ntg], pv[:, :, 64], sinkE[:, h:h + 1], None, ALU.add)
                    k.recip(sm[:, 4:4 + ntg], sm[:, 0:ntg])
                    k.tt(HM[:, 0:ntg, hb:hb + 64], pv[:, :, 0:64], sm[:, 4:4 + ntg].unsqueeze(2).broadcast_to([128, ntg, 64]), ALU.mult)
                    if h % 2 == 1:
                        for ti in range(ntg):
                            k.tr(PS[2][:, ti * 128:(ti + 1) * 128], HM[:, ti, 0:128], ident[:])
                        k.cp(ATTT[:, pair, 0:GS], PS[2][:, 0:GS], eng='act')

                if SUB >= 5:
                    blk = [capture(lambda h=h: att_blocks(h)) for h in range(8)]
                    epi = [capture(lambda h=h: att_epi(h)) for h in range(8)]
                    emit_list(blk[0])
                    for h in range(8):
                        if h + 1 < 8:
                            emit_list(blk[h + 1])
                        emit_list(epi[h])

                def ml_pro(h, d):
                    pair = h // 2
                    base = (h % 2) * 64
                    po = PS[4 + d]
                    if rope:
                        k.mm(PS[3][:, 0:GS], sel[:, d * 4 + h, :], RB[:, c0:c0 + GS])
                        k.act(Dbuf[0][base:base + 64, 0:GS], PS[3][base:base + 64, 0:GS], AF.Exp, scale=-1.0,
                              bias=M0b[base:base + 64, d * 4 + h:d * 4 + h + 1])
                        k.tt(Wbuf[0][base:base + 64, 0:GS], QMT[base:base + 64, pair, c0:c0 + GS], Dbuf[0][base:base + 64, 0:GS], ALU.mult)
                        k.mm(po[0:65, 0:GS], C0A[base:base + 64, d, pair, 0:65], Wbuf[0][base:base + 64, 0:GS], start=True, stop=False)
                    k.mm(PS[3][:, 0:GS], sel[:, d * 4 + h, :], RB[:, c0:c0 + GS])
                    k.cp(qf[:, 0:GS], PS[3][:, 0:GS], eng='act')

                def ml_blocks(h, d):
                    pair = h // 2
                    base = (h % 2) * 64
                    po = PS[4 + d]
                    js = list(range(0, c0t + ntg)) if d == 0 else list(range(nt - 1, c0t - 1, -1))

                    def m_rng(n_):
                        r = js[n_] - c0t
                        if 0 <= r < ntg:
                            return ((r, ntg) if d == 0 else (0, r + 1)), r
                        return (0, ntg), None

                    def m_stage1(n_):
                        j = js[n_]
                        (lo, hi), r = m_rng(n_)
                        ps = PS[n_ % 3]
                        db = Dbuf[n_ % 2]
                        k.mm(ps[:, lo * 128:hi * 128], KMT[base:base + 64, pair, j * 128:(j + 1) * 128],
                             QMT[base:base + 64, pair, c0 + lo * 128:c0 + hi * 128])
                        k.act(db[:, lo * 128:hi * 128], qf[:, lo * 128:hi * 128], AF.Exp, scale=-1.0, bias=A_tok[:, j, d * 32 + h:d * 32 + h + 1])

                    def m_stage2(n_, first_):
                        j = js[n_]
                        (lo, hi), r = m_rng(n_)
                        ps = PS[n_ % 3]
                        db = Dbuf[n_ % 2]
                        wbf = Wbuf[n_ % 2]
                        k.stt(wbf[:, lo * 128:hi * 128], ps[:, lo * 128:hi * 128], 0.125, db[:, lo * 128:hi * 128], ALU.mult, ALU.mult)
                        if r is not None:
                            k.tt(wbf[:, r * 128:(r + 1) * 128], wbf[:, r * 128:(r + 1) * 128], tri01[:, d, :], ALU.mult, eng='pool')
                        k.mm(po[0:65, lo * 128:hi * 128], VM[:, j, h, 0:65], wbf[:, lo * 128:hi * 128], start=first_, stop=(n_ == len(js) - 1),
                             skip_group_check=True)
                    first = not rope
                    m_stage1(0)
                    for n_ in range(len(js)):
                        if n_ + 1 < len(js):
                            m_stage1(n_ + 1)
                        m_stage2(n_, first)
                        first = False

                def ml_epi(h, d):
                    po = PS[4 + d]
                    ot = OTb[d]
                    k.cp(ot[:, 0:GS], po[0:65, 0:GS], eng='act')
                    pt4 = PS[6 + d]
                    for ti in range(ntg):
                        k.tr(pt4[:, ti * 66:ti * 66 + 65], ot[:, ti * 128:(ti + 1) * 128], ident[0:65, 0:65])
                    pv = pt4[:, 0:ntg * 66].rearrange("p (t c) -> p t c", c=66)
                    k.ts(sm[:, 0:ntg], pv[:, :, 64], -1.0, None, ALU.mult)
                    k.tt(sm[:, 0:ntg], sm[:, 0:ntg], pv[:, :, 64], ALU.max)
                    k.tt(sm[:, 0:ntg], sm[:, 0:ntg], E_tok[:, c0t:c0t + ntg, d * 32 + h], ALU.max)
                    k.recip(sm[:, 4:4 + ntg], sm[:, 0:ntg])
                    if d == 0:
                        k.tt(HM[:, 0:ntg, h * 64:(h + 1) * 64], pv[:, :, 0:64],
                             sm[:, 4:4 + ntg].unsqueeze(2).broadcast_to([128, ntg, 64]), ALU.mult)
                    else:
                        for ti in range(ntg):
                            k.stt(HM[:, ti, h * 64:(h + 1) * 64], pv[:, ti, 0:64], sm[:, 4 + ti:5 + ti], HM[:, ti, h * 64:(h + 1) * 64], ALU.mult, ALU.add)

                if SUB >= 6:
                    hd = [(h, d) for h in range(4) for d in range(2)]
                    pro = [capture(lambda h=h, d=d: ml_pro(h, d)) for (h, d) in hd]
                    blk = [capture(lambda h=h, d=d: ml_blocks(h, d)) for (h, d) in hd]
                    epi = [capture(lambda h=h, d=d: ml_epi(h, d)) for (h, d) in hd]
                    emit_list(pro[0])
                    for i in range(len(hd)):
                        emit_list(blk[i])
                        if i + 1 < len(hd):
                            emit_list(pro[i + 1])
                        emit_list(epi[i])
                for ti in range(ntg):
                    for h in range(4):
                        k.ttr(Dbuf[0][:, 0:64], HM[:, ti, h * 64:(h + 1) * 64], HM[:, ti, h * 64:(h + 1) * 64], ALU.mult, ALU.add, sm[:, 8 + h:9 + h])
                    k.act(sm[:, 12:16], sm[:, 8:12], AF.Ln, bias=epsc[:, 0:1], scale=1.0 / 64)
                    k.act(sm[:, 12:16], sm[:, 12:16], AF.Exp, scale=-0.5)
                    for h in range(4):
                        k.ts(HM[:, ti, h * 64:(h + 1) * 64], HM[:, ti, h * 64:(h + 1) * 64], sm[:, 12 + h:13 + h], None, ALU.mult)
                    k.tt(HM[:, ti, :], HM[:, ti, :], mngt[:], ALU.mult)
                    k.tt(HM[:, ti, :], HM[:, ti, :], OM[:, c0t + ti, :], ALU.mult)
                    for p2 in range(2):
                        k.tr(PS[p2][:, ti * 128:(ti + 1) * 128], HM[:, ti, p2 * 128:(p2 + 1) * 128], ident[:])
                for p2 in range(2):
                    k.cp(MLST[:, p2, 0:GS], PS[p2][:, 0:GS], eng='act')

                for ti in range(ntg):
                    i = c0t + ti
                    for gq in range(4):
                        jl = [j for j in (i - 1, i, i + 1) if 0 <= j < nt]
                        for n_, j in enumerate(jl):
                            if j == i - 1:
                                kind = 3
                            elif j == i + 1:
                                kind = 4
                            else:
                                kind = 0 if i == 0 else (2 if i == nt - 1 else 1)
                            k.mm(PS[2][0:64, 0:128], XP[:, j, gq * 64:(gq + 1) * 64], band[:, gq, kind, :], start=(n_ == 0), stop=(n_ == len(jl) - 1))
                        k.cp(DT[:], PS[2][0:64, 0:128], eng='act')
                        k.mm(PS[3][0:64, 0:128], PW[:, gq, :], DT[:])
                        k.ts(PLT[:, gq, ti * 128:(ti + 1) * 128], PS[3][0:64, 0:128], psc[:, gq:gq + 1], None, ALU.mult)

                for ti in range(ntg):
                    tok = t0 + (c0t + ti) * 128
                    k.dma('sp', xt[:], Xd[tok:tok + 128, :])
                    for half in range(2):
                        pp = PS[6 + half]
                        for kc in range(4):
                            k.mm(pp[:], ATTT[:, kc, ti * 128:(ti + 1) * 128], WO[:, kc, half * 512:(half + 1) * 512], start=(kc == 0), stop=False)
                        for kc in range(2):
                            k.mm(pp[:], MLST[:, kc, ti * 128:(ti + 1) * 128], WO[:, 4 + kc, half * 512:(half + 1) * 512], start=False, stop=False)
                        for gq in range(4):
                            k.mm(pp[:], PLT[:, gq, ti * 128:(ti + 1) * 128], WOP[:, gq, half * 512:(half + 1) * 512], start=False, stop=(gq == 3))
                        k.tt(xn[:, half * 512:(half + 1) * 512], pp[:], gB[:, g, 0, half * 512:(half + 1) * 512], ALU.mult)
                    k.tt(xn[:], xn[:], xt[:], ALU.add)
                    k.dma('sp', Xd[tok:tok + 128, :], xn[:])


def phase_b(nc, k, l, E):
    PS = E['PS']; Xd = E['Xd']; st = E['st']
    ident = E['ident']; A2 = E['A2']; modF = E['modF']; gB = E['gB']; epsc = E['epsc']
    iota16 = E['iota16']; qidx = E['qidx']
    from contextlib import ExitStack
    es = ExitStack()

    def sb(name, shape, dt=F32):
        return es.enter_context(nc.sbuf_tensor("%s_%d" % (name, l), list(shape), dt))

    with es:
        WQ = sb("WQ", [128, 8, 2048], BF16)
        fngt = sb("fngt", [128, D])
        junk = sb("junkb", [128, D])
        k.dma('sp', fngt[:], E['fng'])
        SH4 = sb("SH4", [128, 8, 16], U32)
        M15 = sb("M15", [128, 8, 16], U32)
        k.memset(SH4[:], 4)
        k.memset(M15[:], 15)
        KT = sb("KT", [128, 16, 128], BF16)
        xnb = sb("xnb", [128, D])
        stb = sb("stb", [128, 8])
        hTf = sb("hTf", [128, 8, 128])
        hTb = sb("hTb", [128, 8, 128], BF16)
        qT = sb("qT", [128, 16, 128], BF16)
        SC = sb("SC", [128, 16, 128])
        SC2 = sb("SC2", [128, 128])
        TV = sb("TV", [128, 8, 2, 16])
        TI = sb("TI", [128, 8, 2, 16], U32)
        TIf = sb("TIf", [128, 8, 2, 16], BF16)
        CAND = sb("CAND", [128, 8, 256])
        CAND2 = sb("CAND2", [128, 256])
        BSv = sb("BSv", [128, 8, 16])
        BP = sb("BP", [128, 8, 16], U32)
        K1 = sb("K1", [128, 8, 16], U32)
        K2 = sb("K2", [128, 8, 16], U32)
        K1f = sb("K1f", [128, 8, 16], BF16)
        K2f = sb("K2f", [128, 8, 16], BF16)
        EQ = sb("EQ", [128, 8, 16, 16], BF16)
        iob = sb("iob", [128, 16], BF16)
        k.cp(iob[:], iota16[:])
        I1f = sb("I1f", [128, 8, 16])
        I2f = sb("I2f", [128, 8, 16])
        Zs = sb("Zs", [128, 8])
        YO = sb("YO", [128, D])
        XT = [sb("XTb%d" % i, [128, D]) for i in range(2)]
        H2 = [sb("H2b%d" % i, [128, D]) for i in range(2)]
        EX = [sb("EXi%d" % i, [128, 128], I32) for i in range(2)]
        GTs = [sb("GT%d" % i, [128, 8, 16]) for i in range(2)]
        NB = 16
        UV = [sb("UV%d" % i, [128, 2 * D], BF16) for i in range(NB)]
        DG = [sb("DG%d" % i, [128, 128], BF16) for i in range(4)]
        AVs = [sb("AVs%d" % i, [128, 1]) for i in range(8)]
        GAs = [sb("GAs%d" % i, [128, 1]) for i in range(8)]
        puv = E['puv16']
        k.dma('pool', WQ[:], E['wq'][l].rearrange("(kc p) n -> p kc n", p=128))
        k.dma('pool', KT[:], E['keysT'][l])
        tiles = []
        for si, (t0, L, g) in enumerate(SEQS):
            quarter = E['full'] and g == 1
            for tile in range(4 if quarter else L // 128):
                tiles.append((t0 + tile * 128, g, (tile if quarter else None)))
        XB = E['XB']

        def route(info, b):
            k.defer = []
            tok, g, gidx = info
            xt = XT[b]; h2 = H2[b]; EXi = EX[b]; GT = GTs[b]
            if gidx is None:
                k.dma('sp', xt[:], Xd[tok:tok + 128, :])
            else:
                k.gather(xt[:], Xd, qidx[:, gidx:gidx + 1], 0)
            k.ttr(xnb[:], xt[:], xt[:], ALU.mult, ALU.add, stb[:, 0:1])
            k.act(stb[:, 1:2], stb[:, 0:1], AF.Ln, bias=epsc[:, 0:1], scale=1.0 / D)
            k.act(stb[:, 2:3], stb[:, 1:2], AF.Exp, scale=-0.5)
            k.act(xnb[:], xt[:], AF.Copy, scale=stb[:, 2:3])
            for half in range(2):
                for q in range(4):
                    dc = half * 4 + q
                    k.tr(PS[2 + half][:, q * 128:(q + 1) * 128], xnb[:, dc * 128:(dc + 1) * 128], ident[:])
                for q in range(4):
                    dc = half * 4 + q
                    k.act(hTf[:, dc, :], PS[2 + half][:, q * 128:(q + 1) * 128], AF.Identity,
                          bias=modF[:, g, 2, dc:dc + 1], scale=A2[:, g, dc:dc + 1])
                    k.act(hTb[:, dc, :], PS[2 + half][:, q * 128:(q + 1) * 128], AF.Identity,
                          bias=modF[:, g, 2, dc:dc + 1], scale=A2[:, g, dc:dc + 1])
            for half in range(2):
                for q in range(4):
                    dc = half * 4 + q
                    k.tr(PS[4 + half][:, q * 128:(q + 1) * 128], hTf[:, dc, :], ident[:])
                k.cp(h2[:, half * 512:(half + 1) * 512], PS[4 + half][:], eng='act')
            for hc in range(16):
                pp = PS[6 + (hc % 2)]
                for kc in range(8):
                    k.mm(pp[:, 0:128], WQ[:, kc, hc * 128:(hc + 1) * 128], hTb[:, kc, :], start=(kc == 0), stop=(kc == 7))
                k.cp(qT[:, hc, :], pp[:, 0:128], eng='act')
            for q4 in range(4):
                pp = PS[2 + (q4 % 2)]
                for q in range(4):
                    hc = q4 * 4 + q
                    k.mm(pp[:, q * 128:(q + 1) * 128], qT[:, hc, :], KT[:, hc, :])
                k.cp(SC[:, q4 * 4:(q4 + 1) * 4, :], pp[:].rearrange("p (a b) -> p a b", a=4), eng='act')
            for hc in range(16):
                hh, cc = hc // 2, hc % 2
                k.max8(TV[:, hh, cc, 0:8], SC[:, hc, :])
                k.maxidx(TI[:, hh, cc, 0:8], TV[:, hh, cc, 0:8], SC[:, hc, :])
                k.mrep(SC2[:], TV[:, hh, cc, 0:8], SC[:, hc, :], -1e30)
                k.max8(TV[:, hh, cc, 8:16], SC2[:])
                k.maxidx(TI[:, hh, cc, 8:16], TV[:, hh, cc, 8:16], SC2[:])
            k.cp(TIf[:], TI[:])
            k.tt(CAND[:].rearrange("p h (a b) -> p h a b", a=16),
                 TV[:, :, 0, :].unsqueeze(3).broadcast_to([128, 8, 16, 16]),
                 TV[:, :, 1, :].unsqueeze(2).broadcast_to([128, 8, 16, 16]), ALU.add)
            for hh in range(8):
                k.max8(BSv[:, hh, 0:8], CAND[:, hh, :])
                k.maxidx(BP[:, hh, 0:8], BSv[:, hh, 0:8], CAND[:, hh, :])
                k.mrep(CAND2[:], BSv[:, hh, 0:8], CAND[:, hh, :], -1e30)
                k.max8(BSv[:, hh, 8:16], CAND2[:])
                k.maxidx(BP[:, hh, 8:16], BSv[:, hh, 8:16], CAND2[:])
            k.tt(GT[:], BSv[:], BSv[:, :, 0:1].broadcast_to([128, 8, 16]), ALU.subtract)
            k.act(GT[:], GT[:], AF.Exp)
            k.treduce(Zs[:], GT[:], AX.X, ALU.add)
            k.recip(Zs[:], Zs[:])
            k.tt(GT[:], GT[:], Zs[:].unsqueeze(2).broadcast_to([128, 8, 16]), ALU.mult)
            k.tt(K1[:], BP[:], SH4[:], ALU.logical_shift_right)
            k.tt(K2[:], BP[:], M15[:], ALU.bitwise_and)
            k.cp(K1f[:], K1[:])
            k.cp(K2f[:], K2[:])
            for (Kf, cc, If_) in ((K1f, 0, I1f), (K2f, 1, I2f)):
                k.tt(EQ[:], Kf[:].unsqueeze(3).broadcast_to([128, 8, 16, 16]),
                     iob[:].unsqueeze(1).unsqueeze(1).broadcast_to([128, 8, 16, 16]), ALU.is_equal)
                k.tt(EQ[:], EQ[:], TIf[:, :, cc, :].unsqueeze(2).broadcast_to([128, 8, 16, 16]), ALU.mult)
                k.treduce(If_[:], EQ[:], AX.X, ALU.add)
            k.stt(I1f[:], I1f[:], 128.0, I2f[:], ALU.mult, ALU.add)
            k.ts(I1f[:], I1f[:], float(l * 16384), None, ALU.add)
            k.cp(EXi[:], I1f[:].rearrange("p h k -> p (h k)"))
            items = k.defer
            k.defer = None
            return items

        def drain(items):
            if items:
                while items:
                    k.emit(items.pop(0))

        def pull(items):
            if items:
                k.emit(items.pop(0))

        def gather_loop(info, b, nxt):
            tok, g, gidx = info
            xt = XT[b]; h2 = H2[b]; EXi = EX[b]; GT = GTs[b]
            GTf = GT[:].rearrange("p h k -> p (h k)")

            def dot(s_):
                uv = UV[s_ % NB]
                k.gather(uv[:], puv, EXi[:, s_:s_ + 1], 0, skip=('dve',) if s_ >= NB else ())
                k.ttr(junk[:], uv[:, 0:D], h2[:], ALU.mult, ALU.add, AVs[s_ % 8][:])

            def gelu(s_):
                k.act(GAs[s_ % 8][:], AVs[s_ % 8][:], AF.Gelu)
            dot(0)
            dot(1)
            gelu(0)
            for s_ in range(128):
                ga = GAs[s_ % 8]
                dg = DG[s_ % 4]
                if s_ + 2 < 128:
                    dot(s_ + 2)
                pull(nxt)
                if s_ + 1 < 128:
                    gelu(s_ + 1)
                k.act(ga[:], ga[:], AF.Copy, scale=GTf[:, s_:s_ + 1])
                pull(nxt)
                k.act(dg[:], ident[:], AF.Copy, scale=ga[:, 0:1])
                pull(nxt)
                for half in range(2):
                    k.mm(PS[half][:], dg[:], UV[s_ % NB][:, D + half * 512:D + (half + 1) * 512], start=(s_ == 0), stop=(s_ == 127))
            for half in range(2):
                k.tt(YO[:, half * 512:(half + 1) * 512], PS[half][:], gB[:, g, 1, half * 512:(half + 1) * 512], ALU.mult)
            k.tt(YO[:], YO[:], xt[:], ALU.add)
            if l < DEPTH - 1:
                if gidx is None:
                    k.dma('sp', Xd[tok:tok + 128, :], YO[:])
                else:
                    k.dma('sp', XB[gidx // 2][(gidx % 2) * 128:(gidx % 2) * 128 + 128, :], YO[:])
            else:
                k.ttr(junk[:], YO[:], YO[:], ALU.mult, ALU.add, st[:, 4:5])
                k.act(st[:, 5:6], st[:, 4:5], AF.Ln, bias=epsc[:, 0:1], scale=1.0 / D)
                k.act(st[:, 6:7], st[:, 5:6], AF.Exp, scale=-0.5)
                k.stt(YO[:], YO[:], st[:, 6:7], fngt[:], ALU.mult, ALU.mult)
                k.dma('sp', E['Y'][tok:tok + 128, :], YO[:])

        drain(route(tiles[0], 0))
        for i, info in enumerate(tiles):
            nxt = route(tiles[i + 1], (i + 1) % 2) if i + 1 < len(tiles) else None
            gather_loop(info, i % 2, nxt)
            drain(nxt)


def _consts():
    c = {}
    c['ident'] = np.eye(128, dtype=np.float32)
    sel = np.zeros((36, 8, 128), np.float32)
    for d in range(2):
        for h in range(4):
            sel[d * 32 + h, d * 4 + h, :] = 1.0
    c['sel'] = sel
    half = 32
    inv = (10000.0 ** (-np.arange(0, half, 2, dtype=np.float32) / half)).astype(np.float32)
    t = np.arange(2048)
    row = (t // 64).astype(np.float32)
    col = (t % 64).astype(np.float32)
    C = np.zeros((64, 2048), np.float32)
    S = np.zeros((64, 2048), np.float32)
    angr = (row[None, :] * inv[:, None]).astype(np.float32)
    angc = (col[None, :] * inv[:, None]).astype(np.float32)
    C[0:16] = np.cos(angr); C[16:32] = np.cos(angr); C[32:48] = np.cos(angc); C[48:64] = np.cos(angc)
    S[0:16] = np.sin(angr); S[16:32] = np.sin(angr); S[32:48] = np.sin(angc); S[48:64] = np.sin(angc)
    c['ropeC'] = np.concatenate([C, C], 0)
    c['ropeS'] = np.concatenate([S, S], 0)
    P = np.zeros((64, 64), np.float32)
    for o in (0, 32):
        for i in range(16):
            P[o + i, o + 16 + i] = -1.0
            P[o + 16 + i, o + i] = 1.0
    P2 = np.zeros((128, 128), np.float32)
    P2[0:64, 0:64] = P
    P2[64:128, 64:128] = P
    c['prot'] = np.ascontiguousarray(P2.T)
    BIG = 30000.0
    s = np.arange(128)[:, None]
    tt_ = np.arange(128)[None, :]
    trif = np.where(s > tt_, BIG, 0.0).astype(np.float32)
    trib = np.where(s < tt_, BIG, 0.0).astype(np.float32)
    full = np.full((128, 128), BIG, np.float32)
    zero = np.zeros((128, 128), np.float32)
    c['tri01'] = np.stack([(s <= tt_).astype(np.float32), (s >= tt_).astype(np.float32)], 1)
    c['amask'] = np.concatenate([-trif, zero, -trib], 1).astype(np.float32)
    band = np.zeros((128, 4, 5, 128), np.float32)
    Lb = 384
    for gq, w in enumerate((2, 4, 8, 16)):
        def mat(L):
            M = np.zeros((L, L), np.float32)
            for t_ in range(L):
                lo = max(t_ - w // 2, 0); hi = min(t_ + w // 2, L)
                M[t_, lo:hi] = 1.0 / (hi - lo)
                M[t_, t_] -= 1.0
            return M
        M = mat(Lb)
        MT = M.T
        band[:, gq, 0] = MT[0:128, 0:128]
        band[:, gq, 1] = MT[128:256, 128:256]
        band[:, gq, 2] = MT[256:384, 256:384]
        band[:, gq, 3] = MT[0:128, 128:256]
        band[:, gq, 4] = MT[256:384, 128:256]
    c['band'] = band
    c['iota16'] = np.tile(np.arange(16, dtype=np.float32)[None, :], (128, 1))
    return c


_NC_CACHE = {}


def _prep_shared(inp):
    f = lambda a: np.ascontiguousarray(np.asarray(a, dtype=np.float32))
    sh = {}
    w_in = f(inp['w_in'])
    cols = np.concatenate([
        np.arange(0, 512), np.arange(512, 640), np.arange(576, 640), np.arange(512, 576),
        np.arange(768, 1024), np.arange(1024, 1280),
        np.arange(640, 768), np.arange(1280, 1536), np.arange(1792, 1808),
        np.arange(1536, 1792), np.arange(1808, 2064),
        np.arange(512, 640), np.arange(1024, 1280)])
    assert cols.size == NW
    wext = w_in[:, :, cols]
    blocks = [(ob * 128, 128) for ob in range(10)] + [(1280, 400), (1680, 512), (2192, 384)]
    wb_ = np.zeros((DEPTH, 128, 8 * NW), np.float32)
    for (c0, ncol) in blocks:
        blk = wext[:, :, c0:c0 + ncol].reshape(DEPTH, 8, 128, ncol).transpose(0, 2, 1, 3).reshape(DEPTH, 128, 8 * ncol)
        wb_[:, :, 8 * c0:8 * (c0 + ncol)] = blk
    sh['w_in'] = wb_
    sh['w_mod'] = f(inp['w_mod'])
    bm = f(inp['b_mod']).reshape(DEPTH, 6, 8, 128)
    sh['bmodF'] = np.ascontiguousarray(bm[:, [0, 1, 3, 4]].transpose(0, 3, 1, 2))
    sh['bmodG'] = np.ascontiguousarray(np.broadcast_to(f(inp['b_mod']).reshape(DEPTH, 1, 6, D)[:, :, [2, 5]], (DEPTH, 128, 2, D)))
    sh['n1g'] = np.ascontiguousarray(f(inp['norm1_g']).reshape(DEPTH, 8, 128).transpose(0, 2, 1))
    sh['n2g'] = np.ascontiguousarray(f(inp['norm2_g']).reshape(DEPTH, 8, 128).transpose(0, 2, 1))
    sh['gateb'] = np.ascontiguousarray(np.broadcast_to(f(inp['gate_b'])[:, None, :], (DEPTH, 128, 16)))
    sh['sinkb'] = np.ascontiguousarray(np.broadcast_to(f(inp['attn_sink'])[:, None, :], (DEPTH, 128, 8)))
    sh['mng'] = np.ascontiguousarray(np.broadcast_to(f(inp['mlstm_norm_g'])[:, None, :], (DEPTH, 128, 256)))
    sh['poolw'] = np.ascontiguousarray(f(inp['pool_w']).transpose(0, 2, 1, 3))
    sh['pscale'] = np.ascontiguousarray(f(inp['pool_scale']).reshape(DEPTH, 4, 64).transpose(0, 2, 1))
    sh['w_out'] = f(inp['w_out'])
    sh['wq'] = f(inp['peer_wq'])
    pk = f(inp['peer_keys'])
    sh['keysT'] = np.ascontiguousarray(pk.transpose(0, 4, 1, 2, 3).reshape(DEPTH, 128, 16, 128))
    sh['puv'] = np.concatenate([f(inp['peer_u']).reshape(DEPTH * 16384, D), f(inp['peer_v']).reshape(DEPTH * 16384, D)], 1)
    sh['fng'] = np.ascontiguousarray(np.broadcast_to(f(inp['final_norm_g'])[None, :], (128, D)))
    sh.update(_consts())
    return sh


def _prep_core(inp, c):
    f = lambda a: np.ascontiguousarray(np.asarray(a, dtype=np.float32))
    b = c % 2
    m = {}
    m['X'] = np.concatenate([f(inp['x_prompt'][2 * c]), f(inp['x_prompt'][2 * c + 1]), f(inp['x_sample'][b])], 0)
    cond = np.stack([f(inp['c_ctx']), f(inp['c'][b])], 0)
    m['condT'] = np.ascontiguousarray(cond.reshape(2, 8, 128).transpose(2, 0, 1))
    m['cachek'] = f(inp['cache_k'][b]).reshape(DEPTH, 256, 128)
    m['cachev'] = f(inp['cache_v'][b]).reshape(DEPTH, 256, 128)
    sC = f(inp['state_C'][b])
    sn = f(inp['state_n'][b])
    smm = f(inp['state_m'][b])
    ca = np.concatenate([sC, sn[..., None]], -1)
    ca = ca.reshape(DEPTH, 2, 2, 2, 64, 65)
    m['c0a'] = np.ascontiguousarray(ca.transpose(0, 3, 4, 1, 2, 5).reshape(DEPTH, 128, 2, 2, 65))
    m['m0b'] = np.ascontiguousarray(np.broadcast_to(smm.reshape(DEPTH, 1, 8), (DEPTH, 128, 8)))
    m0r = np.zeros((DEPTH, 36, 1), np.float32)
    m0r[:, 0:4, 0] = smm[:, 0]
    m0r[:, 32:36, 0] = smm[:, 1]
    m['m0r'] = m0r
    r = c // 2
    m['qidx'] = (512 + r * 512 + np.arange(4)[None, :] * 128 + np.arange(128)[:, None]).astype(np.int32)
    return m


def kernel(**inputs):
    stage = int(inputs.pop('_stage', 99))
    nl = int(inputs.pop('_nl', DEPTH))
    cores = inputs.pop('_cores', None)
    if (stage, nl) not in _NC_CACHE:
        _NC_CACHE[(stage, nl)] = build(stage, nl)
    nc = _NC_CACHE[(stage, nl)]
    sh = _prep_shared(inputs)
    if stage < 2:
        sh['puv'] = sh['puv'][0:16]
    if cores is not None:
        in_maps = []
        for c in cores:
            m = dict(sh)
            m.update(_prep_core(inputs, c))
            in_maps.append(m)
        res = run_bass_kernel_spmd(nc, in_maps, core_ids=list(range(len(cores))))
        return res.results
    in_maps = []
    for c in range(8):
        m = dict(sh)
        m.update(_prep_core(inputs, c))
        in_maps.append(m)
    res = run_bass_kernel_spmd(nc, in_maps, core_ids=list(range(8)))
    R = res.results
    y_prompt = np.stack([R[c]['Y'][0:512].reshape(2, 256, D) for c in range(8)], 0).reshape(16, 256, D)
    y_sample = np.stack([np.concatenate([R[b + 2 * r]['Y'][512:1024] for r in range(4)], 0) for b in range(2)], 0)
    nk = np.concatenate([R[c]['NK'] for c in range(8)], 0).reshape(16, DEPTH, 256, 2, 64)
    nv = np.concatenate([R[c]['NV'] for c in range(8)], 0).reshape(16, DEPTH, 256, 2, 64)
    nC = np.concatenate([R[c]['NC'] for c in range(8)], 0)
    nn = np.concatenate([R[c]['NN'] for c in range(8)], 0)
    nm = np.concatenate([R[c]['NM'] for c in range(8)], 0)
    return (y_prompt.astype(np.float32), y_sample.astype(np.float32), nk.astype(np.float32), nv.astype(np.float32),
            nC.astype(np.float32), nn.astype(np.float32), nm.astype(np.float32))
```

```python
import numpy as np
import concourse.bass as bass
import concourse.mybir as mybir
from concourse.bass_utils import run_bass_kernel_spmd

F32 = mybir.dt.float32
BF16 = mybir.dt.bfloat16
I32 = mybir.dt.int32
U32 = mybir.dt.uint32
AF = mybir.ActivationFunctionType
ALU = mybir.AluOpType
AX = mybir.AxisListType

D = 1024
DEPTH = 2
NTOK = 2560
SEQS = [(0, 256, 0), (256, 256, 0), (512, 2048, 1)]
NW = 2576
EPS = 1e-6
NDS = 24
SUB = 99
import os
DBG = os.environ.get('KDBG', 'z')


class KB:
    def __init__(self, nc):
        self.nc = nc
        self.eng = dict(pe=nc.tensor, dve=nc.vector, act=nc.scalar, pool=nc.gpsimd, sp=nc.sync)
        self.sems = {e: nc.semaphore("sem_" + e).__enter__() for e in self.eng}
        self.cnt = {e: 0 for e in self.eng}
        self.sems['cc'] = nc.semaphore("sem_cc").__enter__()
        self.cnt['cc'] = 0
        self.dsems = []
        self.dcnt = []
        self.dname = {}
        self.known = {e: {} for e in self.eng}
        self.defer = None
        self.lastw = {}
        self.rd = {}
        self.tiles = {}

    def key(self, ap):
        return ap.tensor.name

    def _deps(self, e, reads, writes, skip=()):
        need = {}

        def add(dep):
            if dep is None:
                return
            kind, a, v = dep
            if kind == 'e':
                if a == 'pe' and e == 'pe':
                    return
                if a in skip:
                    return
                s = self.sems[a]
                kid = ('e', a)
            else:
                s = self.dsems[a]
                v = self.dcnt[a]
                kid = ('d', a)
            if need.get(kid, (None, 0))[1] < v:
                need[kid] = (s, v)

        for r in reads:
            add(self.lastw.get(r))
        for w in writes:
            add(self.lastw.get(w))
            for d in self.rd.get(w, ()):
                add(d)
        for kid, (s, v) in need.items():
            if self.known[e].get(kid, 0) >= v:
                continue
            self.eng[e].wait_ge(s, v)
            self.known[e][kid] = v

    def _record(self, dep, reads, writes):
        for r in reads:
            self.rd.setdefault(r, []).append(dep)
        for w in writes:
            self.lastw[w] = dep
            self.rd[w] = []

    def op(self, e, fn, reads=(), writes=()):
        if self.defer is not None:
            self.defer.append(('op', e, fn, reads, writes))
            return
        reads = [self.key(r) if not isinstance(r, str) else r for r in reads]
        writes = [self.key(w) if not isinstance(w, str) else w for w in writes]
        self._deps(e, reads, writes)
        ins = fn(self.eng[e])
        self.cnt[e] += 1
        ins.then_inc(self.sems[e], 1)
        self._record(('e', e, self.cnt[e]), reads, writes)

    def dma(self, q, out, in_, si=None, extra_reads=(), wkey=None, **kw):
        if self.defer is not None:
            self.defer.append(('dma', q, out, in_, kw))
            return
        reads = [self.key(in_)] + [self.key(r) for r in extra_reads]
        writes = [wkey if wkey is not None else self.key(out)]
        self._deps(q, reads, writes)
        si = self._dsem(writes[0])
        ins = self.eng[q].dma_start(out=out, in_=in_, **kw)
        self.dcnt[si] += 16
        ins.then_inc(self.dsems[si], 16)
        self._record(('d', si, self.dcnt[si]), reads, writes)

    def _dsem(self, name):
        parts = name.rsplit('_', 1)
        if len(parts) == 2 and parts[1].isdigit() and len(parts[1]) == 1:
            name = parts[0]
        if name not in self.dname:
            self.dname[name] = len(self.dsems)
            self.dsems.append(self.nc.semaphore("dsem%d" % len(self.dsems)).__enter__())
            self.dcnt.append(0)
        return self.dname[name]

    def emit(self, item):
        if item[0] == 'op':
            self.op(item[1], item[2], item[3], item[4])
        elif item[0] == 'dma':
            self.dma(item[1], item[2], item[3], **item[4])
        else:
            self.gather(item[1], item[2], item[3], 0)

    def max8(self, out, in_):
        self.op('dve', lambda e: e.max(out, in_), reads=[in_], writes=[out])

    def maxidx(self, out, mx, vals):
        self.op('dve', lambda e: e.max_index(out, mx, vals), reads=[mx, vals], writes=[out])

    def mrep(self, out, rep, vals, imm):
        self.op('dve', lambda e: e.match_replace(out, rep, vals, imm), reads=[rep, vals], writes=[out])

    def treduce(self, out, in_, axis, op):
        self.op('dve', lambda e: e.tensor_reduce(out, in_, axis, op), reads=[in_], writes=[out])

    def gather(self, out, table, idx, si, skip=()):
        if self.defer is not None:
            self.defer.append(('gather', out, table, idx))
            return
        reads = [self.key(table), self.key(idx)]
        writes = [self.key(out)]
        self._deps('pool', reads, writes, skip)
        si = self._dsem(writes[0])
        ins = self.nc.gpsimd.indirect_dma_start(
            out=out, out_offset=None, in_=table,
            in_offset=bass.IndirectOffsetOnAxis(ap=idx, axis=0))
        self.dcnt[si] += 16
        ins.then_inc(self.dsems[si], 16)
        self._record(('d', si, self.dcnt[si]), reads, writes)

    def allgather(self, out, in_, groups):
        reads = [self.key(in_)]
        writes = [self.key(out)]
        self._deps('pool', reads, writes)
        ins = self.nc.gpsimd.collective_compute("AllGather", mybir.AluOpType.bypass, replica_groups=groups,
                                                ins=[in_.opt()], outs=[out.opt()])
        self.cnt['cc'] += 1
        ins.then_inc(self.sems['cc'])
        self._record(('e', 'cc', self.cnt['cc']), reads, writes)

    def barrier_all(self):
        for e in self.eng:
            for o in self.sems:
                v = self.cnt[o]
                if v and self.known[e].get(('e', o), 0) < v:
                    self.eng[e].wait_ge(self.sems[o], v)
                    self.known[e][('e', o)] = v
            for i in range(len(self.dsems)):
                v = self.dcnt[i]
                if v and self.known[e].get(('d', i), 0) < v:
                    self.eng[e].wait_ge(self.dsems[i], v)
                    self.known[e][('d', i)] = v

    def sb(self, name, shape, dt=F32):
        t = self.nc.sbuf_tensor(name, list(shape), dt).__enter__()
        self.tiles[name] = t
        return t

    def mm(self, out, lhsT, rhs, start=True, stop=True, **kw):
        self.op('pe', lambda e: e.matmul(out, lhsT, rhs, start=start, stop=stop, **kw),
                reads=[lhsT, rhs], writes=[out])

    def tr(self, out, in_, ident):
        self.op('pe', lambda e: e.transpose(out, in_, ident), reads=[in_, ident], writes=[out])

    def act(self, out, in_, func, bias=None, scale=1.0, accum_out=None, eng='act'):
        reads = [in_]
        kw = {}
        if bias is not None:
            kw['bias'] = bias
            if not isinstance(bias, (int, float)):
                reads.append(bias)
        if not isinstance(scale, (int, float)):
            reads.append(scale)
        writes = [out]
        if accum_out is not None:
            kw['accum_out'] = accum_out
            writes.append(accum_out)
        self.op('act', lambda e: e.activation(out, in_, func, scale=scale, **kw), reads=reads, writes=writes)

    def tt(self, out, in0, in1, op, eng='dve'):
        self.op(eng, lambda e: e.tensor_tensor(out, in0, in1, op), reads=[in0, in1], writes=[out])

    def ts(self, out, in0, s1, s2, op0, op1=None, eng='dve', accum_out=None):
        reads = [in0] + [s for s in (s1, s2) if s is not None and not isinstance(s, (int, float))]
        writes = [out] + ([accum_out] if accum_out is not None else [])
        kw = {}
        if op1 is not None:
            kw['op1'] = op1
        if accum_out is not None:
            kw['accum_out'] = accum_out
        self.op(eng, lambda e: e.tensor_scalar(out, in0, s1, s2, op0, **kw), reads=reads, writes=writes)

    def stt(self, out, in0, scalar, in1, op0, op1):
        reads = [in0, in1] + ([scalar] if not isinstance(scalar, (int, float)) else [])
        self.op('dve', lambda e: e.scalar_tensor_tensor(out, in0, scalar, in1, op0, op1), reads=reads, writes=[out])

    def ttr(self, out, in0, in1, op0, op1, accum_out, scale=1.0, scalar=0.0):
        self.op('dve', lambda e: e.scalar_tensor_tensor(out, in0, 1.0, in1, ALU.mult, ALU.mult, accum_out=accum_out),
                reads=[in0, in1], writes=[out, accum_out])

    def cp(self, out, in_, eng='dve'):
        if eng == 'act':
            self.op('act', lambda e: e.copy(out, in_), reads=[in_], writes=[out])
        else:
            self.op(eng, lambda e: e.tensor_copy(out, in_), reads=[in_], writes=[out])

    def memset(self, ap, val, eng='dve'):
        self.op(eng, lambda e: e.memset(ap, val), writes=[ap])

    def recip(self, out, in_):
        self.op('dve', lambda e: e.reciprocal(out, in_), reads=[in_], writes=[out])

    def scan(self, out, d0, d1, init, op0, op1):
        reads = [d0, d1] + ([init] if not isinstance(init, (int, float)) else [])
        self.op('dve', lambda e: e.tensor_tensor_scan(out, d0, d1, init, op0, op1), reads=reads, writes=[out])


def build(stage=99, nl=DEPTH):
    nc = bass.Bass("TRN2", target_bir_lowering=False)
    k = KB(nc)

    def din(name, shape, dt=F32):
        return nc.dram_tensor(name, list(shape), dt, kind="ExternalInput").ap()

    def dout(name, shape, dt=F32):
        return nc.dram_tensor(name, list(shape), dt, kind="ExternalOutput").ap()

    X = din("X", [NTOK, D])
    condT = din("condT", [128, 2, 8])
    cachek = din("cachek", [DEPTH, 256, 128])
    cachev = din("cachev", [DEPTH, 256, 128])
    c0a_d = din("c0a", [DEPTH, 128, 2, 2, 65])
    m0b_d = din("m0b", [DEPTH, 128, 8])
    m0r_d = din("m0r", [DEPTH, 36, 1])
    w_mod = din("w_mod", [DEPTH, D, 6 * D])
    bmodF = din("bmodF", [DEPTH, 128, 4, 8])
    bmodG = din("bmodG", [DEPTH, 128, 2, D])
    n1g = din("n1g", [DEPTH, 128, 8])
    n2g = din("n2g", [DEPTH, 128, 8])
    w_in = din("w_in", [DEPTH, 128, 8 * NW])
    gateb = din("gateb", [DEPTH, 128, 16])
    sinkb = din("sinkb", [DEPTH, 128, 8])
    mng = din("mng", [DEPTH, 128, 256])
    poolw = din("poolw", [DEPTH, 64, 4, 64])
    pscale = din("pscale", [DEPTH, 64, 4])
    w_out = din("w_out", [DEPTH, D, D])
    wq = din("wq", [DEPTH, D, 2048])
    keysT = din("keysT", [DEPTH, 128, 16, 128])
    ntab = DEPTH * 16384 if stage >= 2 else 16
    puv = din("puv", [ntab, 2 * D])
    fng = din("fng", [128, D])
    ident_d = din("ident", [128, 128])
    sel_d = din("sel", [36, 8, 128])
    ropeC_d = din("ropeC", [128, 2048])
    ropeS_d = din("ropeS", [128, 2048])
    prot_d = din("prot", [128, 128])
    tri01_d = din("tri01", [128, 2, 128])
    amask_d = din("amask", [128, 384])
    band_d = din("band", [128, 4, 5, 128])
    iota_d = din("iota16", [128, 16])

    full = not (stage < 2 or nl < DEPTH)
    Y = dout("Y", [1024 if full else NTOK, D])
    qidx_d = din("qidx", [128, 4], I32)
    NK = dout("NK", [2, DEPTH, 256, 128])
    NV = dout("NV", [2, DEPTH, 256, 128])
    NC_ = dout("NC", [2, DEPTH, 2, 4, 64, 64])
    NN = dout("NN", [2, DEPTH, 2, 4, 64])
    NM = dout("NM", [2, DEPTH, 2, 4])

    Xd = nc.dram_tensor("Xd", [NTOK, D], F32).ap()
    XB = [nc.dram_tensor("XB%d" % i, [256, D], F32).ap() for i in range(2)]
    XG = [nc.dram_tensor("XG%d" % i, [1024, D], F32).ap() for i in range(2)]

    PS = [nc.psum_tensor("ps%d" % i, [128, 512], F32).__enter__() for i in range(8)]

    ident = k.sb("identf", [128, 128])
    identb = k.sb("identb", [128, 128], BF16)
    sel = k.sb("selc", [36, 8, 128])
    prot = k.sb("protc", [128, 128])
    tri01 = k.sb("tri01c", [128, 2, 128], BF16)
    amask = k.sb("amaskc", [128, 384], BF16)
    band = k.sb("bandc", [128, 4, 5, 128], BF16)
    iota16 = k.sb("iota16c", [128, 16])
    onec = k.sb("onec", [128, 1])
    epsc = k.sb("epsc", [128, 1])
    k.dma('sp', ident[:], ident_d)
    k.dma('pool', identb[:], ident_d)
    k.dma('sp', sel[:], sel_d)
    k.dma('sp', prot[:], prot_d)
    k.dma('pool', tri01[:], tri01_d)
    k.dma('pool', amask[:], amask_d)
    k.dma('pool', band[:], band_d)
    k.dma('sp', iota16[:], iota_d)
    k.memset(onec[:], 1.0)
    k.memset(epsc[:], EPS)

    xt = k.sb("xt", [128, D])
    xn = k.sb("xn", [128, D])
    st = k.sb("stat", [128, 8])

    for i in range(NTOK // 128):
        k.dma('sp', xt[:], X[i * 128:(i + 1) * 128, :])
        k.dma('sp', Xd[i * 128:(i + 1) * 128, :], xt[:])

    puv16 = nc.dram_tensor("puv16", [ntab, 2 * D], BF16).ap()
    w16 = nc.dram_tensor("w16", [DEPTH, 128, 8 * NW], BF16).ap()
    cT = k.sb("cT", [128, 2, 8])
    scT = k.sb("scT", [128, 2, 8])
    k.dma('pool', cT[:], condT)
    k.act(scT[:], cT[:], AF.Silu)
    modFs = [k.sb("modF%d" % l, [128, 2, 4, 8]) for l in range(DEPTH)]
    A1s = [k.sb("A1_%d" % l, [128, 2, 8]) for l in range(DEPTH)]
    A2s = [k.sb("A2_%d" % l, [128, 2, 8]) for l in range(DEPTH)]
    gB = k.sb("gB", [128, 2, 2, D], BF16)
    gBd = nc.dram_tensor("gBd", [DEPTH, 128, 2 * 2 * D], BF16).ap()
    n1t = k.sb("n1t", [128, 8])
    n2t = k.sb("n2t", [128, 8])
    bmF = k.sb("bmF", [128, 4, 8])

    qidx = k.sb("qidx_sb", [128, 4], I32)
    k.dma('sp', qidx[:], qidx_d)

    def rmsnorm_tile(tok0, gidx=None):
        if gidx is None:
            k.dma('sp', xt[:], Xd[tok0:tok0 + 128, :])
        else:
            k.gather(xt[:], Xd, qidx[:, gidx:gidx + 1], 0)
        k.ttr(xn[:], xt[:], xt[:], ALU.mult, ALU.add, st[:, 0:1])
        k.act(st[:, 1:2], st[:, 0:1], AF.Ln, bias=epsc[:, 0:1], scale=1.0 / D)
        k.act(st[:, 2:3], st[:, 1:2], AF.Exp, scale=-0.5)
        k.ts(xn[:], xt[:], st[:, 2:3], None, ALU.mult)

    with nc.sbuf_tensor("pcf0", [128, 4 * D], F32) as pcf0, nc.sbuf_tensor("pcf1", [128, 4 * D], F32) as pcf1, \
            nc.sbuf_tensor("pcf2", [128, 4 * D], F32) as pcf2, nc.sbuf_tensor("pcf3", [128, 4 * D], F32) as pcf3, \
            nc.sbuf_tensor("pcb0", [128, 4 * D], BF16) as pcb0, nc.sbuf_tensor("pcb1", [128, 4 * D], BF16) as pcb1, \
            nc.sbuf_tensor("pcb2", [128, 4 * D], BF16) as pcb2, nc.sbuf_tensor("pcb3", [128, 4 * D], BF16) as pcb3:
        pcf = [pcf0, pcf1, pcf2, pcf3]
        pcb = [pcb0, pcb1, pcb2, pcb3]
        wi = 0
        for l in range(DEPTH):
            for o in range(0, 8 * NW, 4096):
                n = min(4096, 8 * NW - o)
                k.dma('sp', pcf[wi % 4][:, 0:n], w_in[l, :, o:o + n])
                k.cp(pcb[wi % 4][:, 0:n], pcf[wi % 4][:, 0:n], eng='act')
                k.dma('act', w16[l, :, o:o + n], pcb[wi % 4][:, 0:n], wkey="w16w%d" % (wi % 4))
                wi += 1
        if stage >= 2:
            for ch in range(ntab // 256):
                src = puv[ch * 256:(ch + 1) * 256, :].rearrange("(p r) n -> p (r n)", r=2)
                dst = puv16[ch * 256:(ch + 1) * 256, :].rearrange("(p r) n -> p (r n)", r=2)
                k.dma('sp', pcf[(ch + wi) % 4][:], src)
                k.cp(pcb[(ch + wi) % 4][:], pcf[(ch + wi) % 4][:], eng='act')
                k.dma('act', dst, pcb[(ch + wi) % 4][:], wkey="puv16w%d" % (ch % 4))
        with nc.sbuf_tensor("wmB0", [128, 8, 512], F32) as wmB0, nc.sbuf_tensor("wmB1", [128, 8, 512], F32) as wmB1, \
                nc.sbuf_tensor("screp", [128, 2, 8, 128], F32) as screp, \
                nc.sbuf_tensor("bmG", [128, 2, D], F32) as bmG:
            wmBs = [wmB0, wmB1]
            for g in range(2):
                for kc in range(8):
                    k.cp(screp[:, g, kc, :], scT[:, g, kc:kc + 1].to_broadcast([128, 128]))
            for l in range(nl):
                modF = modFs[l]; A1 = A1s[l]; A2 = A2s[l]
                k.dma('pool', n1t[:], n1g[l])
                k.dma('pool', n2t[:], n2g[l])
                k.dma('pool', bmF[:], bmodF[l])
                k.dma('pool', bmG[:], bmodG[l])
                parts = [0, 1, 3, 4]
                cnt = 0
                for pi, p in enumerate(parts):
                    for hf in range(2):
                        c0 = p * D + hf * 512
                        wm = wmBs[cnt % 2]
                        cnt += 1
                        k.dma('pool', wm[:], w_mod[l, :, c0:c0 + 512].rearrange("(kc p) n -> p kc n", p=128))
                        for q in range(4):
                            dc = hf * 4 + q
                            for kc in range(8):
                                k.mm(PS[q % 2][:, 0:2], wm[:, kc, q * 128:(q + 1) * 128], scT[:, :, kc], start=(kc == 0), stop=(kc == 7))
                            k.cp(modF[:, :, pi, dc], PS[q % 2][:, 0:2])
                for g in range(2):
                    k.tt(modF[:, g, :, :], modF[:, g, :, :], bmF[:], ALU.add)
                    k.ts(A1[:, g, :], modF[:, g, 1, :], 1.0, None, ALU.add)
                    k.tt(A1[:, g, :], A1[:, g, :], n1t[:], ALU.mult)
                    k.ts(A2[:, g, :], modF[:, g, 3, :], 1.0, None, ALU.add)
                    k.tt(A2[:, g, :], A2[:, g, :], n2t[:], ALU.mult)
                for gi, p in enumerate([2, 5]):
                    for hf in range(2):
                        c0 = p * D + hf * 512
                        wm = wmBs[cnt % 2]
                        cnt += 1
                        k.dma('pool', wm[:], w_mod[l, :, c0:c0 + 512].rearrange("(kc p) n -> p kc n", p=128))
                        for g in range(2):
                            pp = PS[2 + 2 * (cnt % 2) + g]
                            for kc in range(8):
                                k.mm(pp[:], screp[:, g, kc, :], wm[:, kc, :], start=(kc == 0), stop=(kc == 7))
                            k.tt(gB[:, g, gi, hf * 512:(hf + 1) * 512], pp[:], bmG[:, gi, hf * 512:(hf + 1) * 512], ALU.add)
                k.dma('pool', gBd[l], gB[:].rearrange("p a b d -> p (a b d)"))
        k.barrier_all()

    pending_exchange = False
    for l in range(nl):
        modF = modFs[l]; A1 = A1s[l]; A2 = A2s[l]
        k.dma('sp', gB[:].rearrange("p a b d -> p (a b d)"), gBd[l])
        if stage >= 1:
            phase_a(nc, k, l, locals())
        k.barrier_all()
        if stage >= 2:
            phase_b(nc, k, l, locals())
        k.barrier_all()
        if full and l < DEPTH - 1:
            for hf in range(2):
                k.allgather(XG[hf], XB[hf], [[0, 2, 4, 6], [1, 3, 5, 7]])
            pending_exchange = True

    if stage < 2 or nl < DEPTH:
        for i in range(NTOK // 128):
            k.dma('sp', xt[:], Xd[i * 128:(i + 1) * 128, :])
            k.dma('sp', Y[i * 128:(i + 1) * 128, :], xt[:])
    k.barrier_all()
    return nc


def phase_a(nc, k, l, E):
    PS = E['PS']; Xd = E['Xd']; xt = E['xt']; xn = E['xn']; st = E['st']
    ident = E['ident']; identb = E['identb']; sel = E['sel']; prot = E['prot']
    tri01 = E['tri01']; amask = E['amask']; band = E['band']
    A1 = E['A1']; modF = E['modF']; gB = E['gB']; onec = E['onec']; epsc = E['epsc']
    rmsnorm_tile = E['rmsnorm_tile']
    w16 = E['w16']; w_out = E['w_out']
    from contextlib import ExitStack
    es = ExitStack()

    def sb(name, shape, dt=F32):
        return es.enter_context(nc.sbuf_tensor("%s_%d" % (name, l), list(shape), dt))

    with es:
        hT = sb("hT", [128, 8, 512], BF16)
        wfm = [sb("wfm%d" % i, [128, 8, 128], BF16) for i in range(2)]
        wtm = [sb("wtm0", [128, 8, 512], BF16)]
        WO = sb("WO", [128, 6, D], BF16)
        WOP = sb("WOP", [64, 4, D], BF16)
        ropeC = sb("ropeC", [128, 512])
        ropeS = sb("ropeS", [128, 512])
        QAT = sb("QAT", [128, 4, 2048], BF16)
        KAT = sb("KAT", [128, 2, 2048], BF16)
        VA = sb("VA", [128, 16, 2, 66], BF16)
        KCT = sb("KCT", [128, 2, 256], BF16)
        VC = sb("VC", [128, 2, 2, 66], BF16)
        QMT = sb("QMT", [128, 2, 2048], BF16)
        KMT = sb("KMT", [128, 2, 2048], BF16)
        VM = sb("VM", [128, 16, 4, 66], BF16)
        OM = sb("OM", [128, 16, 256], BF16)
        XP = sb("XP", [128, 16, 256], BF16)
        GPI = sb("GPI", [128, 16, 36])
        GPF = sb("GPF", [128, 16, 36])
        A_tok = sb("A_tok", [128, 16, 36])
        B_tok = sb("B_tok", [128, 16, 36])
        E_tok = sb("E_tok", [128, 16, 36])
        RA = sb("RA", [36, 2048])
        RB = sb("RB", [36, 2048])
        M0r = sb("M0r", [36, 1])
        M0b = sb("M0b", [128, 8])
        BT = sb("BT", [36, 1])
        MF = sb("MF", [36, 1])
        C0A = sb("C0A", [128, 2, 2, 66], BF16)
        gbt = sb("gbt", [128, 16])
        sinkE = sb("sinkE", [128, 8])
        mngt = sb("mngt", [128, 256])
        PW = sb("PW", [64, 4, 64], BF16)
        psc = sb("psc", [64, 4])
        qf = sb("qf", [128, 512])
        Dbuf = [sb("Dbuf%d" % i, [128, 512]) for i in range(2)]
        t1, t2 = Dbuf
        Wbuf = [sb("Wbuf%d" % i, [128, 512], BF16) for i in range(2)]
        OTb = [sb("OTb%d" % i, [65, 512]) for i in range(2)]
        HM = sb("HM", [128, 4, 256])
        ATTT = sb("ATTT", [128, 4, 512], BF16)
        MLST = sb("MLST", [128, 2, 512], BF16)
        PLT = sb("PLT", [64, 4, 512], BF16)
        DT = sb("DTb", [64, 128], BF16)
        sm = sb("sm", [128, 16])
        kcf = sb("kcf", [128, 2, 128])
        ktok = sb("ktok", [128, 128])
        vtok = sb("vtok", [128, 128])
        KMtok = sb("KMtok", [128, 2, 256])
        MLB = sb("MLB", [128, 8])
        Ftok = sb("Ftok", [128, 2, 8])
        kw_ = sb("kw", [128, 64], BF16)
        cst = sb("cst", [64, 65])

        k.dma('sp', gbt[:], E['gateb'][l])
        k.dma('sp', sinkE[:], E['sinkb'][l])
        k.act(sinkE[:], sinkE[:], AF.Exp)
        k.dma('sp', mngt[:], E['mng'][l])
        k.dma('pool', PW[:], E['poolw'][l])
        k.dma('sp', psc[:], E['pscale'][l])
        k.dma('pool', WO[:], w_out[l, 0:768, :].rearrange("(kc p) n -> p kc n", p=128))
        k.dma('pool', WOP[:], w_out[l, 768:1024, :].rearrange("(g p) n -> p g n", p=64))
        k.dma('pool', C0A[:, :, :, 0:65], E['c0a_d'][l])
        k.memset(VA[:, :, :, 64:65], 1.0)
        k.memset(VC[:, :, :, 64:65], 1.0)
        k.memset(VM[:, :, :, 64:65], 1.0)
        k.memset(GPI[:], 0.0)
        k.memset(GPF[:], 0.0)
        for j in range(2):
            k.dma('sp', kcf[:, j, :], E['cachek'][l, j * 128:(j + 1) * 128, :])
        for j in range(2):
            k.tr(PS[0][:, j * 128:(j + 1) * 128], kcf[:, j, :], ident[:])
        k.cp(KCT[:, 0, :], PS[0][:, 0:256])
        for j in range(2):
            k.tr(PS[1][0:64, j * 128:(j + 1) * 128], kcf[:, j, 64:128], ident[:])
        k.cp(KCT[0:64, 1, :], PS[1][0:64, 0:256])
        for j in range(2):
            k.tr(PS[2][0:64, j * 128:(j + 1) * 128], kcf[:, j, 0:64], ident[:])
        k.cp(t1[0:64, 0:256], PS[2][0:64, 0:256])
        k.cp(Wbuf[0][0:64, 0:256], t1[0:64, 0:256])
        k.dma('sp', KCT[64:128, 1, :], Wbuf[0][0:64, 0:256])
        for j in range(2):
            k.dma('sp', vtok[:], E['cachev'][l, j * 128:(j + 1) * 128, :])
            k.cp(VC[:, j, :, 0:64], vtok[:].rearrange("p (a b) -> p a b", a=2))

        for si, (t0, L, g) in enumerate(SEQS):
            nt = L // 128
            rope = (g == 1)
            if SUB < 1:
                continue
            if rope and E['pending_exchange']:
                XG = E['XG']
                for r_ in range(4):
                    for j_ in range(4):
                        src = XG[j_ // 2][r_ * 256 + (j_ % 2) * 128:r_ * 256 + (j_ % 2) * 128 + 128, :]
                        k.dma('sp', Xd[512 + r_ * 512 + j_ * 128:512 + r_ * 512 + (j_ + 1) * 128, :], src)
            GS = min(L, 512)
            ntg = GS // 128
            ngrp = L // GS
            for gi in range(ngrp):
                for ti in range(ntg):
                    tile = gi * ntg + ti
                    rmsnorm_tile(t0 + tile * 128)
                    for half in range(2):
                        for q in range(4):
                            dc = half * 4 + q
                            k.tr(PS[half][:, q * 128:(q + 1) * 128], xn[:, dc * 128:(dc + 1) * 128], ident[:])
                        for q in range(4):
                            dc = half * 4 + q
                            k.act(hT[:, dc, ti * 128:(ti + 1) * 128], PS[half][:, q * 128:(q + 1) * 128], AF.Identity,
                                  bias=modF[:, g, 0, dc:dc + 1], scale=A1[:, g, dc:dc + 1])
                c0 = gi * GS
                if DBG < 'b':
                    continue
                if rope:
                    k.dma('sp', ropeC[:, 0:GS], E['ropeC_d'][:, c0:c0 + GS])
                    k.dma('sp', ropeS[:, 0:GS], E['ropeS_d'][:, c0:c0 + GS])
                for ob in range(10):
                    wb = wfm[ob % 2]
                    k.dma('sp', wb[:], w16[l, :, 8 * ob * 128:8 * (ob + 1) * 128].rearrange("p (kc n) -> p kc n", kc=8))
                    pp = PS[2 + (ob % 2)]
                    for kc in range(8):
                        k.mm(pp[:, 0:GS], wb[:, kc, :], hT[:, kc, 0:GS], start=(kc == 0), stop=(kc == 7))
                    if ob < 4:
                        dst = QAT[:, ob, c0:c0 + GS]
                    elif ob < 6:
                        dst = KAT[:, ob - 4, c0:c0 + GS]
                    elif ob < 8:
                        dst = QMT[:, ob - 6, c0:c0 + GS]
                    else:
                        dst = KMT[:, ob - 8, c0:c0 + GS]
                    if rope and ob < 6:
                        k.cp(qf[:, 0:GS], pp[:, 0:GS], eng='act')
                        k.mm(PS[4][:, 0:GS], prot[:], qf[:, 0:GS])
                        k.tt(t1[:, 0:GS], PS[4][:, 0:GS], ropeS[:, 0:GS], ALU.mult)
                        k.tt(t2[:, 0:GS], qf[:, 0:GS], ropeC[:, 0:GS], ALU.mult, eng='pool')
                        k.tt(dst, t1[:, 0:GS], t2[:, 0:GS], ALU.add)
                    else:
                        k.cp(dst, pp[:, 0:GS], eng='act')
                if DBG < 'c':
                    continue
                tmb = [(1280, 400), (1680, 512)] + ([(2192, 384)] if not rope else [])
                for bi, (cb, ncol) in enumerate(tmb):
                    wb = wtm[0]
                    k.dma('sp', wb[:, :, 0:ncol], w16[l, :, 8 * cb:8 * (cb + ncol)].rearrange("p (kc n) -> p kc n", kc=8))
                    for ti in range(ntg):
                        tile = gi * ntg + ti
                        pp = PS[5 + (ti % 2)]
                        for kc in range(8):
                            k.mm(pp[:, 0:ncol], hT[:, kc, ti * 128:(ti + 1) * 128], wb[:, kc, 0:ncol], start=(kc == 0), stop=(kc == 7))
                        if DBG < 'd':
                            continue
                        if bi == 0:
                            D2 = os.environ.get('KD2', '1234')
                            if '1' in D2:
                                k.cp(VA[:, tile, :, 0:64], pp[:, 0:128].rearrange("p (a b) -> p a b", a=2))
                            if not rope and '2' in D2:
                                k.cp(vtok[:], pp[:, 0:128])
                                k.dma('sp', E['NV'][si, l, tile * 128:(tile + 1) * 128, :], vtok[:])
                            if '3' in D2:
                                k.cp(VM[:, tile, :, 0:64], pp[:, 128:384].rearrange("p (a b) -> p a b", a=4))
                            if '4' in D2:
                                k.tt(GPI[:, tile, 0:4], pp[:, 384:388], gbt[:, 0:4], ALU.add)
                                k.tt(GPF[:, tile, 0:4], pp[:, 388:392], gbt[:, 4:8], ALU.add)
                                k.tt(GPI[:, tile, 32:36], pp[:, 392:396], gbt[:, 8:12], ALU.add)
                                k.tt(GPF[:, tile, 32:36], pp[:, 396:400], gbt[:, 12:16], ALU.add)
                        elif bi == 1 and DBG >= 'e':
                            k.act(OM[:, tile, :], pp[:, 0:256], AF.Sigmoid)
                            k.cp(XP[:, tile, :], pp[:, 256:512], eng='act')
                        elif bi == 2 and DBG >= 'f':
                            k.cp(ktok[:], pp[:, 0:128])
                            k.dma('sp', E['NK'][si, l, tile * 128:(tile + 1) * 128, :], ktok[:])
                            k.cp(KMtok[:, tile, :], pp[:, 128:384])

            if SUB < 2:
                continue
            if rope:
                k.dma('sp', M0r[:], E['m0r_d'][l])
                k.dma('sp', M0b[:], E['m0b_d'][l])
            else:
                k.memset(M0r[:], 0.0)
                k.memset(M0b[:], 0.0)
            for tile in range(nt):
                k.tr(PS[0][0:36, 0:128], GPI[:, tile, :], ident[:])
                k.cp(RA[:, tile * 128:(tile + 1) * 128], PS[0][0:36, 0:128])
                k.tr(PS[1][0:36, 0:128], GPF[:, tile, :], ident[:])
                k.cp(RB[:, tile * 128:(tile + 1) * 128], PS[1][0:36, 0:128], eng='act')
            k.act(RB[:, 0:L], RB[:, 0:L], AF.Exp, scale=-1.0)
            k.act(RB[:, 0:L], RB[:, 0:L], AF.Ln, bias=onec[0:36, 0:1])
            k.scan(RB[0:4, 0:L], RB[0:4, 0:L], RB[0:4, 0:L], 0.0, ALU.add, ALU.max)
            k.scan(RB[32:36, 0:L][:, ::-1], RB[32:36, 0:L][:, ::-1], RB[32:36, 0:L][:, ::-1], 0.0, ALU.add, ALU.max)
            k.tt(RA[:, 0:L], RA[:, 0:L], RB[:, 0:L], ALU.add)
            k.cp(BT[0:4, :], RB[0:4, L - 1:L])
            k.cp(BT[32:36, :], RB[32:36, 0:1])
            for tile in range(nt):
                k.tr(PS[0][:, 0:36], RA[:, tile * 128:(tile + 1) * 128], ident[0:36, 0:36])
                k.cp(A_tok[:, tile, :], PS[0][:, 0:36])
                k.tr(PS[1][:, 0:36], RB[:, tile * 128:(tile + 1) * 128], ident[0:36, 0:36])
                k.cp(B_tok[:, tile, :], PS[1][:, 0:36], eng='act')
            k.scan(RB[0:4, 0:L], RA[0:4, 0:L], RA[0:4, 0:L], M0r[0:4, 0:1], ALU.max, ALU.max)
            k.scan(RB[32:36, 0:L][:, ::-1], RA[32:36, 0:L][:, ::-1], RA[32:36, 0:L][:, ::-1], M0r[32:36, 0:1], ALU.max, ALU.max)
            for tile in range(nt):
                k.tr(PS[0][:, 0:36], RB[:, tile * 128:(tile + 1) * 128], ident[0:36, 0:36])
                k.cp(E_tok[:, tile, :], PS[0][:, 0:36])
            k.tt(E_tok[:, 0:nt, :], B_tok[:, 0:nt, :], E_tok[:, 0:nt, :], ALU.subtract)
            k.act(E_tok[:, 0:nt, :], E_tok[:, 0:nt, :], AF.Exp)

            if not rope and SUB >= 3:
                k.tt(MF[0:4, :], RB[0:4, L - 1:L], BT[0:4, :], ALU.subtract)
                k.tt(MF[32:36, :], RB[32:36, 0:1], BT[32:36, :], ALU.subtract)
                for d in range(2):
                    k.dma('sp', E['NM'][si, l, d, :].rearrange("(h o) -> h o", o=1), MF[d * 32:d * 32 + 4, :])
                for d in range(2):
                    col = (L - 1) if d == 0 else 0
                    for h in range(4):
                        k.mm(PS[2][:, d * 4 + h:d * 4 + h + 1], sel[:, d * 4 + h, :], RB[:, col:col + 1])
                k.cp(MLB[:], PS[2][:, 0:8])
                for j in range(nt):
                    for d in range(2):
                        k.tt(Ftok[:, j, d * 4:d * 4 + 4], A_tok[:, j, d * 32:d * 32 + 4], MLB[:, d * 4:d * 4 + 4], ALU.subtract)
                k.act(Ftok[:], Ftok[:], AF.Exp)
                for d in range(2):
                    for h in range(4):
                        for j in range(nt):
                            k.ts(kw_[:], KMtok[:, j, h * 64:(h + 1) * 64], Ftok[:, j, d * 4 + h:d * 4 + h + 1], 0.125, ALU.mult, ALU.mult)
                            k.mm(PS[3][0:64, 0:65], kw_[:], VM[:, j, h, 0:65], start=(j == 0), stop=(j == nt - 1))
                        k.cp(cst[:], PS[3][0:64, 0:65])
                        k.dma('sp', E['NC_'][si, l, d, h], cst[:, 0:64])
                        k.dma('sp', E['NN'][si, l, d, h, :].rearrange("(p o) -> p o", o=1), cst[:, 64:65])

            if SUB < 4:
                continue
            for ci in range(ngrp):
                c0t = ci * ntg
                c0 = c0t * 128
                def capture(fn):
                    k.defer = []
                    fn()
                    items = k.defer
                    k.defer = None
                    return items

                def emit_list(items):
                    for it in items:
                        k.emit(it)

                def att_blocks(h):
                    kvh = h // 4
                    pair = h // 2
                    base = (h % 2) * 64
                    var = 0 if kvh * 64 == base else 1
                    po = PS[4 + (h % 2)]
                    keyl = []
                    if rope:
                        keyl += [('c', 0), ('c', 1)]
                        for j in range(max(0, c0t - 1), min(nt, c0t + ntg + 1)):
                            keyl.append(('b', j))
                    else:
                        keyl += [('f', j) for j in range(nt)]

                    def a_stage1(n_):
                        kind, j = keyl[n_]
                        ps = PS[n_ % 2]
                        wbf = Wbuf[n_ % 2]
                        if kind == 'c':
                            lo, hi = 0, ntg
                            k.mm(ps[:, 0:GS], KCT[base:base + 64, var, j * 128:(j + 1) * 128], QAT[base:base + 64, pair, c0:c0 + GS])
                        elif kind == 'f':
                            lo, hi = 0, ntg
                            k.mm(ps[:, 0:GS], KAT[base:base + 64, var, j * 128:(j + 1) * 128], QAT[base:base + 64, pair, c0:c0 + GS])
                        else:
                            ilo = max(c0t, j - 1)
                            ihi = min(c0t + ntg - 1, j + 1)
                            lo, hi = ilo - c0t, ihi - c0t + 1
                            ncol = (hi - lo) * 128
                            k.mm(ps[:, lo * 128:hi * 128], KAT[base:base + 64, var, j * 128:(j + 1) * 128],
                                 QAT[base:base + 64, pair, c0 + lo * 128:c0 + hi * 128], start=True, stop=False)
                            mo = (ilo - (j - 1)) * 128
                            k.mm(ps[:, lo * 128:hi * 128], identb[:], amask[:, mo:mo + ncol], start=False, stop=True)
                        k.act(wbf[:, lo * 128:hi * 128], ps[:, lo * 128:hi * 128], AF.Exp, scale=0.125)
                        return lo, hi

                    def a_stage2(n_, lo, hi):
                        kind, j = keyl[n_]
                        wbf = Wbuf[n_ % 2]
                        vsrc = VC[:, j, kvh, 0:65] if kind == 'c' else VA[:, j, kvh, 0:65]
                        k.mm(po[0:65, lo * 128:hi * 128], vsrc, wbf[:, lo * 128:hi * 128], start=(n_ == 0),
                             stop=(n_ == len(keyl) - 1), skip_group_check=True)
                    rng = a_stage1(0)
                    for n_ in range(len(keyl)):
                        nrng = a_stage1(n_ + 1) if n_ + 1 < len(keyl) else None
                        a_stage2(n_, *rng)
                        rng = nrng

                def att_epi(h):
                    pair = h // 2
                    hb = (h % 2) * 64
                    po = PS[4 + (h % 2)]
                    ot = OTb[h % 2]
                    k.cp(ot[:, 0:GS], po[0:65, 0:GS])
                    pt4 = PS[6 + (h % 2)]
                    for ti in range(ntg):
                        k.tr(pt4[:, ti * 66:ti * 66 + 65], ot[:, ti * 128:(ti + 1) * 128], ident[0:65, 0:65])
                    pv = pt4[:, 0:ntg * 66].rearrange("p (t c) -> p t c", c=66)
                    k.ts(sm[:, 0:ntg], pv[:, :, 64], sinkE[:, h:h + 1], None, ALU.add)
                    k.recip(sm[:, 4:4 + ntg], sm[:, 0:ntg])
                    k.tt(HM[:, 0:ntg, hb:hb + 64], pv[:, :, 0:64], sm[:, 4:4 + ntg].unsqueeze(2).broadcast_to([128, ntg, 64]), ALU.mult)
                    if h % 2 == 1:
                        for ti in range(ntg):
                            k.tr(PS[2][:, ti * 128:(ti + 1) * 128], HM[:, ti, 0:128], ident[:])
                        k.cp(ATTT[:, pair, 0:GS], PS[2][:, 0:GS], eng='act')

                if SUB >= 5:
                    blk = [capture(lambda h=h: att_blocks(h)) for h in range(8)]
                    epi = [capture(lambda h=h: att_epi(h)) for h in range(8)]
                    emit_list(blk[0])
                    for h in range(8):
                        if h + 1 < 8:
                            emit_list(blk[h + 1])
                        emit_list(epi[h])

                def ml_pro(h, d):
                    pair = h // 2
                    base = (h % 2) * 64
                    po = PS[4 + d]
                    if rope:
                        k.mm(PS[3][:, 0:GS], sel[:, d * 4 + h, :], RB[:, c0:c0 + GS])
                        k.act(Dbuf[0][base:base + 64, 0:GS], PS[3][base:base + 64, 0:GS], AF.Exp, scale=-1.0,
                              bias=M0b[base:base + 64, d * 4 + h:d * 4 + h + 1])
                        k.tt(Wbuf[0][base:base + 64, 0:GS], QMT[base:base + 64, pair, c0:c0 + GS], Dbuf[0][base:base + 64, 0:GS], ALU.mult)
                        k.mm(po[0:65, 0:GS], C0A[base:base + 64, d, pair, 0:65], Wbuf[0][base:base + 64, 0:GS], start=True, stop=False)
                    k.mm(PS[3][:, 0:GS], sel[:, d * 4 + h, :], RB[:, c0:c0 + GS])
                    k.cp(qf[:, 0:GS], PS[3][:, 0:GS], eng='act')

                def ml_blocks(h, d):
                    pair = h // 2
                    base = (h % 2) * 64
                    po = PS[4 + d]
                    js = list(range(0, c0t + ntg)) if d == 0 else list(range(nt - 1, c0t - 1, -1))

                    def m_rng(n_):
                        r = js[n_] - c0t
                        if 0 <= r < ntg:
                            return ((r, ntg) if d == 0 else (0, r + 1)), r
                        return (0, ntg), None

                    def m_stage1(n_):
                        j = js[n_]
                        (lo, hi), r = m_rng(n_)
                        ps = PS[n_ % 3]
                        db = Dbuf[n_ % 2]
                        k.mm(ps[:, lo * 128:hi * 128], KMT[base:base + 64, pair, j * 128:(j + 1) * 128],
                             QMT[base:base + 64, pair, c0 + lo * 128:c0 + hi * 128])
                        k.act(db[:, lo * 128:hi * 128], qf[:, lo * 128:hi * 128], AF.Exp, scale=-1.0, bias=A_tok[:, j, d * 32 + h:d * 32 + h + 1])

                    def m_stage2(n_, first_):
                        j = js[n_]
                        (lo, hi), r = m_rng(n_)
                        ps = PS[n_ % 3]
                        db = Dbuf[n_ % 2]
                        wbf = Wbuf[n_ % 2]
                        k.stt(wbf[:, lo * 128:hi * 128], ps[:, lo * 128:hi * 128], 0.125, db[:, lo * 128:hi * 128], ALU.mult, ALU.mult)
                        if r is not None:
                            k.tt(wbf[:, r * 128:(r + 1) * 128], wbf[:, r * 128:(r + 1) * 128], tri01[:, d, :], ALU.mult, eng='pool')
                        k.mm(po[0:65, lo * 128:hi * 128], VM[:, j, h, 0:65], wbf[:, lo * 128:hi * 128], start=first_, stop=(n_ == len(js) - 1),
                             skip_group_check=True)
                    first = not rope
                    m_stage1(0)
                    for n_ in range(len(js)):
                        if n_ + 1 < len(js):
                            m_stage1(n_ + 1)
                        m_stage2(n_, first)
                        first = False

                def ml_epi(h, d):
                    po = PS[4 + d]
                    ot = OTb[d]
                    k.cp(ot[:, 0:GS], po[0:65, 0:GS], eng='act')
                    pt4 = PS[6 + d]
                    for ti in range(ntg):
                        k.tr(pt4[:, ti * 66:ti * 66 + 65], ot[:, ti * 128:(ti + 1) * 128], ident[0:65, 0:65])
                    pv = pt4[:, 0:ntg * 66].rearrange("p (t c) -> p t c", c=66)
                    k.ts(sm[:, 0:ntg], pv[:, :, 64], -1.0, None, ALU.mult)
                    k.tt(sm[:, 0:ntg], sm[:, 0:ntg], pv[:, :, 64], ALU.max)
                    k.tt(sm[:, 0:ntg], sm[:, 0:ntg], E_tok[:, c0t:c0t + ntg, d * 32 + h], ALU.max)
                    k.recip(sm[:, 4:4 + ntg], sm[:, 0:ntg])
                    if d == 0:
                        k.tt(HM[:, 0:ntg, h * 64:(h + 1) * 64], pv[:, :, 0:64],
                             sm[:, 4:4 + ntg].unsqueeze(2).broadcast_to([128, ntg, 64]), ALU.mult)
                    else:
                        for ti in range(ntg):
                            k.stt(HM[:, ti, h * 64:(h + 1) * 64], pv[:, ti, 0:64], sm[:, 4 + ti:5 + ti], HM[:, ti, h * 64:(h + 1) * 64], ALU.mult, ALU.add)

                if SUB >= 6:
                    hd = [(h, d) for h in range(4) for d in range(2)]
                    pro = [capture(lambda h=h, d=d: ml_pro(h, d)) for (h, d) in hd]
                    blk = [capture(lambda h=h, d=d: ml_blocks(h, d)) for (h, d) in hd]
                    epi = [capture(lambda h=h, d=d: ml_epi(h, d)) for (h, d) in hd]
                    emit_list(pro[0])
                    for i in range(len(hd)):
                        emit_list(blk[i])
                        if i + 1 < len(hd):
                            emit_list(pro[i + 1])
                        emit_list(epi[i])
                for ti in range(ntg):
                    for h in range(4):
                        k.ttr(Dbuf[0][:, 0:64], HM[:, ti, h * 64:(h + 1) * 64], HM[:, ti, h * 64:(h + 1) * 64], ALU.mult, ALU.add, sm[:, 8 + h:9 + h])
                    k.act(sm[:, 12:16], sm[:, 8:12], AF.Ln, bias=epsc[:, 0:1], scale=1.0 / 64)
                    k.act(sm[:, 12:16], sm[:, 12:16], AF.Exp, scale=-0.5)
                    for h in range(4):
                        k.ts(HM[:, ti, h * 64:(h + 1) * 64], HM[:, ti, h * 64:(h + 1) * 64], sm[:, 12 + h:13 + h], None, ALU.mult)
                    k.tt(HM[:, ti, :], HM[:, ti, :], mngt[:], ALU.mult)
                    k.tt(HM[:, ti, :], HM[:, ti, :], OM[:, c0t + ti, :], ALU.mult)
                    for p2 in range(2):
                        k.tr(PS[p2][:, ti * 128:(ti + 1) * 128], HM[:, ti, p2 * 128:(p2 + 1) * 128], ident[:])
                for p2 in range(2):
                    k.cp(MLST[:, p2, 0:GS], PS[p2][:, 0:GS], eng='act')

                for ti in range(ntg):
                    i = c0t + ti
                    for gq in range(4):
                        jl = [j for j in (i - 1, i, i + 1) if 0 <= j < nt]
                        for n_, j in enumerate(jl):
                            if j == i - 1:
                                kind = 3
                            elif j == i + 1:
                                kind = 4
                            else:
                                kind = 0 if i == 0 else (2 if i == nt - 1 else 1)
                            k.mm(PS[2][0:64, 0:128], XP[:, j, gq * 64:(gq + 1) * 64], band[:, gq, kind, :], start=(n_ == 0), stop=(n_ == len(jl) - 1))
                        k.cp(DT[:], PS[2][0:64, 0:128], eng='act')
                        k.mm(PS[3][0:64, 0:128], PW[:, gq, :], DT[:])
                        k.ts(PLT[:, gq, ti * 128:(ti + 1) * 128], PS[3][0:64, 0:128], psc[:, gq:gq + 1], None, ALU.mult)

                for ti in range(ntg):
                    tok = t0 + (c0t + ti) * 128
                    k.dma('sp', xt[:], Xd[tok:tok + 128, :])
                    for half in range(2):
                        pp = PS[6 + half]
                        for kc in range(4):
                            k.mm(pp[:], ATTT[:, kc, ti * 128:(ti + 1) * 128], WO[:, kc, half * 512:(half + 1) * 512], start=(kc == 0), stop=False)
                        for kc in range(2):
                            k.mm(pp[:], MLST[:, kc, ti * 128:(ti + 1) * 128], WO[:, 4 + kc, half * 512:(half + 1) * 512], start=False, stop=False)
                        for gq in range(4):
                            k.mm(pp[:], PLT[:, gq, ti * 128:(ti + 1) * 128], WOP[:, gq, half * 512:(half + 1) * 512], start=False, stop=(gq == 3))
                        k.tt(xn[:, half * 512:(half + 1) * 512], pp[:], gB[:, g, 0, half * 512:(half + 1) * 512], ALU.mult)
                    k.tt(xn[:], xn[:], xt[:], ALU.add)
                    k.dma('sp', Xd[tok:tok + 128, :], xn[:])


def phase_b(nc, k, l, E):
    PS = E['PS']; Xd = E['Xd']; st = E['st']
    ident = E['ident']; A2 = E['A2']; modF = E['modF']; gB = E['gB']; epsc = E['epsc']
    iota16 = E['iota16']; qidx = E['qidx']
    from contextlib import ExitStack
    es = ExitStack()

    def sb(name, shape, dt=F32):
        return es.enter_context(nc.sbuf_tensor("%s_%d" % (name, l), list(shape), dt))

    with es:
        WQ = sb("WQ", [128, 8, 2048], BF16)
        fngt = sb("fngt", [128, D])
        junk = sb("junkb", [128, D], BF16)
        k.dma('sp', fngt[:], E['fng'])
        SH4 = sb("SH4", [128, 8, 16], U32)
        M15 = sb("M15", [128, 8, 16], U32)
        k.memset(SH4[:], 4)
        k.memset(M15[:], 15)
        KT = sb("KT", [128, 16, 128], BF16)
        xnb = sb("xnb", [128, D])
        stb = sb("stb", [128, 8])
        hTf = sb("hTf", [128, 8, 128])
        hTb = sb("hTb", [128, 8, 128], BF16)
        qT = sb("qT", [128, 16, 128], BF16)
        SC = sb("SC", [128, 16, 128])
        SC2 = sb("SC2", [128, 128])
        TV = sb("TV", [128, 8, 2, 16])
        TI = sb("TI", [128, 8, 2, 16], U32)
        TIf = sb("TIf", [128, 8, 2, 16], BF16)
        CAND = sb("CAND", [128, 8, 256])
        CAND2 = sb("CAND2", [128, 256])
        BSv = sb("BSv", [128, 8, 16])
        BP = sb("BP", [128, 8, 16], U32)
        K1 = sb("K1", [128, 8, 16], U32)
        K2 = sb("K2", [128, 8, 16], U32)
        K1f = sb("K1f", [128, 8, 16], BF16)
        K2f = sb("K2f", [128, 8, 16], BF16)
        EQ = sb("EQ", [128, 8, 16, 16], BF16)
        iob = sb("iob", [128, 16], BF16)
        k.cp(iob[:], iota16[:])
        I1f = sb("I1f", [128, 8, 16])
        I2f = sb("I2f", [128, 8, 16])
        Zs = sb("Zs", [128, 8])
        YO = sb("YO", [128, D])
        XT = [sb("XTb%d" % i, [128, D]) for i in range(2)]
        H2 = [sb("H2b%d" % i, [128, D]) for i in range(2)]
        EX = [sb("EXi%d" % i, [128, 128], I32) for i in range(2)]
        GTs = [sb("GT%d" % i, [128, 8, 16]) for i in range(2)]
        NB = 16
        UV = [sb("UV%d" % i, [128, 2 * D], BF16) for i in range(NB)]
        DG = [sb("DG%d" % i, [128, 128], BF16) for i in range(4)]
        AVs = [sb("AVs%d" % i, [128, 1]) for i in range(8)]
        GAs = [sb("GAs%d" % i, [128, 1]) for i in range(8)]
        puv = E['puv16']
        k.dma('pool', WQ[:], E['wq'][l].rearrange("(kc p) n -> p kc n", p=128))
        k.dma('pool', KT[:], E['keysT'][l])
        tiles = []
        for si, (t0, L, g) in enumerate(SEQS):
            quarter = E['full'] and g == 1
            for tile in range(4 if quarter else L // 128):
                tiles.append((t0 + tile * 128, g, (tile if quarter else None)))
        XB = E['XB']

        def route(info, b):
            k.defer = []
            tok, g, gidx = info
            xt = XT[b]; h2 = H2[b]; EXi = EX[b]; GT = GTs[b]
            if gidx is None:
                k.dma('sp', xt[:], Xd[tok:tok + 128, :])
            else:
                k.gather(xt[:], Xd, qidx[:, gidx:gidx + 1], 0)
            k.ttr(xnb[:], xt[:], xt[:], ALU.mult, ALU.add, stb[:, 0:1])
            k.act(stb[:, 1:2], stb[:, 0:1], AF.Ln, bias=epsc[:, 0:1], scale=1.0 / D)
            k.act(stb[:, 2:3], stb[:, 1:2], AF.Exp, scale=-0.5)
            k.act(xnb[:], xt[:], AF.Copy, scale=stb[:, 2:3])
            for half in range(2):
                for q in range(4):
                    dc = half * 4 + q
                    k.tr(PS[2 + half][:, q * 128:(q + 1) * 128], xnb[:, dc * 128:(dc + 1) * 128], ident[:])
                for q in range(4):
                    dc = half * 4 + q
                    k.act(hTf[:, dc, :], PS[2 + half][:, q * 128:(q + 1) * 128], AF.Identity,
                          bias=modF[:, g, 2, dc:dc + 1], scale=A2[:, g, dc:dc + 1])
                    k.act(hTb[:, dc, :], PS[2 + half][:, q * 128:(q + 1) * 128], AF.Identity,
                          bias=modF[:, g, 2, dc:dc + 1], scale=A2[:, g, dc:dc + 1])
            for half in range(2):
                for q in range(4):
                    dc = half * 4 + q
                    k.tr(PS[4 + half][:, q * 128:(q + 1) * 128], hTf[:, dc, :], ident[:])
                k.cp(h2[:, half * 512:(half + 1) * 512], PS[4 + half][:], eng='act')
            for hc in range(16):
                pp = PS[6 + (hc % 2)]
                for kc in range(8):
                    k.mm(pp[:, 0:128], WQ[:, kc, hc * 128:(hc + 1) * 128], hTb[:, kc, :], start=(kc == 0), stop=(kc == 7))
                k.cp(qT[:, hc, :], pp[:, 0:128], eng='act')
            for q4 in range(4):
                pp = PS[2 + (q4 % 2)]
                for q in range(4):
                    hc = q4 * 4 + q
                    k.mm(pp[:, q * 128:(q + 1) * 128], qT[:, hc, :], KT[:, hc, :])
                k.cp(SC[:, q4 * 4:(q4 + 1) * 4, :], pp[:].rearrange("p (a b) -> p a b", a=4), eng='act')
            for hc in range(16):
                hh, cc = hc // 2, hc % 2
                k.max8(TV[:, hh, cc, 0:8], SC[:, hc, :])
                k.maxidx(TI[:, hh, cc, 0:8], TV[:, hh, cc, 0:8], SC[:, hc, :])
                k.mrep(SC2[:], TV[:, hh, cc, 0:8], SC[:, hc, :], -1e30)
                k.max8(TV[:, hh, cc, 8:16], SC2[:])
                k.maxidx(TI[:, hh, cc, 8:16], TV[:, hh, cc, 8:16], SC2[:])
            k.cp(TIf[:], TI[:])
            k.tt(CAND[:].rearrange("p h (a b) -> p h a b", a=16),
                 TV[:, :, 0, :].unsqueeze(3).broadcast_to([128, 8, 16, 16]),
                 TV[:, :, 1, :].unsqueeze(2).broadcast_to([128, 8, 16, 16]), ALU.add)
            for hh in range(8):
                k.max8(BSv[:, hh, 0:8], CAND[:, hh, :])
                k.maxidx(BP[:, hh, 0:8], BSv[:, hh, 0:8], CAND[:, hh, :])
                k.mrep(CAND2[:], BSv[:, hh, 0:8], CAND[:, hh, :], -1e30)
                k.max8(BSv[:, hh, 8:16], CAND2[:])
                k.maxidx(BP[:, hh, 8:16], BSv[:, hh, 8:16], CAND2[:])
            k.tt(GT[:], BSv[:], BSv[:, :, 0:1].broadcast_to([128, 8, 16]), ALU.subtract)
            k.act(GT[:], GT[:], AF.Exp)
            k.treduce(Zs[:], GT[:], AX.X, ALU.add)
            k.recip(Zs[:], Zs[:])
            k.tt(GT[:], GT[:], Zs[:].unsqueeze(2).broadcast_to([128, 8, 16]), ALU.mult)
            k.tt(K1[:], BP[:], SH4[:], ALU.logical_shift_right)
            k.tt(K2[:], BP[:], M15[:], ALU.bitwise_and)
            k.cp(K1f[:], K1[:])
            k.cp(K2f[:], K2[:])
            for (Kf, cc, If_) in ((K1f, 0, I1f), (K2f, 1, I2f)):
                k.tt(EQ[:], Kf[:].unsqueeze(3).broadcast_to([128, 8, 16, 16]),
                     iob[:].unsqueeze(1).unsqueeze(1).broadcast_to([128, 8, 16, 16]), ALU.is_equal)
                k.tt(EQ[:], EQ[:], TIf[:, :, cc, :].unsqueeze(2).broadcast_to([128, 8, 16, 16]), ALU.mult)
                k.treduce(If_[:], EQ[:], AX.X, ALU.add)
            k.stt(I1f[:], I1f[:], 128.0, I2f[:], ALU.mult, ALU.add)
            k.ts(I1f[:], I1f[:], float(l * 16384), None, ALU.add)
            k.cp(EXi[:], I1f[:].rearrange("p h k -> p (h k)"))
            items = k.defer
            k.defer = None
            return items

        def drain(items):
            if items:
                while items:
                    k.emit(items.pop(0))

        def pull(items):
            if items:
                k.emit(items.pop(0))

        def gather_loop(info, b, nxt):
            tok, g, gidx = info
            xt = XT[b]; h2 = H2[b]; EXi = EX[b]; GT = GTs[b]
            GTf = GT[:].rearrange("p h k -> p (h k)")

            def dot(s_):
                uv = UV[s_ % NB]
                k.gather(uv[:], puv, EXi[:, s_:s_ + 1], 0, skip=('dve',) if s_ >= NB else ())
                k.ttr(junk[:], uv[:, 0:D], h2[:], ALU.mult, ALU.add, AVs[s_ % 8][:])

            def gelu(s_):
                k.act(GAs[s_ % 8][:], AVs[s_ % 8][:], AF.Gelu)
            dot(0)
            dot(1)
            gelu(0)
            for s_ in range(128):
                ga = GAs[s_ % 8]
                dg = DG[s_ % 4]
                if s_ + 2 < 128:
                    dot(s_ + 2)
                pull(nxt)
                if s_ + 1 < 128:
                    gelu(s_ + 1)
                k.act(ga[:], ga[:], AF.Copy, scale=GTf[:, s_:s_ + 1])
                pull(nxt)
                k.act(dg[:], ident[:], AF.Copy, scale=ga[:, 0:1])
                pull(nxt)
                for half in range(2):
                    k.mm(PS[half][:], dg[:], UV[s_ % NB][:, D + half * 512:D + (half + 1) * 512], start=(s_ == 0), stop=(s_ == 127))
            for half in range(2):
                k.tt(YO[:, half * 512:(half + 1) * 512], PS[half][:], gB[:, g, 1, half * 512:(half + 1) * 512], ALU.mult)
            k.tt(YO[:], YO[:], xt[:], ALU.add)
            if l < DEPTH - 1:
                if gidx is None:
                    k.dma('sp', Xd[tok:tok + 128, :], YO[:])
                else:
                    k.dma('sp', XB[gidx // 2][(gidx % 2) * 128:(gidx % 2) * 128 + 128, :], YO[:])
            else:
                k.ttr(junk[:], YO[:], YO[:], ALU.mult, ALU.add, st[:, 4:5])
                k.act(st[:, 5:6], st[:, 4:5], AF.Ln, bias=epsc[:, 0:1], scale=1.0 / D)
                k.act(st[:, 6:7], st[:, 5:6], AF.Exp, scale=-0.5)
                k.stt(YO[:], YO[:], st[:, 6:7], fngt[:], ALU.mult, ALU.mult)
                k.dma('sp', E['Y'][tok:tok + 128, :], YO[:])

        drain(route(tiles[0], 0))
        for i, info in enumerate(tiles):
            nxt = route(tiles[i + 1], (i + 1) % 2) if i + 1 < len(tiles) else None
            gather_loop(info, i % 2, nxt)
            drain(nxt)


def _consts():
    c = {}
    c['ident'] = np.eye(128, dtype=np.float32)
    sel = np.zeros((36, 8, 128), np.float32)
    for d in range(2):
        for h in range(4):
            sel[d * 32 + h, d * 4 + h, :] = 1.0
    c['sel'] = sel
    half = 32
    inv = (10000.0 ** (-np.arange(0, half, 2, dtype=np.float32) / half)).astype(np.float32)
    t = np.arange(2048)
    row = (t // 64).astype(np.float32)
    col = (t % 64).astype(np.float32)
    C = np.zeros((64, 2048), np.float32)
    S = np.zeros((64, 2048), np.float32)
    angr = (row[None, :] * inv[:, None]).astype(np.float32)
    angc = (col[None, :] * inv[:, None]).astype(np.float32)
    C[0:16] = np.cos(angr); C[16:32] = np.cos(angr); C[32:48] = np.cos(angc); C[48:64] = np.cos(angc)
    S[0:16] = np.sin(angr); S[16:32] = np.sin(angr); S[32:48] = np.sin(angc); S[48:64] = np.sin(angc)
    c['ropeC'] = np.concatenate([C, C], 0)
    c['ropeS'] = np.concatenate([S, S], 0)
    P = np.zeros((64, 64), np.float32)
    for o in (0, 32):
        for i in range(16):
            P[o + i, o + 16 + i] = -1.0
            P[o + 16 + i, o + i] = 1.0
    P2 = np.zeros((128, 128), np.float32)
    P2[0:64, 0:64] = P
    P2[64:128, 64:128] = P
    c['prot'] = np.ascontiguousarray(P2.T)
    BIG = 30000.0
    s = np.arange(128)[:, None]
    tt_ = np.arange(128)[None, :]
    trif = np.where(s > tt_, BIG, 0.0).astype(np.float32)
    trib = np.where(s < tt_, BIG, 0.0).astype(np.float32)
    full = np.full((128, 128), BIG, np.float32)
    zero = np.zeros((128, 128), np.float32)
    c['tri01'] = np.stack([(s <= tt_).astype(np.float32), (s >= tt_).astype(np.float32)], 1)
    c['amask'] = np.concatenate([-trif, zero, -trib], 1).astype(np.float32)
    band = np.zeros((128, 4, 5, 128), np.float32)
    Lb = 384
    for gq, w in enumerate((2, 4, 8, 16)):
        def mat(L):
            M = np.zeros((L, L), np.float32)
            for t_ in range(L):
                lo = max(t_ - w // 2, 0); hi = min(t_ + w // 2, L)
                M[t_, lo:hi] = 1.0 / (hi - lo)
                M[t_, t_] -= 1.0
            return M
        M = mat(Lb)
        MT = M.T
        band[:, gq, 0] = MT[0:128, 0:128]
        band[:, gq, 1] = MT[128:256, 128:256]
        band[:, gq, 2] = MT[256:384, 256:384]
        band[:, gq, 3] = MT[0:128, 128:256]
        band[:, gq, 4] = MT[256:384, 128:256]
    c['band'] = band
    c['iota16'] = np.tile(np.arange(16, dtype=np.float32)[None, :], (128, 1))
    return c


_NC_CACHE = {}


def _prep_shared(inp):
    f = lambda a: np.ascontiguousarray(np.asarray(a, dtype=np.float32))
    sh = {}
    w_in = f(inp['w_in'])
    cols = np.concatenate([
        np.arange(0, 512), np.arange(512, 640), np.arange(576, 640), np.arange(512, 576),
        np.arange(768, 1024), np.arange(1024, 1280),
        np.arange(640, 768), np.arange(1280, 1536), np.arange(1792, 1808),
        np.arange(1536, 1792), np.arange(1808, 2064),
        np.arange(512, 640), np.arange(1024, 1280)])
    assert cols.size == NW
    wext = w_in[:, :, cols]
    blocks = [(ob * 128, 128) for ob in range(10)] + [(1280, 400), (1680, 512), (2192, 384)]
    wb_ = np.zeros((DEPTH, 128, 8 * NW), np.float32)
    for (c0, ncol) in blocks:
        blk = wext[:, :, c0:c0 + ncol].reshape(DEPTH, 8, 128, ncol).transpose(0, 2, 1, 3).reshape(DEPTH, 128, 8 * ncol)
        wb_[:, :, 8 * c0:8 * (c0 + ncol)] = blk
    sh['w_in'] = wb_
    sh['w_mod'] = f(inp['w_mod'])
    bm = f(inp['b_mod']).reshape(DEPTH, 6, 8, 128)
    sh['bmodF'] = np.ascontiguousarray(bm[:, [0, 1, 3, 4]].transpose(0, 3, 1, 2))
    sh['bmodG'] = np.ascontiguousarray(np.broadcast_to(f(inp['b_mod']).reshape(DEPTH, 1, 6, D)[:, :, [2, 5]], (DEPTH, 128, 2, D)))
    sh['n1g'] = np.ascontiguousarray(f(inp['norm1_g']).reshape(DEPTH, 8, 128).transpose(0, 2, 1))
    sh['n2g'] = np.ascontiguousarray(f(inp['norm2_g']).reshape(DEPTH, 8, 128).transpose(0, 2, 1))
    sh['gateb'] = np.ascontiguousarray(np.broadcast_to(f(inp['gate_b'])[:, None, :], (DEPTH, 128, 16)))
    sh['sinkb'] = np.ascontiguousarray(np.broadcast_to(f(inp['attn_sink'])[:, None, :], (DEPTH, 128, 8)))
    sh['mng'] = np.ascontiguousarray(np.broadcast_to(f(inp['mlstm_norm_g'])[:, None, :], (DEPTH, 128, 256)))
    sh['poolw'] = np.ascontiguousarray(f(inp['pool_w']).transpose(0, 2, 1, 3))
    sh['pscale'] = np.ascontiguousarray(f(inp['pool_scale']).reshape(DEPTH, 4, 64).transpose(0, 2, 1))
    sh['w_out'] = f(inp['w_out'])
    sh['wq'] = f(inp['peer_wq'])
    pk = f(inp['peer_keys'])
    sh['keysT'] = np.ascontiguousarray(pk.transpose(0, 4, 1, 2, 3).reshape(DEPTH, 128, 16, 128))
    sh['puv'] = np.concatenate([f(inp['peer_u']).reshape(DEPTH * 16384, D), f(inp['peer_v']).reshape(DEPTH * 16384, D)], 1)
    sh['fng'] = np.ascontiguousarray(np.broadcast_to(f(inp['final_norm_g'])[None, :], (128, D)))
    sh.update(_consts())
    return sh


def _prep_core(inp, c):
    f = lambda a: np.ascontiguousarray(np.asarray(a, dtype=np.float32))
    b = c % 2
    m = {}
    m['X'] = np.concatenate([f(inp['x_prompt'][2 * c]), f(inp['x_prompt'][2 * c + 1]), f(inp['x_sample'][b])], 0)
    cond = np.stack([f(inp['c_ctx']), f(inp['c'][b])], 0)
    m['condT'] = np.ascontiguousarray(cond.reshape(2, 8, 128).transpose(2, 0, 1))
    m['cachek'] = f(inp['cache_k'][b]).reshape(DEPTH, 256, 128)
    m['cachev'] = f(inp['cache_v'][b]).reshape(DEPTH, 256, 128)
    sC = f(inp['state_C'][b])
    sn = f(inp['state_n'][b])
    smm = f(inp['state_m'][b])
    ca = np.concatenate([sC, sn[..., None]], -1)
    ca = ca.reshape(DEPTH, 2, 2, 2, 64, 65)
    m['c0a'] = np.ascontiguousarray(ca.transpose(0, 3, 4, 1, 2, 5).reshape(DEPTH, 128, 2, 2, 65))
    m['m0b'] = np.ascontiguousarray(np.broadcast_to(smm.reshape(DEPTH, 1, 8), (DEPTH, 128, 8)))
    m0r = np.zeros((DEPTH, 36, 1), np.float32)
    m0r[:, 0:4, 0] = smm[:, 0]
    m0r[:, 32:36, 0] = smm[:, 1]
    m['m0r'] = m0r
    r = c // 2
    m['qidx'] = (512 + r * 512 + np.arange(4)[None, :] * 128 + np.arange(128)[:, None]).astype(np.int32)
    return m


def kernel(**inputs):
    stage = int(inputs.pop('_stage', 99))
    nl = int(inputs.pop('_nl', DEPTH))
    cores = inputs.pop('_cores', None)
    if (stage, nl) not in _NC_CACHE:
        _NC_CACHE[(stage, nl)] = build(stage, nl)
    nc = _NC_CACHE[(stage, nl)]
    sh = _prep_shared(inputs)
    if stage < 2:
        sh['puv'] = sh['puv'][0:16]
    if cores is not None:
        in_maps = []
        for c in cores:
            m = dict(sh)
            m.update(_prep_core(inputs, c))
            in_maps.append(m)
        res = run_bass_kernel_spmd(nc, in_maps, core_ids=list(range(len(cores))))
        return res.results
    in_maps = []
    for c in range(8):
        m = dict(sh)
        m.update(_prep_core(inputs, c))
        in_maps.append(m)
    res = run_bass_kernel_spmd(nc, in_maps, core_ids=list(range(8)))
    R = res.results
    y_prompt = np.stack([R[c]['Y'][0:512].reshape(2, 256, D) for c in range(8)], 0).reshape(16, 256, D)
    y_sample = np.stack([np.concatenate([R[b + 2 * r]['Y'][512:1024] for r in range(4)], 0) for b in range(2)], 0)
    nk = np.concatenate([R[c]['NK'] for c in range(8)], 0).reshape(16, DEPTH, 256, 2, 64)
    nv = np.concatenate([R[c]['NV'] for c in range(8)], 0).reshape(16, DEPTH, 256, 2, 64)
    nC = np.concatenate([R[c]['NC'] for c in range(8)], 0)
    nn = np.concatenate([R[c]['NN'] for c in range(8)], 0)
    nm = np.concatenate([R[c]['NM'] for c in range(8)], 0)
    return (y_prompt.astype(np.float32), y_sample.astype(np.float32), nk.astype(np.float32), nv.astype(np.float32),
            nC.astype(np.float32), nn.astype(np.float32), nm.astype(np.float32))
```
